# Optimizing a Trainium2 kernel written in Bass

```python
import jax
import jax.numpy as jnp
from jax import lax
import numpy as np


D_MODEL = 1024
BATCH = 8
SEQ = 4096
DEPTH = 4

N_MIXERS = 2
N_A_LAYERS = (DEPTH + 1) // 2
N_B_LAYERS = DEPTH // 2

HG_EXPAND = 128
HG_HEADS = D_MODEL // HG_EXPAND
HG_KDIM = HG_EXPAND
HG_VDIM = D_MODEL // HG_HEADS
HG_CHUNK = 32

ATT_HEAD_DIM = 128
ATT_HEADS = D_MODEL // ATT_HEAD_DIM
DILATED_GROUPS = ((128, 1), (512, 4), (2048, 16))
N_GROUPS = len(DILATED_GROUPS)
ATT_QKV_COLS = N_GROUPS * 3 * ATT_HEADS * ATT_HEAD_DIM
ROPE_THETA = 500000.0
ROPE_DIM = ATT_HEAD_DIM // 4

D_FF = -(-(8 * D_MODEL) // (3 * 256)) * 256

NORM_EPS = 1e-6

kernel_name = 'hybrid_hgrn2_dilated_attn_block'


def rms_norm(x, g):
    xf = x.astype(jnp.float32)
    y = xf * lax.rsqrt(jnp.mean(xf * xf, axis=-1, keepdims=True) + NORM_EPS)
    return (y * g.astype(jnp.float32)).astype(x.dtype)


def apply_partial_rope(t, cos, sin):
    half = ROPE_DIM // 2
    t1 = t[..., :half]
    t2 = t[..., half:ROPE_DIM]
    return jnp.concatenate([t1 * cos - t2 * sin, t2 * cos + t1 * sin, t[..., ROPE_DIM:]], axis=-1)


def hgrn2_mixer(u, w_in, lb, norm_g, w_out):
    B, S, _ = u.shape
    H, K, C = HG_HEADS, HG_KDIM, HG_CHUNK
    proj = (u @ w_in).astype(jnp.float32).reshape(B, S, 4, H, K)
    q_raw, f_raw, v, g = proj[:, :, 0], proj[:, :, 1], proj[:, :, 2], proj[:, :, 3]
    lb = lb.astype(jnp.float32).reshape(H, K)
    log_f = jnp.log(lb + (1.0 - lb) * jax.nn.sigmoid(f_raw))
    k = (1.0 - lb) * jax.nn.sigmoid(-f_raw)
    q = jax.nn.silu(q_raw)
    NC = S // C
    rs = lambda t: t.reshape(B, NC, C, H, t.shape[-1])
    q, k, v, log_f = rs(q), rs(k), rs(v), rs(log_f)
    b = jnp.cumsum(log_f, axis=2)
    q_dec = q * jnp.exp(b)
    k_inv = k * jnp.exp(-b)
    causal = jnp.tril(jnp.ones((C, C), dtype=bool))
    a = jnp.einsum('bnchk,bnshk->bnhcs', q_dec, k_inv)
    a = jnp.where(causal, a, 0.0)
    o_intra = jnp.einsum('bnhcs,bnshv->bnchv', a, v)
    b_last = b[:, :, -1:]
    k_end = k * jnp.exp(b_last - b)
    chunk_decay = jnp.exp(b_last[:, :, 0])

    def step(state, xs):
        q_c, k_c, v_c, dec = xs
        o_c = jnp.einsum('bchk,bhkv->bchv', q_c, state)
        state = dec[..., None] * state + jnp.einsum('bchk,bchv->bhkv', k_c, v_c)
        return state, o_c

    xs = (jnp.moveaxis(q_dec, 1, 0), jnp.moveaxis(k_end, 1, 0),
          jnp.moveaxis(v, 1, 0), jnp.moveaxis(chunk_decay, 1, 0))
    state0 = jnp.zeros((B, H, K, HG_VDIM), jnp.float32)
    _, o_inter = lax.scan(step, state0, xs)
    o = (o_intra + jnp.moveaxis(o_inter, 0, 1)).reshape(B, S, H, HG_VDIM)
    o = o * lax.rsqrt(jnp.mean(o * o, axis=-1, keepdims=True) + NORM_EPS)
    o = o * norm_g.astype(jnp.float32).reshape(H, HG_VDIM)
    o = o.reshape(B, S, D_MODEL) * jax.nn.silu(g.reshape(B, S, D_MODEL))
    return o.astype(u.dtype) @ w_out


def dilated_group_attention(q, k, v, window, dilation):
    B, S, H, Dh = q.shape
    L = S // dilation
    n_win = window // dilation
    blk = n_win
    nb = -(-L // blk)
    Lp = nb * blk

    def gather(t):
        t = t.reshape(B, L, dilation, H, Dh).transpose(0, 2, 1, 3, 4).reshape(B * dilation, L, H, Dh)
        t = jnp.pad(t, ((0, 0), (0, Lp - L), (0, 0), (0, 0)))
        return t.reshape(B * dilation, nb, blk, H, Dh)

    def with_prev(t):
        prev = jnp.pad(t, ((0, 0), (1, 0), (0, 0), (0, 0), (0, 0)))[:, :-1]
        return jnp.concatenate([prev, t], axis=2)

    qb = gather(q)
    kw = with_prev(gather(k))
    vw = with_prev(gather(v))
    s = jnp.einsum('bnqhd,bnkhd->bnhqk', qb, kw)
    qi = jnp.arange(blk)[:, None]
    ki = jnp.arange(2 * blk)[None, :]
    dist = blk + qi - ki
    band = (dist >= 0) & (dist <= n_win)
    has_prev = (jnp.arange(nb)[:, None, None] > 0) | (ki >= blk)[None]
    valid = band[None] & has_prev
    s = jnp.where(valid[None, :, None], s, -jnp.inf)
    m = jnp.max(s, axis=-1, keepdims=True)
    p = jnp.exp(s - m)
    z = jnp.sum(p, axis=-1, keepdims=True)
    o = jnp.einsum('bnhqk,bnkhd->bnqhd', p, vw) / jnp.swapaxes(z, 2, 3)
    lse = jnp.swapaxes((m + jnp.log(z))[..., 0], 2, 3)

    def scatter_back(t):
        rest = t.shape[3:]
        t = t.reshape(B, dilation, Lp, *rest)[:, :, :L]
        return jnp.moveaxis(t, 1, 2).reshape(B, S, *rest)

    return scatter_back(o), scatter_back(lse)


def dilated_attention_mixer(u, cos, sin, w_in, w_out):
    B, S, _ = u.shape
    proj = (u @ w_in).astype(jnp.float32).reshape(B, S, N_GROUPS, 3, ATT_HEADS, ATT_HEAD_DIM)
    scale = ATT_HEAD_DIM ** -0.5
    outs = []
    lses = []
    for gi, (window, dilation) in enumerate(DILATED_GROUPS):
        q = apply_partial_rope(proj[:, :, gi, 0], cos, sin) * scale
        k = apply_partial_rope(proj[:, :, gi, 1], cos, sin)
        v = proj[:, :, gi, 2]
        o_g, lse_g = dilated_group_attention(q, k, v, window, dilation)
        outs.append(o_g)
        lses.append(lse_g)
    wts = jax.nn.softmax(jnp.stack(lses, axis=0), axis=0)
    o = jnp.einsum('gbsh,gbshd->bshd', wts, jnp.stack(outs, axis=0))
    return o.reshape(B, S, D_MODEL).astype(u.dtype) @ w_out


def swiglu_ffn(u, w_in, w_out):
    gate, up = jnp.split(u @ w_in, 2, axis=-1)
    return (jax.nn.silu(gate) * up) @ w_out


def setup_inputs(seed: int = 0) -> dict:
    key = jax.random.key(seed)
    ks = jax.random.split(key, 16)
    D = D_MODEL
    nrm = jax.random.normal
    x = nrm(ks[0], (BATCH, SEQ, D), jnp.float32)
    c = nrm(ks[1], (BATCH, D), jnp.float32)
    offset = jax.random.randint(ks[2], (BATCH, 1), 0, 1024, dtype=jnp.int32)
    positions = offset + jnp.arange(SEQ, dtype=jnp.int32)[None, :]
    ada_w = nrm(ks[3], (DEPTH, D, 6 * D), jnp.float32) * (0.5 * D ** -0.5)
    ada_b = nrm(ks[4], (DEPTH, 6 * D), jnp.float32) * 0.02
    norm_g = 1.0 + 0.05 * nrm(ks[5], (DEPTH, 4, D), jnp.float32)
    hgrn_w_in = nrm(ks[6], (N_A_LAYERS, D, 4 * D), jnp.float32) * D ** -0.5
    hgrn_lower_bounds = 1.0 + 0.5 * nrm(ks[7], (N_A_LAYERS, D), jnp.float32)
    hgrn_norm_g = 1.0 + 0.05 * nrm(ks[8], (N_A_LAYERS, D), jnp.float32)
    hgrn_w_out = nrm(ks[9], (N_A_LAYERS, D, D), jnp.float32) * D ** -0.5
    attn_w_in = nrm(ks[10], (N_B_LAYERS, D, ATT_QKV_COLS), jnp.float32) * D ** -0.5
    attn_w_out = nrm(ks[11], (N_B_LAYERS, D, D), jnp.float32) * D ** -0.5
    ffn_w_in = nrm(ks[12], (DEPTH, D, 2 * D_FF), jnp.float32) * D ** -0.5
    ffn_w_out = nrm(ks[13], (DEPTH, D_FF, D), jnp.float32) * D_FF ** -0.5
    return {'x': x, 'c': c, 'positions': positions, 'ada_w': ada_w, 'ada_b': ada_b,
            'norm_g': norm_g, 'hgrn_w_in': hgrn_w_in, 'hgrn_lower_bounds': hgrn_lower_bounds,
            'hgrn_norm_g': hgrn_norm_g, 'hgrn_w_out': hgrn_w_out, 'attn_w_in': attn_w_in,
            'attn_w_out': attn_w_out, 'ffn_w_in': ffn_w_in, 'ffn_w_out': ffn_w_out}


def reference(x, c, positions, ada_w, ada_b, norm_g, hgrn_w_in, hgrn_lower_bounds,
              hgrn_norm_g, hgrn_w_out, attn_w_in, attn_w_out, ffn_w_in, ffn_w_out):
    cond = jax.nn.silu(c)
    inv_freq = ROPE_THETA ** (-jnp.arange(0, ROPE_DIM, 2, dtype=jnp.float32) / ROPE_DIM)
    ang = positions.astype(jnp.float32)[..., None] * inv_freq
    cos = jnp.cos(ang)[:, :, None, :]
    sin = jnp.sin(ang)[:, :, None, :]
    lbs = jnp.cumsum(jax.nn.softmax(hgrn_lower_bounds.astype(jnp.float32), axis=0), axis=0)
    lbs = lbs - lbs[0:1]
    h = x
    for layer in range(DEPTH):
        mod = (cond @ ada_w[layer] + ada_b[layer])[:, None, :]
        sh1, sc1, g1, sh2, sc2, g2 = jnp.split(mod, 6, axis=-1)
        u = rms_norm(h, norm_g[layer, 0]) * (1.0 + sc1) + sh1
        idx = layer // N_MIXERS
        if layer % N_MIXERS == 0:
            y = hgrn2_mixer(u, hgrn_w_in[idx], lbs[idx], hgrn_norm_g[idx], hgrn_w_out[idx])
        else:
            y = dilated_attention_mixer(u, cos, sin, attn_w_in[idx], attn_w_out[idx])
        h = h + (1.0 + g1) * rms_norm(y, norm_g[layer, 1])
        u = rms_norm(h, norm_g[layer, 2]) * (1.0 + sc2) + sh2
        y = swiglu_ffn(u, ffn_w_in[layer], ffn_w_out[layer])
        h = h + (1.0 + g2) * rms_norm(y, norm_g[layer, 3])
    return h
```

```python
import numpy as np
import concourse.bass as bass
import concourse.mybir as mybir
from concourse.bass_utils import run_bass_kernel_spmd

F32 = mybir.dt.float32
BF16 = mybir.dt.bfloat16
I32 = mybir.dt.int32
AF = mybir.ActivationFunctionType
ALU = mybir.AluOpType
AX = mybir.AxisListType

PE, ACT, DVE, POOL, SP = "tensor", "scalar", "vector", "gpsimd", "sync"
ENGINES = [PE, ACT, DVE, POOL, SP]

D = 1024
T = 4096
DFF = 2816
NH = 8
NEG = -30000.0
EPS = 1e-6
TWO_PI = 6.283185307179586


class Region:
    __slots__ = ("name", "last_write", "reads", "dma_sem", "dma_count", "accum", "writers", "last_dma")

    def __init__(self, name, accum=False):
        self.name = name
        self.last_write = None
        self.reads = {}
        self.dma_sem = None
        self.dma_count = 0
        self.accum = accum
        self.writers = {}
        self.last_dma = None


class Instr:
    __slots__ = ("eng", "fn", "deps", "needed", "count", "is_dma", "dma_sem", "dma_val")

    def __init__(self, eng, fn, is_dma=False):
        self.eng = eng
        self.fn = fn
        self.deps = []
        self.needed = False
        self.count = None
        self.is_dma = is_dma
        self.dma_sem = None
        self.dma_val = None


class _Rec:
    def __init__(self):
        self.call = None

    def __getattr__(self, name):
        def f(*a, **k):
            self.call = (name, a, k)
        return f


class Prog:
    def __init__(self, nc):
        self.nc = nc
        self.streams = {e: [] for e in ENGINES}
        self.n_dma_sems = 0

    def _collect(self, ins, reads, writes, extra=()):
        deps = {}

        def add(i):
            if i is not None and i is not ins:
                deps[id(i)] = i
        for i in extra:
            add(i)
        for r in reads:
            if r.accum:
                for i in r.writers.values():
                    add(i)
            else:
                add(r.last_write)
        for w in writes:
            if not w.accum:
                add(w.last_write)
            for i in w.reads.values():
                add(i)
        ins.deps = list(deps.values())
        key = ("s", id(ins.dma_sem)) if ins.is_dma else ins.eng
        for r in reads:
            r.reads[key] = ins
        for w in writes:
            if w.accum:
                w.writers[key] = ins
            else:
                w.last_write = ins
                w.reads = {}

    def op(self, eng, fn, reads=(), writes=()):
        rec = _Rec()
        fn(rec)
        name, a, k = rec.call
        ins = Instr(eng, lambda e, name=name, a=a, k=k: getattr(e, name)(*a, **k))
        self._collect(ins, reads, writes)
        self.streams[eng].append(ins)
        return ins

    def dma(self, queue, out_ap, in_ap, reads=(), writes=(), sem_region=None, **kw):
        outs = out_ap if isinstance(out_ap, (list, tuple)) else [out_ap]
        ins_ = in_ap if isinstance(in_ap, (list, tuple)) else [in_ap]
        n = len(outs)

        def fn(eng, outs=outs, ins_=ins_, kw=kw):
            return [eng.dma_start(out=o, in_=i, **kw) for o, i in zip(outs, ins_)]
        ins = Instr(queue, fn, is_dma=True)
        sr = sem_region
        if sr is None:
            for w in writes:
                if not w.accum:
                    sr = w
                    break
        if sr is None:
            for r in reads:
                if not r.accum:
                    sr = r
                    break
        if sr.dma_sem is None:
            sr.dma_sem = self.nc.alloc_semaphore("d_" + sr.name)
            self.n_dma_sems += 1
        sr.dma_count += 16 * n
        ins.dma_sem = sr.dma_sem
        ins.dma_val = sr.dma_count
        extra = (sr.last_dma,) if sr.last_dma is not None else ()
        sr.last_dma = ins
        self._collect(ins, reads, writes, extra)
        self.streams[queue].append(ins)
        return ins

    def emit(self, final_regions=()):
        nc = self.nc
        for e in ENGINES:
            for ins in self.streams[e]:
                for d in ins.deps:
                    if d.is_dma:
                        continue
                    if d.eng == ins.eng and d.eng == PE and not ins.is_dma:
                        continue
                    d.needed = True
        sems = {e: nc.alloc_semaphore("s_" + e) for e in ENGINES}
        for e in ENGINES:
            c = 0
            for ins in self.streams[e]:
                if not ins.is_dma and ins.needed:
                    c += 1
                    ins.count = c
        streams = self.streams
        final_waits = []
        for r in final_regions:
            for i in r.writers.values():
                final_waits.append((i.dma_sem, i.dma_val))

        def body(e):
            def run(eng):
                waited = {}
                for ins in streams[e]:
                    need = {}
                    for d in ins.deps:
                        if d.is_dma:
                            s, v = d.dma_sem, d.dma_val
                        else:
                            if d.eng == e and e == PE and not ins.is_dma:
                                continue
                            s, v = sems[d.eng], d.count
                        key = id(s)
                        if waited.get(key, 0) >= v:
                            continue
                        if key not in need or need[key][1] < v:
                            need[key] = (s, v)
                    for key, (s, v) in need.items():
                        eng.wait_ge(s, v)
                        waited[key] = v
                    r = ins.fn(eng)
                    if ins.is_dma:
                        for rr_ in r:
                            rr_.then_inc(ins.dma_sem, 16)
                    elif ins.needed:
                        r.then_inc(sems[e], 1)
                if e == SP:
                    for (s, v) in final_waits:
                        eng.wait_ge(s, v)
            return run

        with nc.Block() as block:
            block.sync(body(SP))
            block.tensor(body(PE))
            block.scalar(body(ACT))
            block.vector(body(DVE))
            block.gpsimd(body(POOL))


class Buf:
    def __init__(self, B, off, dtype, shape):
        self.B = B
        self.off = off
        self.dtype = dtype
        esz = 2 if dtype == BF16 else 4
        n = 1
        for s in shape:
            n *= s
        self.size = n * esz
        assert off % 4 == 0
        w0 = off // 4
        w1 = (off + self.size + 3) // 4
        v = B.arena[:, w0:w1]
        if dtype != F32:
            v = v.bitcast(dtype)
        if len(shape) == 2:
            v = v.rearrange("p (a b) -> p a b", b=shape[1])
        elif len(shape) == 3:
            v = v.rearrange("p (a b c) -> p a b c", b=shape[1], c=shape[2])
        elif len(shape) == 4:
            v = v.rearrange("p (a b c d) -> p a b c d", b=shape[1], c=shape[2], d=shape[3])
        self.t = v
        self.R = self.rr(0, self.size)

    def rr(self, lo, hi):
        p0 = (self.off + lo) // 1024
        p1 = (self.off + hi - 1) // 1024
        return [self.B.pages[i] for i in range(p0, p1 + 1)]


KIB = 1024


class SB:
    def __init__(self, nc, name, free, dtype):
        self.tensor = nc.alloc_sbuf_tensor(name, [128, free], dtype)
        self.t = self.tensor[:, :]
        self.R = [Region(name)]

    def rr(self, lo, hi):
        return self.R


class Builder:
    def __init__(self, n_layers=4, dbg=False):
        self.n_layers = n_layers
        nc = bass.Bass("TRN2", target_bir_lowering=False)
        self.nc = nc
        self.P = Prog(nc)
        dt = nc.dram_tensor
        self.x = dt("x", [T, D], F32, kind="ExternalInput").ap()
        self.c = dt("c", [1, D], F32, kind="ExternalInput").ap()
        self.pos = dt("pos", [1, T], I32, kind="ExternalInput").ap()
        self.cst = dt("cst", [128, 1680], F32, kind="ExternalInput").ap()
        self.ada_w = dt("ada_w", [4, D, 6 * D], F32, kind="ExternalInput").ap()
        self.ada_b = dt("ada_b", [4, 6 * D], F32, kind="ExternalInput").ap()
        self.norm_g = dt("norm_g", [4, 4, D], F32, kind="ExternalInput").ap()
        self.hg_w_in = dt("hgrn_w_in", [2, D, 4 * D], F32, kind="ExternalInput").ap()
        self.hg_lb = dt("hgrn_lower_bounds", [2, D], F32, kind="ExternalInput").ap()
        self.hg_ng = dt("hgrn_norm_g", [2, D], F32, kind="ExternalInput").ap()
        self.hg_w_out = dt("hgrn_w_out", [2, D, D], F32, kind="ExternalInput").ap()
        self.at_w_in = dt("attn_w_in", [2, D, 9 * D], F32, kind="ExternalInput").ap()
        self.at_w_out = dt("attn_w_out", [2, D, D], F32, kind="ExternalInput").ap()
        self.ff_w_in = dt("ffn_w_in", [4, D, 2 * DFF], F32, kind="ExternalInput").ap()
        self.ff_w_out = dt("ffn_w_out", [4, DFF, D], F32, kind="ExternalInput").ap()
        self.out = dt("out", [T, D], F32, kind="ExternalOutput").ap()
        self.actT = dt("actT", [DFF, T], BF16).ap()
        self.oT = dt("oT", [D, T], BF16).ap()
        self.QT = dt("QT", [NH, 128, T], BF16).ap()
        self.KT = dt("KT", [NH, 128, T], BF16).ap()
        self.Vd = dt("Vd", [NH, 128, 32, 128], BF16).ap()
        self.Og = dt("Og", [3, T, NH, 136], F32).ap()
        self.rope = dt("rope", [3, 2, 128, T], F32).ap()
        self.R_out = [Region(f"out{i}", accum=True) for i in range(8)]
        self.R_x = Region("x", accum=True)
        self.R_actT = Region("actT", accum=True)
        self.R_oT = Region("oT", accum=True)
        self.R_QKV = Region("QKV", accum=True)
        self.R_Og = Region("Og", accum=True)
        self.R_rope = Region("rope", accum=True)
        self.R_in = Region("inputs", accum=True)
        self.ARENA_KIB = 190
        self.arena = nc.alloc_sbuf_tensor("arena", [128, self.ARENA_KIB * 256], F32)
        self.pages = [Region(f"pg{i}") for i in range(self.ARENA_KIB)]
        self.ps = [nc.alloc_psum_tensor(f"ps{i}", [128, 512], F32) for i in range(8)]
        self.PB = [Region(f"psb{i}") for i in range(8)]
        B = lambda off, dtype, shape: Buf(self, off, dtype, shape)
        self.mk = B
        self.cstb = B(0, F32, [1680])
        c = self.cstb.t
        self.ident = c[:, 0:128]
        self.ones = c[:, 128:256]
        self.maskT = c[:, 256:384]
        self.m2 = c[:, 384:640]
        self.scanmask = c[:, 640:1664]
        self.invf = c[:, 1664:1665]
        self.epsc = c[:, 1665:1666]
        self.one11 = c[0:1, 1666:1667]
        self.halfpi = c[:, 1667:1668]
        self.cmask = c[:, 1668:1672]
        self.zeroc = c[:, 1672:1673]
        self.identb = B(7 * KIB, BF16, [128])
        self.small = B(7 * KIB + 256, F32, [192])
        s = self.small.t
        self.condc = s[:, 0:8]
        self.modcols = s[:, 8:40]
        self.hgc = s[:, 40:104]
        self.Gb = [B(8 * KIB, F32, [1024]), B(12 * KIB, F32, [1024])]
        self.UT = 16 * KIB
        self.ZW = 80 * KIB
        self.ZX = 128 * KIB
        self.uT = B(self.UT, BF16, [8, T])
        self.wst = [B(self.ZW, F32, [8, 512]), B(self.ZW + 16 * KIB, F32, [8, 512])]
        self.wbf = [B(self.ZW + 32 * KIB, BF16, [8, 512]), B(self.ZW + 40 * KIB, BF16, [8, 512])]
        mc2 = SB(nc, "modcols2", 32, F32)
        self.modsets = [(self.modcols, self.small.R, self.Gb),
                        (mc2.t, mc2.R, [SB(nc, "gb2a", 1024, F32), SB(nc, "gb2b", 1024, F32)])]
        self.set_layer(0)
        self.maskb = nc.alloc_sbuf_tensor("maskb", [128, 256], BF16)
        self.R_maskb = [Region("maskb")]
        self.psi = 0
        self.bank_pool = list(range(8))
        self.wi = 0

    def set_layer(self, l):
        self.modcols, self.modR, self.Gb = self.modsets[l % 2]

    def bank(self):
        pool = self.bank_pool
        i = pool[self.psi % len(pool)]
        self.psi += 1
        return self.ps[i], self.PB[i]

    def init_consts(self):
        P = self.P
        P.dma(SP, self.cstb.t, self.cst, reads=[self.R_in], writes=self.cstb.R)
        P.op(POOL, lambda e: e.tensor_copy(out=self.identb.t, in_=self.ident),
             reads=self.cstb.R, writes=self.identb.R)
        P.op(POOL, lambda e: e.tensor_copy(out=self.maskb[:, :], in_=self.m2), reads=self.cstb.R, writes=self.R_maskb)

    def columnize(self, row_ap, row_R, out_ap, out_R, nch, func=None):
        P = self.P
        ps, pr = self.bank()
        for cch in range(nch):
            P.op(PE, lambda e, cch=cch, ps=ps: e.matmul(ps[:, cch:cch + 1], lhsT=row_ap[0:1, cch * 128:(cch + 1) * 128],
                                                        rhs=self.one11, start=True, stop=True),
                 reads=row_R + self.cstb.R, writes=[pr])
        if func is None:
            P.op(DVE, lambda e, ps=ps: e.tensor_copy(out=out_ap, in_=ps[:, 0:nch]), reads=[pr], writes=out_R)
        else:
            P.op(ACT, lambda e, ps=ps: e.activation(out=out_ap, in_=ps[:, 0:nch], func=func), reads=[pr], writes=out_R)

    def rstd_from_ss(self, ss_ap, R, n_div):
        P = self.P
        npart = ss_ap.shape[0]
        P.op(ACT, lambda e: e.activation(out=ss_ap, in_=ss_ap, func=AF.Ln, scale=1.0 / n_div, bias=self.epsc[0:npart, :]),
             reads=R + self.cstb.R, writes=R)
        P.op(ACT, lambda e: e.activation(out=ss_ap, in_=ss_ap, func=AF.Exp, scale=-0.5), reads=R, writes=R)

    def prologue(self):
        P = self.P
        Z = self.ZX
        rowa = self.mk(Z, F32, [1024])
        rowb = self.mk(Z + 4 * KIB, F32, [1024])
        P.dma(SP, rowa.t[0:1, :], self.c, reads=[self.R_in], writes=rowa.R)
        self.columnize(rowa.t, rowa.R, self.condc, self.small.R, 8, func=AF.Silu)
        hgc = self.hgc
        for idx in range(2):
            base = idx * 32
            C1 = hgc[:, base:base + 8]
            C2 = hgc[:, base + 8:base + 16]
            NC1 = hgc[:, base + 16:base + 24]
            HNG = hgc[:, base + 24:base + 32]
            if idx == 0:
                P.op(DVE, lambda e, C1=C1: e.memset(C1, 0.5), writes=self.small.R)
                P.op(DVE, lambda e, C2=C2: e.memset(C2, 0.5), writes=self.small.R)
                P.op(DVE, lambda e, NC1=NC1: e.memset(NC1, -0.5), writes=self.small.R)
            else:
                P.dma(SP, rowa.t[0:1, :], self.hg_lb[0:1, :], reads=[self.R_in], writes=rowa.R)
                P.dma(SP, rowb.t[0:1, :], self.hg_lb[1:2, :], reads=[self.R_in], writes=rowb.R)
                P.op(DVE, lambda e: e.tensor_tensor(out=rowa.t[0:1, :], in0=rowa.t[0:1, :], in1=rowb.t[0:1, :], op=ALU.subtract),
                     reads=rowa.R + rowb.R, writes=rowa.R)
                self.columnize(rowa.t, rowa.R, C1, self.small.R, 8, func=AF.Exp)
                P.op(DVE, lambda e, C1=C1: e.tensor_scalar(out=C1, in0=C1, scalar1=1.0, scalar2=None, op0=ALU.add),
                     reads=self.small.R, writes=self.small.R)
                P.op(DVE, lambda e, C1=C1, C2=C2: e.reciprocal(out=C2, in_=C1), reads=self.small.R, writes=self.small.R)
                P.op(DVE, lambda e, C1=C1, C2=C2: e.tensor_scalar(out=C1, in0=C2, scalar1=-0.5, scalar2=0.5, op0=ALU.mult, op1=ALU.add),
                     reads=self.small.R, writes=self.small.R)
                P.op(DVE, lambda e, C1=C1, NC1=NC1: e.tensor_scalar(out=NC1, in0=C1, scalar1=-1.0, scalar2=None, op0=ALU.mult),
                     reads=self.small.R, writes=self.small.R)
                P.op(DVE, lambda e, C2=C2: e.tensor_scalar(out=C2, in0=C2, scalar1=0.5, scalar2=0.5, op0=ALU.mult, op1=ALU.add),
                     reads=self.small.R, writes=self.small.R)
            P.dma(SP, rowb.t[0:1, :], self.hg_ng[idx:idx + 1, :], reads=[self.R_in], writes=rowb.R)
            self.columnize(rowb.t, rowb.R, HNG, self.small.R, 8)
        posi = self.mk(self.UT, I32, [T])
        ang = self.mk(self.UT + 16 * KIB, F32, [T])
        t1 = self.mk(self.UT + 32 * KIB, F32, [T])
        t2 = self.mk(self.UT + 48 * KIB, F32, [T])
        ti = self.mk(self.ZW, I32, [T])
        tab = [self.mk(self.ZW + 16 * KIB, F32, [T]), self.mk(self.ZW + 32 * KIB, F32, [T])]
        prm = self.mk(self.ZX + 8 * KIB, F32, [T])
        P.dma(SP, posi.t, self.pos[0:1, :].broadcast_to([128, T]), reads=[self.R_in], writes=posi.R)
        P.op(DVE, lambda e: e.tensor_copy(out=ang.t, in_=posi.t), reads=posi.R, writes=ang.R)
        P.op(DVE, lambda e: e.tensor_scalar(out=ang.t, in0=ang.t, scalar1=self.invf, scalar2=None, op0=ALU.mult),
             reads=ang.R + self.cstb.R, writes=ang.R)
        for which in range(2):
            src = ang
            if which == 0:
                P.op(DVE, lambda e: e.tensor_scalar(out=t2.t, in0=ang.t, scalar1=float(np.pi / 2), scalar2=None, op0=ALU.add),
                     reads=ang.R, writes=t2.R)
                src = t2
            P.op(DVE, lambda e, src=src: e.tensor_scalar(out=t1.t, in0=src.t, scalar1=float(1.0 / TWO_PI), scalar2=None, op0=ALU.mult),
                 reads=src.R, writes=t1.R)
            P.op(DVE, lambda e: e.tensor_copy(out=ti.t, in_=t1.t), reads=t1.R, writes=ti.R)
            P.op(DVE, lambda e: e.tensor_copy(out=t1.t, in_=ti.t), reads=ti.R, writes=t1.R)
            r = tab[which]
            P.op(DVE, lambda e, src=src, r=r: e.scalar_tensor_tensor(out=r.t, in0=t1.t, scalar=-TWO_PI, in1=src.t, op0=ALU.mult, op1=ALU.add),
                 reads=t1.R + src.R, writes=r.R)
            P.op(DVE, lambda e, r=r: e.tensor_scalar(out=t1.t, in0=r.t, scalar1=float(np.pi), scalar2=-TWO_PI, op0=ALU.is_gt, op1=ALU.mult),
                 reads=r.R, writes=t1.R)
            P.op(DVE, lambda e, r=r: e.tensor_tensor(out=r.t, in0=r.t, in1=t1.t, op=ALU.add), reads=r.R + t1.R, writes=r.R)
            P.op(DVE, lambda e, r=r: e.tensor_scalar(out=t1.t, in0=r.t, scalar1=float(-np.pi), scalar2=TWO_PI, op0=ALU.is_lt, op1=ALU.mult),
                 reads=r.R, writes=t1.R)
            P.op(DVE, lambda e, r=r: e.tensor_tensor(out=r.t, in0=r.t, in1=t1.t, op=ALU.add), reads=r.R + t1.R, writes=r.R)
            P.op(DVE, lambda e, r=r: e.tensor_scalar(out=r.t, in0=r.t, scalar1=3.14159, scalar2=-3.14159, op0=ALU.min, op1=ALU.max),
                 reads=r.R, writes=r.R)
            P.op(ACT, lambda e, r=r: e.activation(out=r.t, in_=r.t, func=AF.Sin), reads=r.R, writes=r.R)
        for g, d in enumerate((1, 4, 16)):
            for which in range(2):
                if d == 1:
                    src = tab[which]
                else:
                    P.op(POOL, lambda e, which=which, d=d: e.tensor_copy(
                        out=prm.t.rearrange("p (r m) -> p r m", r=d), in_=tab[which].t.rearrange("p (m r) -> p r m", r=d)),
                        reads=tab[which].R, writes=prm.R)
                    src = prm
                P.dma(SP, self.rope[g, which], src.t, reads=src.R, writes=[self.R_rope])

    def adaln(self, l):
        for _ in self.adaln_gen(l):
            pass

    def adaln_gen(self, l):
        P = self.P
        Z = self.ZW + 32 * KIB
        modcols, modR, Gbs = self.modsets[l % 2]
        brow = self.mk(Z, F32, [512])
        nrow = self.mk(Z + 2 * KIB, F32, [512])
        rowt = self.mk(Z + 4 * KIB, F32, [512])
        def load_ada(s_):
            if s_ < 12:
                P.dma(SP, self.wst[s_ % 2].t, self.ada_w[l, :, s_ * 512:(s_ + 1) * 512].rearrange("(k p) n -> p k n", p=128),
                      reads=[self.R_in], writes=self.wst[s_ % 2].R)
        load_ada(0)
        for s in range(12):
            v, half = s // 2, s % 2
            w = self.wst[s % 2]
            load_ada(s + 1)
            P.dma(SP, brow.t[0:1, :], self.ada_b[l:l + 1, s * 512:(s + 1) * 512], reads=[self.R_in], writes=brow.R)
            ps, pr = self.bank()
            for kc in range(8):
                P.op(PE, lambda e, kc=kc, ps=ps, w=w: e.matmul(ps[0:1, :], lhsT=self.condc[:, kc:kc + 1], rhs=w.t[:, kc, :],
                                                               start=(kc == 0), stop=(kc == 7)),
                     reads=self.small.R + w.R, writes=[pr])
            P.op(DVE, lambda e, ps=ps: e.tensor_tensor(out=rowt.t[0:1, :], in0=ps[0:1, :], in1=brow.t[0:1, :], op=ALU.add),
                 reads=[pr] + brow.R, writes=rowt.R)
            if v in (1, 2, 4, 5):
                gi = {1: 0, 2: 1, 4: 2, 5: 3}[v]
                P.dma(SP, nrow.t[0:1, :], self.norm_g[l, gi:gi + 1, half * 512:(half + 1) * 512], reads=[self.R_in], writes=nrow.R)
                P.op(DVE, lambda e: e.scalar_tensor_tensor(out=rowt.t[0:1, :], in0=rowt.t[0:1, :], scalar=1.0, in1=nrow.t[0:1, :],
                                                           op0=ALU.add, op1=ALU.mult),
                     reads=rowt.R + nrow.R, writes=rowt.R)
            if v in (0, 1, 3, 4):
                base = {1: 0, 0: 8, 4: 16, 3: 24}[v] + 4 * half
                self.columnize(rowt.t, rowt.R, modcols[:, base:base + 4], modR, 4)
            else:
                G = Gbs[0 if v == 2 else 1]
                ps2, pr2 = self.bank()
                P.op(PE, lambda e, ps2=ps2: e.matmul(ps2[:, :], lhsT=self.ones[0:1, :], rhs=rowt.t[0:1, :], start=True, stop=True),
                     reads=self.cstb.R + rowt.R, writes=[pr2])
                P.op(ACT, lambda e, ps2=ps2, G=G, half=half: e.copy(out=G.t[:, half * 512:(half + 1) * 512], in_=ps2[:, :]),
                     reads=[pr2], writes=G.rr(half * 2048, half * 2048 + 2048))
            yield s

    def tok_rows(self, src, d, pos0, n):
        L = T // d
        r, m = pos0 // L, pos0 % L
        t0 = m * d + r
        return bass.AP(tensor=src.tensor, offset=src.offset + t0 * D, ap=[[d * D, n], [1, D]])

    def norm_T(self, first, d, mset):
        P = self.P
        src = self.x if first else self.out
        Z = self.ZX
        hin = [self.mk(Z, F32, [4, 1024]), self.mk(Z + 16 * KIB, F32, [4, 1024])]
        xn = self.mk(Z + 32 * KIB, F32, [4, 1024])
        junk = self.mk(Z + 48 * KIB, BF16, [1024])
        ssb = self.mk(Z + 50 * KIB, F32, [8])
        Ac = self.modcols[:, mset * 16:mset * 16 + 8]
        Bc = self.modcols[:, mset * 16 + 8:mset * 16 + 16]
        srcR = [self.R_x] if first else (self.R_out if d > 1 else None)
        for tt in range(8):
            h = hin[tt % 2]
            rr = srcR if srcR is not None else [self.R_out[tt]]
            P.dma(SP, [h.t[:, j, :] for j in range(4)], [self.tok_rows(src, d, tt * 512 + j * 128, 128) for j in range(4)],
                  reads=rr, writes=h.R)
            ss = ssb.t[:, (tt % 2) * 4:(tt % 2) * 4 + 4]
            for j in range(4):
                P.op(ACT, lambda e, h=h, j=j, ss=ss: e.activation(out=junk.t, in_=h.t[:, j, :], func=AF.Square, accum_out=ss[:, j:j + 1]),
                     reads=h.rr(j * 4096, j * 4096 + 4096), writes=junk.R + ssb.R)
            self.rstd_from_ss(ss, ssb.R, D)
            for j in range(4):
                P.op(DVE, lambda e, h=h, j=j, ss=ss: e.tensor_scalar(out=xn.t[:, j, :], in0=h.t[:, j, :], scalar1=ss[:, j:j + 1], scalar2=None, op0=ALU.mult),
                     reads=h.rr(j * 4096, j * 4096 + 4096) + ssb.R, writes=xn.rr(j * 4096, j * 4096 + 4096))
            for cch in range(8):
                ps, pr = self.bank()
                for j in range(4):
                    P.op(PE, lambda e, ps=ps, j=j, cch=cch: e.transpose(out=ps[:, j * 128:(j + 1) * 128], in_=xn.t[:, j, cch * 128:(cch + 1) * 128], identity=self.ident),
                         reads=xn.rr(j * 4096 + cch * 512, j * 4096 + cch * 512 + 512) + self.cstb.R, writes=[pr])
                o = self.uT.t[:, cch, tt * 512:(tt + 1) * 512]
                oR = self.uT.rr((cch * T + tt * 512) * 2, (cch * T + tt * 512 + 512) * 2)
                if cch % 2 == 0:
                    P.op(ACT, lambda e, ps=ps, o=o, cch=cch: e.activation(out=o, in_=ps[:, :], func=AF.Identity, scale=Ac[:, cch:cch + 1], bias=Bc[:, cch:cch + 1]),
                         reads=[pr] + self.modR, writes=oR)
                else:
                    P.op(DVE, lambda e, ps=ps, o=o, cch=cch: e.tensor_scalar(out=o, in0=ps[:, :], scalar1=Ac[:, cch:cch + 1], scalar2=Bc[:, cch:cch + 1], op0=ALU.mult, op1=ALU.add),
                         reads=[pr] + self.modR, writes=oR)

    def load_w(self, dram_ap, ncols, cast_views=None):
        P = self.P
        i = self.wi % 2
        self.wi += 1
        st, wb = self.wst[i], self.wbf[i]
        P.dma(SP, st.t[:, :, 0:ncols], dram_ap.rearrange("(k p) n -> p k n", p=128), reads=[self.R_in], writes=st.R)
        P.op(POOL, lambda e: e.tensor_copy(out=wb.t[:, :, 0:ncols], in_=st.t[:, :, 0:ncols]), reads=st.R, writes=wb.R)
        return wb

    def resid_setup(self, base, hbase):
        self.r_h = [self.mk(hbase + i * 4 * KIB, F32, [1024]) for i in range(3)]
        self.r_t = [self.mk(base + i * 4 * KIB, F32, [1024]) for i in range(2)]
        self.r_ss = self.mk(base + 8 * KIB, F32, [8])
        self.r_junk = self.mk(base + 9 * KIB, BF16, [512])
        self.r_i = 0

    def resid_prefetch(self, first, st):
        if st >= 32:
            return
        h = self.r_h[st % 3]
        src = self.x if first else self.out
        Rsrc = [self.R_x] if first else [self.R_out[st // 4]]
        self.P.dma(SP, h.t, src[st * 128:(st + 1) * 128, :], reads=Rsrc, writes=h.R)

    def resid(self, first, st, banks, G):
        P = self.P
        i = self.r_i % 2
        self.r_i += 1
        h, t = self.r_h[st % 3], self.r_t[i]
        ss = self.r_ss.t[:, i * 4:i * 4 + 2]
        sst = self.r_ss.t[:, i * 4 + 2:i * 4 + 3]
        for n, (ps, pr) in enumerate(banks):
            P.op(ACT, lambda e, ps=ps, n=n, ss=ss: e.activation(out=self.r_junk.t, in_=ps[:, :], func=AF.Square, accum_out=ss[:, n:n + 1]),
                 reads=[pr], writes=self.r_junk.R + self.r_ss.R)
        P.op(DVE, lambda e, ss=ss, sst=sst: e.tensor_tensor(out=sst, in0=ss[:, 0:1], in1=ss[:, 1:2], op=ALU.add),
             reads=self.r_ss.R, writes=self.r_ss.R)
        self.rstd_from_ss(sst, self.r_ss.R, D)
        for n, (ps, pr) in enumerate(banks):
            P.op(DVE, lambda e, ps=ps, n=n, sst=sst, t=t: e.scalar_tensor_tensor(out=t.t[:, n * 512:(n + 1) * 512], in0=ps[:, :], scalar=sst, in1=G.t[:, n * 512:(n + 1) * 512], op0=ALU.mult, op1=ALU.mult),
                 reads=[pr] + self.r_ss.R + G.rr(n * 2048, n * 2048 + 2048), writes=t.rr(n * 2048, n * 2048 + 2048))
        P.op(POOL, lambda e, t=t, h=h: e.tensor_tensor(out=t.t, in0=t.t, in1=h.t, op=ALU.add), reads=t.R + h.R, writes=t.R)
        P.dma(SP, self.out[st * 128:(st + 1) * 128, :], t.t, reads=t.R, writes=[self.R_out[st // 4]])

    def ffn(self, l):
        P = self.P
        self.norm_T(False, 1, 1)
        Z = self.ZX
        sg = [self.mk(Z + i * 2 * KIB, F32, [512]) for i in range(2)]
        ao = [self.mk(Z + 4 * KIB + i * KIB, BF16, [512]) for i in range(4)]
        k = 0

        def load_slab(s):
            st, wb = self.wst[s % 2], self.wbf[s % 2]
            P.dma(SP, [st.t[:, :, 0:256], st.t[:, :, 256:512]],
                  [self.ff_w_in[l, :, s * 256:(s + 1) * 256].rearrange("(k p) n -> p k n", p=128),
                   self.ff_w_in[l, :, DFF + s * 256:DFF + (s + 1) * 256].rearrange("(k p) n -> p k n", p=128)],
                  reads=[self.R_in], writes=st.R)
            P.op(POOL, lambda e: e.tensor_copy(out=wb.t, in_=st.t), reads=st.R, writes=wb.R)
        load_slab(0)
        for s in range(11):
            wb = self.wbf[s % 2]
            if s + 1 < 11:
                load_slab(s + 1)
            for tt in range(8):
                for cc in range(2):
                    pg, rg = self.bank()
                    pu, ru = self.bank()
                    for which, ps, pr in ((0, pg, rg), (1, pu, ru)):
                        for kc in range(8):
                            P.op(PE, lambda e, ps=ps, kc=kc, wb=wb, which=which, cc=cc, tt=tt: e.matmul(
                                ps[:, :], lhsT=wb.t[:, kc, which * 256 + cc * 128:which * 256 + cc * 128 + 128],
                                rhs=self.uT.t[:, kc, tt * 512:(tt + 1) * 512], start=(kc == 0), stop=(kc == 7)),
                                reads=wb.R + self.uT.rr((kc * T + tt * 512) * 2, (kc * T + tt * 512 + 512) * 2), writes=[pr])
                    sgb = sg[k % 2]
                    aob = ao[k % 4]
                    k += 1
                    P.op(ACT, lambda e, pg=pg, sgb=sgb: e.activation(out=sgb.t, in_=pg[:, :], func=AF.Silu), reads=[rg], writes=sgb.R)
                    P.op(DVE, lambda e, pu=pu, sgb=sgb, aob=aob: e.tensor_tensor(out=aob.t, in0=pu[:, :], in1=sgb.t, op=ALU.mult),
                         reads=[ru] + sgb.R, writes=aob.R)
                    row0 = (s * 2 + cc) * 128
                    P.dma(SP, self.actT[row0:row0 + 128, tt * 512:(tt + 1) * 512], aob.t, reads=aob.R, writes=[self.R_actT])
        wo = self.mk(self.UT, BF16, [22, 1024])
        for kc in range(22):
            i = self.wi % 2
            self.wi += 1
            st = self.wst[i]
            stv = st.t.rearrange("p a b -> p (a b)")[:, 0:1024]
            P.dma(SP, stv, self.ff_w_out[l, kc * 128:(kc + 1) * 128, :], reads=[self.R_in], writes=st.rr(0, 4096))
            P.op(POOL, lambda e, stv=stv, kc=kc: e.tensor_copy(out=wo.t[:, kc, :], in_=stv), reads=st.rr(0, 4096),
                 writes=wo.rr(kc * 2048, kc * 2048 + 2048))
        at = [self.mk(self.ZX + 18 * KIB, BF16, [22, 512]), self.mk(self.ZX + 40 * KIB, BF16, [22, 512])]
        self.resid_setup(self.ZX, self.UT + 48 * KIB)
        gen = self.adaln_gen(l + 1) if l + 1 < self.n_layers else None

        def load_act(tt_):
            if tt_ < 8:
                P.dma(SP, at[tt_ % 2].t, self.actT[:, tt_ * 512:(tt_ + 1) * 512].rearrange("(k p) t -> p k t", p=128), reads=[self.R_actT], writes=at[tt_ % 2].R)
        load_act(0)
        self.resid_prefetch(False, 0)
        for st_ in range(32):
            if gen is not None and st_ >= 2 and st_ % 2 == 0:
                next(gen, None)
            a = at[(st_ // 4) % 2]
            j_ = st_ % 4
            if j_ == 0:
                load_act(st_ // 4 + 1)
            self.resid_prefetch(False, st_ + 1)
            banks = []
            for n in range(2):
                ps, pr = self.bank()
                for kc in range(22):
                    P.op(PE, lambda e, ps=ps, kc=kc, a=a, n=n, j_=j_: e.matmul(ps[:, :], lhsT=a.t[:, kc, j_ * 128:(j_ + 1) * 128], rhs=wo.t[:, kc, n * 512:(n + 1) * 512],
                                                                        start=(kc == 0), stop=(kc == 21)),
                         reads=a.R + wo.rr(kc * 2048 + n * 1024, kc * 2048 + n * 1024 + 1024), writes=[pr])
                banks.append((ps, pr))
            self.resid(False, st_, banks, self.Gb[1])
        if gen is not None:
            for _ in gen:
                pass

    def out_proj(self, w_dram, first):
        P = self.P
        wo = self.mk(self.ZW + 32 * KIB, BF16, [8, 1024])
        for kc in range(8):
            st = self.wst[kc % 2]
            stv = st.t.rearrange("p a b -> p (a b)")[:, 0:1024]
            P.dma(SP, stv, w_dram[kc * 128:(kc + 1) * 128, :], reads=[self.R_in], writes=st.rr(0, 4096))
            P.op(POOL, lambda e, stv=stv, kc=kc: e.tensor_copy(out=wo.t[:, kc, :], in_=stv), reads=st.rr(0, 4096),
                 writes=wo.rr(kc * 2048, kc * 2048 + 2048))
        at = [self.mk(self.UT + i * 8 * KIB, BF16, [8, 512]) for i in range(2)]
        self.resid_setup(self.ZX, self.UT + 48 * KIB)

        def load_o(tt):
            if tt < 8:
                P.dma(SP, at[tt % 2].t, self.oT[:, tt * 512:(tt + 1) * 512].rearrange("(k p) t -> p k t", p=128), reads=[self.R_oT], writes=at[tt % 2].R)
        load_o(0)
        self.resid_prefetch(first, 0)
        for tt in range(8):
            a = at[tt % 2]
            load_o(tt + 1)
            for j in range(4):
                self.resid_prefetch(first, tt * 4 + j + 1)
                banks = []
                for n in range(2):
                    ps, pr = self.bank()
                    for kc in range(8):
                        P.op(PE, lambda e, ps=ps, kc=kc, a=a, n=n, j=j: e.matmul(ps[:, :], lhsT=a.t[:, kc, j * 128:(j + 1) * 128],
                                                                                 rhs=wo.t[:, kc, n * 512:(n + 1) * 512], start=(kc == 0), stop=(kc == 7)),
                             reads=a.R + wo.rr(kc * 2048 + n * 1024, kc * 2048 + n * 1024 + 1024), writes=[pr])
                    banks.append((ps, pr))
                self.resid(first, tt * 4 + j, banks, self.Gb[0])

    def hgrn(self, l):
        P = self.P
        idx = l // 2
        first = (l == 0)
        self.norm_T(first, 1, 0)
        hb = idx * 32
        C1 = self.hgc[:, hb:hb + 8]
        C2 = self.hgc[:, hb + 8:hb + 16]
        NC1 = self.hgc[:, hb + 16:hb + 24]
        HNG = self.hgc[:, hb + 24:hb + 32]
        Z = self.ZX
        N = 1024
        th = self.mk(Z, F32, [N])
        qs = self.mk(Z + 4 * KIB, F32, [N])
        kk = self.mk(Z + 8 * KIB, F32, [N])
        bb = self.mk(Z + 12 * KIB, F32, [N])
        sets = []
        for i in range(2):
            b0 = Z + 16 * KIB + i * 13 * KIB
            sets.append(dict(qd=self.mk(b0, BF16, [N]), ki=self.mk(b0 + 2 * KIB, BF16, [N]), ke=self.mk(b0 + 4 * KIB, BF16, [N]),
                             vtm=self.mk(b0 + 6 * KIB, BF16, [8, 128]), gs=self.mk(b0 + 8 * KIB, F32, [N]), dec=self.mk(b0 + 12 * KIB, F32, [32])))
        ketm = self.mk(Z + 42 * KIB, BF16, [8, 128])
        vexp = self.mk(Z + 44 * KIB, BF16, [8, 4, 128])
        Sbf = self.mk(Z + 52 * KIB, BF16, [32, 128])
        sqo = self.mk(Z + 60 * KIB, F32, [512])
        t1 = sqo
        S32 = self.mk(self.ZW + 24 * KIB, F32, [33, 128])
        amt = self.mk(self.ZW + 41 * KIB, BF16, [4, 128])
        oo = [self.mk(self.ZW + 42 * KIB + i * KIB, BF16, [512]) for i in range(2)]
        rso = self.mk(self.ZW + 44 * KIB, F32, [512])
        stg = self.wst[0]
        wb = self.mk(self.ZW + 16 * KIB, BF16, [8, 4, 128])
        w_in = self.hg_w_in[idx]
        okc = [0]
        tbanks = {}

        def S1a(ui):
            h, qd_ = ui // 4, ui % 4
            B = sets[ui % 2]
            vtm, gs = B["vtm"], B["gs"]
            tok0 = qd_ * N
            if qd_ == 0:
                stv = stg.t.rearrange("p k (a b) -> p k a b", a=4)
                P.dma(SP, [stv[:, :, a, :] for a in range(4)],
                      [w_in[:, a * 1024 + h * 128:a * 1024 + (h + 1) * 128].rearrange("(k p) n -> p k n", p=128) for a in range(4)],
                      reads=[self.R_in], writes=stg.R)
                P.op(POOL, lambda e: e.tensor_copy(out=wb.t, in_=stv), reads=stg.R, writes=wb.R)
            for which, dst, func, scale in ((1, th, AF.Tanh, 0.5), (0, qs, AF.Silu, 1.0), (3, gs, AF.Silu, 1.0)):
                for t2 in range(2):
                    ps, pr = self.bank()
                    for kc in range(8):
                        P.op(PE, lambda e: e.matmul(ps[:, :], lhsT=wb.t[:, kc, which, :], rhs=self.uT.t[:, kc, tok0 + t2 * 512:tok0 + (t2 + 1) * 512],
                                                    start=(kc == 0), stop=(kc == 7)),
                             reads=wb.R + self.uT.rr((kc * T + tok0 + t2 * 512) * 2, (kc * T + tok0 + t2 * 512 + 512) * 2), writes=[pr])
                    P.op(ACT, lambda e: e.activation(out=dst.t[:, t2 * 512:(t2 + 1) * 512], in_=ps[:, :], func=func, scale=scale),
                         reads=[pr], writes=dst.rr(t2 * 2048, t2 * 2048 + 2048))
            for t2 in range(2):
                ps, pr = self.bank()
                for j in range(4):
                    for kc in range(8):
                        p0 = tok0 + t2 * 512 + j * 128
                        P.op(PE, lambda e: e.matmul(ps[:, j * 128:(j + 1) * 128], lhsT=self.uT.t[:, kc, p0:p0 + 128], rhs=wb.t[:, kc, 2, :],
                                                    start=(kc == 0), stop=(kc == 7)),
                             reads=wb.R + self.uT.rr((kc * T + p0) * 2, (kc * T + p0 + 128) * 2), writes=[pr])
                P.op(ACT, lambda e: e.copy(out=vtm.t[:, t2 * 4:(t2 + 1) * 4, :].rearrange("p a b -> p (a b)"), in_=ps[:, :]),
                     reads=[pr], writes=vtm.rr(t2 * 1024, t2 * 1024 + 1024))

        def S1b(ui):
            h, qd_ = ui // 4, ui % 4
            B = sets[ui % 2]
            qd, ki, ke, dec = B["qd"], B["ki"], B["ke"], B["dec"]
            c1, c2, nc1 = C1[:, h:h + 1], C2[:, h:h + 1], NC1[:, h:h + 1]
            P.op(DVE, lambda e: e.tensor_scalar(out=kk.t, in0=th.t, scalar1=nc1, scalar2=c1, op0=ALU.mult, op1=ALU.add),
                 reads=th.R + self.small.R, writes=kk.R)
            yield
            P.op(DVE, lambda e: e.tensor_scalar(out=th.t, in0=th.t, scalar1=c1, scalar2=c2, op0=ALU.mult, op1=ALU.add),
                 reads=th.R + self.small.R, writes=th.R)
            P.op(ACT, lambda e: e.activation(out=th.t, in_=th.t, func=AF.Ln), reads=th.R, writes=th.R)
            yield
            P.op(DVE, lambda e: e.tensor_tensor_scan(out=bb.t, data0=self.scanmask, data1=th.t, initial=0.0, op0=ALU.mult, op1=ALU.add),
                 reads=th.R + self.cstb.R, writes=bb.R)
            P.op(ACT, lambda e: e.activation(out=th.t, in_=bb.t, func=AF.Exp), reads=bb.R, writes=th.R)
            b3 = bb.t.rearrange("p (n c) -> p n c", c=32)
            blast = b3[:, :, 31:32]
            P.op(ACT, lambda e: e.activation(out=dec.t, in_=blast.rearrange("p n c -> p (n c)"), func=AF.Exp), reads=bb.R, writes=dec.R)
            yield
            P.op(DVE, lambda e: e.tensor_tensor(out=qd.t, in0=qs.t, in1=th.t, op=ALU.mult), reads=qs.R + th.R, writes=qd.R)
            P.op(ACT, lambda e: e.activation(out=th.t, in_=bb.t, func=AF.Exp, scale=-1.0), reads=bb.R, writes=th.R)
            yield
            P.op(DVE, lambda e: e.tensor_tensor(out=ki.t, in0=kk.t, in1=th.t, op=ALU.mult), reads=kk.R + th.R, writes=ki.R)
            yield
            P.op(DVE, lambda e: e.tensor_tensor(out=th.t.rearrange("p (n c) -> p n c", c=32), in0=blast.broadcast_to([128, 32, 32]),
                                                in1=b3, op=ALU.subtract), reads=bb.R, writes=th.R)
            P.op(ACT, lambda e: e.activation(out=th.t, in_=th.t, func=AF.Exp), reads=th.R, writes=th.R)
            yield
            P.op(DVE, lambda e: e.tensor_tensor(out=ke.t, in0=kk.t, in1=th.t, op=ALU.mult), reads=kk.R + th.R, writes=ke.R)
            yield

        def S2T(ui):
            ke = sets[ui % 2]["ke"]
            tbanks[ui] = []
            for t2 in range(2):
                ps, pr = self.ps[6 + t2], self.PB[6 + t2]
                pb = ps[:, :].bitcast(BF16)
                for j in range(4):
                    blk = t2 * 4 + j
                    P.op(PE, lambda e: e.transpose(out=pb[:, j * 128:(j + 1) * 128], in_=ke.t[:, blk * 128:(blk + 1) * 128], identity=self.identb.t),
                         reads=ke.R + self.identb.R, writes=[pr])
                tbanks[ui].append((pb, pr))

        def S2E(ui):
            vtm = sets[ui % 2]["vtm"]
            for t2 in range(2):
                pb, pr = tbanks[ui][t2]
                P.op(DVE, lambda e: e.tensor_copy(out=ketm.t[:, t2 * 4:(t2 + 1) * 4, :].rearrange("p a b -> p (a b)"), in_=pb[:, 0:512]),
                     reads=[pr], writes=ketm.rr(t2 * 1024, t2 * 1024 + 1024))
            for i4 in range(4):
                P.op(POOL, lambda e: e.tensor_scalar(out=vexp.t[:, :, i4, :], in0=vtm.t, scalar1=self.cmask[:, i4:i4 + 1], scalar2=1.0,
                                                     op0=ALU.mult, op1=ALU.mult),
                     reads=vtm.R + self.cstb.R, writes=vexp.R)

        def S2K(ui):
            qd_ = ui % 4
            dec = sets[ui % 2]["dec"]
            if qd_ == 0:
                P.op(DVE, lambda e: e.memset(S32.t[:, 0, :], 0.0), writes=S32.rr(0, 512))
            else:
                P.op(DVE, lambda e: e.tensor_copy(out=S32.t[:, 0, :], in_=S32.t[:, 32, :]), reads=S32.rr(32 * 512, 33 * 512), writes=S32.rr(0, 512))
            for blk in range(8):
                ps, pr = self.bank()
                P.op(PE, lambda e: e.matmul(ps[:, :], lhsT=ketm.t[:, blk, :], rhs=vexp.t[:, blk, :, :].rearrange("p a b -> p (a b)"), start=True, stop=True),
                     reads=ketm.R + vexp.R, writes=[pr])
                for i4 in range(4):
                    j = blk * 4 + i4
                    P.op(DVE, lambda e: e.scalar_tensor_tensor(
                        out=S32.t[:, j + 1, :], in0=S32.t[:, j, :], scalar=dec.t[:, j:j + 1], in1=ps[:, i4 * 128:(i4 + 1) * 128], op0=ALU.mult, op1=ALU.add),
                        reads=S32.rr(j * 512, j * 512 + 512) + dec.R + [pr], writes=S32.rr((j + 1) * 512, (j + 1) * 512 + 512))
                yield
            P.op(ACT, lambda e: e.copy(out=Sbf.t, in_=S32.t[:, 0:32, :]), reads=S32.R, writes=Sbf.R)

        def S2R(ui):
            h, qd_ = ui // 4, ui % 4
            B = sets[ui % 2]
            qd, ki, vtm, gs = B["qd"], B["ki"], B["vtm"], B["gs"]
            tok0 = qd_ * N
            for t2 in range(2):
                pa, ra = self.bank()
                for j in range(4):
                    blk = t2 * 4 + j
                    P.op(PE, lambda e: e.matmul(pa[:, j * 128:(j + 1) * 128], lhsT=ki.t[:, blk * 128:(blk + 1) * 128],
                                                rhs=qd.t[:, blk * 128:(blk + 1) * 128], start=True, stop=True),
                         reads=ki.R + qd.R, writes=[ra])
                P.op(DVE, lambda e: e.tensor_tensor(out=amt.t, in0=pa[:, :].rearrange("p (a b) -> p a b", a=4),
                                                    in1=self.maskT.unsqueeze(1).broadcast_to([128, 4, 128]), op=ALU.mult),
                     reads=[ra] + self.cstb.R, writes=amt.R)
                po, ro = self.bank()
                for j in range(4):
                    blk = t2 * 4 + j
                    P.op(PE, lambda e: e.matmul(po[:, j * 128:(j + 1) * 128], lhsT=vtm.t[:, blk, :], rhs=amt.t[:, j, :], start=True, stop=False),
                         reads=vtm.R + amt.R, writes=[ro])
                    for i4 in range(4):
                        ch = blk * 4 + i4
                        P.op(PE, lambda e: e.matmul(po[:, j * 128 + i4 * 32:j * 128 + (i4 + 1) * 32], lhsT=Sbf.t[:, ch, :], rhs=qd.t[:, ch * 32:(ch + 1) * 32],
                                                    start=False, stop=(i4 == 3)),
                             reads=Sbf.R + qd.R, writes=[ro])
                P.op(ACT, lambda e: e.activation(out=sqo.t, in_=po[:, :], func=AF.Square), reads=[ro], writes=sqo.R)
                pn, rn = self.bank()
                P.op(PE, lambda e: e.matmul(pn[:, :], lhsT=self.ones, rhs=sqo.t, start=True, stop=True), reads=self.cstb.R + sqo.R, writes=[rn])
                P.op(ACT, lambda e: e.activation(out=rso.t, in_=pn[:, :], func=AF.Ln, scale=1.0 / 128, bias=self.epsc), reads=[rn] + self.cstb.R, writes=rso.R)
                P.op(ACT, lambda e: e.activation(out=rso.t, in_=rso.t, func=AF.Exp, scale=-0.5), reads=rso.R, writes=rso.R)
                P.op(DVE, lambda e: e.tensor_tensor(out=t1.t, in0=po[:, :], in1=rso.t, op=ALU.mult), reads=[ro] + rso.R, writes=t1.R)
                o = oo[okc[0] % 2]
                okc[0] += 1
                P.op(DVE, lambda e: e.scalar_tensor_tensor(out=o.t, in0=t1.t, scalar=HNG[:, h:h + 1], in1=gs.t[:, t2 * 512:(t2 + 1) * 512],
                                                           op0=ALU.mult, op1=ALU.mult),
                     reads=t1.R + self.small.R + gs.rr(t2 * 2048, t2 * 2048 + 2048), writes=o.R)
                P.dma(SP, self.oT[h * 128:(h + 1) * 128, tok0 + t2 * 512:tok0 + (t2 + 1) * 512], o.t, reads=o.R, writes=[self.R_oT])

        NU = NH * 4
        self.bank_pool = list(range(6))
        S1a(0)
        for _ in S1b(0):
            pass
        for ui in range(NU):
            S2T(ui)
            nxt = ui + 1 < NU
            if nxt:
                S1a(ui + 1)
            S2E(ui)
            g1 = S1b(ui + 1) if nxt else iter(())
            g2 = S2K(ui)
            d1 = d2 = False
            while not (d1 and d2):
                if not d1:
                    try:
                        next(g1)
                    except StopIteration:
                        d1 = True
                if not d2:
                    try:
                        next(g2)
                    except StopIteration:
                        d2 = True
            S2R(ui)
        self.bank_pool = list(range(8))
        self.out_proj(self.hg_w_out[idx], first)

    def attn(self, l):
        P = self.P
        idx = l // 2
        w_in = self.at_w_in[idx]
        scale = 128.0 ** -0.5
        Z = self.ZX
        for g, d in enumerate((1, 4, 16)):
            self.norm_T(False, d, 0)
            cs = [[self.mk(Z + (i * 2 + w) * 2 * KIB, F32, [512]) for w in range(2)] for i in range(2)]
            rt = [self.mk(Z + 8 * KIB + i * 2 * KIB, F32, [512]) for i in range(4)]
            ob = [self.mk(Z + 16 * KIB + i * KIB, BF16, [512]) for i in range(4)]
            wqk = self.mk(self.ZW + 32 * KIB, BF16, [8, 8, 128])
            oi = 0
            for qk in range(2):
                dst = self.QT if qk == 0 else self.KT
                col0 = g * 3072 + qk * 1024
                for half in range(2):
                    st = self.wst[half]
                    P.dma(SP, st.t, w_in[:, col0 + half * 512:col0 + (half + 1) * 512].rearrange("(k p) n -> p k n", p=128),
                          reads=[self.R_in], writes=st.R)
                    for kc in range(8):
                        src = st.t[:, kc, :].rearrange("p (h c j) -> p c h j", h=4, c=8)
                        dv = wqk.t[:, kc, :, :].rearrange("p c (h j) -> p c h j", h=8)[:, :, half * 4:(half + 1) * 4, :]
                        P.op(POOL, lambda e, src=src, dv=dv: e.tensor_copy(out=dv, in_=src), reads=st.rr(kc * 2048, kc * 2048 + 2048),
                             writes=wqk.rr(kc * 2048, kc * 2048 + 2048))
                for tt in range(8):
                    cst_ = cs[tt % 2]
                    for w in range(2):
                        P.dma(SP, cst_[w].t, self.rope[g, w, :, tt * 512:(tt + 1) * 512], reads=[self.R_rope], writes=cst_[w].R)
                    pss = []
                    for cch in range(8):
                        ps, pr = self.bank()
                        for kc in range(8):
                            P.op(PE, lambda e, ps=ps, kc=kc, cch=cch, tt=tt: e.matmul(ps[:, :], lhsT=wqk.t[:, kc, cch, :], rhs=self.uT.t[:, kc, tt * 512:(tt + 1) * 512],
                                                                                      start=(kc == 0), stop=(kc == 7)),
                                 reads=wqk.rr(kc * 2048, kc * 2048 + 2048) + self.uT.rr((kc * T + tt * 512) * 2, (kc * T + tt * 512 + 512) * 2), writes=[pr])
                        pss.append((ps, pr))
                        if cch == 1:
                            (pa, ra), (pb_, rb) = pss[0], pss[1]
                            cosb, sinb = cst_[0], cst_[1]
                            qsc = scale if qk == 0 else 1.0
                            for ri, (psx, rx, tb) in enumerate(((pa, ra, cosb), (pb_, rb, sinb), (pa, ra, sinb), (pb_, rb, cosb))):
                                P.op(DVE, lambda e: e.scalar_tensor_tensor(out=rt[ri].t, in0=psx[:, :], scalar=qsc, in1=tb.t, op0=ALU.mult, op1=ALU.mult),
                                     reads=[rx] + tb.R, writes=rt[ri].R)
                            o0, o1 = ob[oi % 4], ob[(oi + 1) % 4]
                            oi += 2
                            P.op(POOL, lambda e, o0=o0: e.tensor_tensor(out=o0.t, in0=rt[0].t, in1=rt[1].t, op=ALU.subtract), reads=rt[0].R + rt[1].R, writes=o0.R)
                            P.op(POOL, lambda e, o1=o1: e.tensor_tensor(out=o1.t, in0=rt[3].t, in1=rt[2].t, op=ALU.add), reads=rt[2].R + rt[3].R, writes=o1.R)
                            outs = [(0, o0), (1, o1)]
                        elif cch >= 2:
                            o0 = ob[oi % 4]
                            oi += 1
                            P.op(ACT, lambda e: e.activation(out=o0.t, in_=ps[:, :], func=AF.Copy, scale=(scale if qk == 0 else 1.0)), reads=[pr], writes=o0.R)
                            outs = [(cch, o0)]
                        else:
                            outs = []
                        for (cc, o) in outs:
                            P.dma(SP, dst[cc, :, tt * 512:(tt + 1) * 512], o.t, reads=o.R, writes=[self.R_QKV])
            vo = [self.mk(Z + 20 * KIB + i * KIB, BF16, [512]) for i in range(2)]
            vi = 0
            for n in range(2):
                wb = self.load_w(w_in[:, g * 3072 + 2048 + n * 512:g * 3072 + 2048 + (n + 1) * 512], 512)
                for st_ in range(32):
                    ps, pr = self.bank()
                    for kc in range(8):
                        P.op(PE, lambda e, ps=ps, kc=kc, st_=st_, wb=wb: e.matmul(ps[:, :], lhsT=self.uT.t[:, kc, st_ * 128:(st_ + 1) * 128], rhs=wb.t[:, kc, :],
                                                                                  start=(kc == 0), stop=(kc == 7)),
                             reads=wb.R + self.uT.rr((kc * T + st_ * 128) * 2, (kc * T + st_ * 128 + 128) * 2), writes=[pr])
                    v = vo[vi % 2]
                    vi += 1
                    P.op(ACT, lambda e, ps=ps, v=v: e.copy(out=v.t, in_=ps[:, :]), reads=[pr], writes=v.R)
                    dap = bass.AP(tensor=self.Vd.tensor, offset=self.Vd.offset + (n * 4) * T * 128 + st_ * 128, ap=[[32 * 128, 128], [T * 128, 4], [1, 128]])
                    P.dma(SP, dap, v.t.rearrange("p (a b) -> p a b", a=4), reads=v.R, writes=[self.R_QKV])
            nb = 32 // d
            qkv = [[self.mk(self.UT + (i * 3 + w) * 8 * KIB, BF16, [T]) for w in range(3)] for i in range(2)]
            NB3 = 6
            pp = [self.mk(Z + i * KIB, BF16, [256]) for i in range(NB3)]
            pT = [self.mk(Z + 6 * KIB + i * KIB, BF16, [2, 128]) for i in range(NB3)]
            Ot = [self.mk(Z + 12 * KIB + i * KIB, F32, [136]) for i in range(NB3)]
            stt_ = [self.mk(Z + 18 * KIB + i * KIB, F32, [8]) for i in range(NB3)]
            def load_head(h_, part):
                if h_ >= NH:
                    return
                qb_, kb_, vb_ = qkv[h_ % 2]
                if part == 0:
                    P.dma(SP, [qb_.t[cc * 16:(cc + 1) * 16, :] for cc in range(8)], [self.QT[cc, h_ * 16:(h_ + 1) * 16, :] for cc in range(8)],
                          reads=[self.R_QKV], writes=qb_.R)
                elif part == 1:
                    P.dma(SP, [kb_.t[cc * 16:(cc + 1) * 16, :] for cc in range(8)], [self.KT[cc, h_ * 16:(h_ + 1) * 16, :] for cc in range(8)],
                          reads=[self.R_QKV], writes=kb_.R)
                else:
                    P.dma(SP, vb_.t, self.Vd[h_].rearrange("p b v -> p (b v)"), reads=[self.R_QKV], writes=vb_.R)
            for part in range(3):
                load_head(0, part)
            for h in range(NH):
                qb, kb, vb = qkv[h % 2]
                vv = vb.t.rearrange("p (b v) -> p b v", v=128)
                stA = {}

                def stage_A(u):
                    hasprev = (u % nb) != 0
                    ps, pr = self.bank()
                    k0 = (u - 1) * 128 if hasprev else u * 128
                    nk = 256 if hasprev else 128
                    mview = self.maskb[:, 0:256] if hasprev else self.maskb[:, 128:256]
                    P.op(PE, lambda e: e.matmul(ps[:, 0:nk], lhsT=qb.t[:, u * 128:(u + 1) * 128], rhs=kb.t[:, k0:k0 + nk], start=True, stop=False),
                         reads=qb.R + kb.R, writes=[pr])
                    P.op(PE, lambda e: e.matmul(ps[:, 0:nk], lhsT=self.identb.t, rhs=mview, start=False, stop=True),
                         reads=self.identb.R + self.R_maskb, writes=[pr])
                    i = u % NB3
                    p_, st = pp[i], stt_[i]
                    P.op(DVE, lambda e: e.tensor_reduce(out=st.t[:, 1:2], in_=ps[:, 0:nk], axis=AX.X, op=ALU.max, negate=True), reads=[pr], writes=st.R)
                    P.op(ACT, lambda e: e.activation(out=p_.t[:, 0:nk], in_=ps[:, 0:nk], func=AF.Exp, bias=st.t[:, 1:2], accum_out=st.t[:, 2:3]),
                         reads=[pr] + st.R, writes=p_.R + st.R)
                    stA[u] = (hasprev, nk)

                def stage_B(u):
                    hasprev, nk = stA[u]
                    i = u % NB3
                    p_, pt = pp[i], pT[i]
                    ps, pr = self.bank()
                    pb = ps[:, :].bitcast(BF16)
                    nparts = 2 if hasprev else 1
                    for a in range(nparts):
                        P.op(PE, lambda e: e.transpose(out=pb[:, a * 128:(a + 1) * 128], in_=p_.t[:, a * 128:(a + 1) * 128], identity=self.identb.t),
                             reads=p_.R + self.identb.R, writes=[pr])
                    P.op(DVE, lambda e: e.tensor_copy(out=pt.t[:, 0:nparts, :].rearrange("p a b -> p (a b)"), in_=pb[:, 0:nparts * 128]),
                         reads=[pr], writes=pt.R)

                def stage_C(u):
                    hasprev, nk = stA[u]
                    i = u % NB3
                    pt, O, st = pT[i], Ot[i], stt_[i]
                    ps, pr = self.bank()
                    if hasprev:
                        P.op(PE, lambda e: e.matmul(ps[:, 0:128], lhsT=pt.t[:, 0, :], rhs=vv[:, u - 1, :], start=True, stop=False), reads=pt.R + vb.R, writes=[pr])
                        P.op(PE, lambda e: e.matmul(ps[:, 0:128], lhsT=pt.t[:, 1, :], rhs=vv[:, u, :], start=False, stop=True), reads=pt.R + vb.R, writes=[pr])
                    else:
                        P.op(PE, lambda e: e.matmul(ps[:, 0:128], lhsT=pt.t[:, 0, :], rhs=vv[:, u, :], start=True, stop=True), reads=pt.R + vb.R, writes=[pr])
                    P.op(DVE, lambda e: e.reciprocal(out=st.t[:, 3:4], in_=st.t[:, 2:3]), reads=st.R, writes=st.R)
                    P.op(DVE, lambda e: e.tensor_scalar(out=O.t[:, 0:128], in0=ps[:, 0:128], scalar1=st.t[:, 3:4], scalar2=None, op0=ALU.mult),
                         reads=[pr] + st.R, writes=O.R)
                    P.op(ACT, lambda e: e.activation(out=st.t[:, 4:5], in_=st.t[:, 2:3], func=AF.Ln), reads=st.R, writes=st.R)
                    P.op(POOL, lambda e: e.tensor_tensor(out=O.t[:, 128:129], in0=st.t[:, 4:5], in1=st.t[:, 1:2], op=ALU.subtract), reads=st.R, writes=O.R)
                    r, n = u // nb, u % nb
                    t0 = n * 128 * d + r
                    dap = bass.AP(tensor=self.Og.tensor, offset=self.Og.offset + g * T * NH * 136 + t0 * NH * 136 + h * 136, ap=[[d * NH * 136, 128], [1, 136]])
                    P.dma(SP, dap, O.t, reads=O.R, writes=[self.R_Og])

                for s in range(32 + 4):
                    if s in (5, 13, 21):
                        load_head(h + 1, (s - 5) // 8)
                    if s < 32:
                        stage_A(s)
                    if 0 <= s - 2 < 32:
                        stage_B(s - 2)
                    if 0 <= s - 4 < 32:
                        stage_C(s - 4)
        ogb = [self.mk(self.UT + i * 14 * KIB, F32, [3, NH, 136]) for i in range(4)]
        cws = [self.mk(self.ZX + 18 * KIB + i * KIB, F32, [64]) for i in range(2)]
        oc = [self.mk(self.ZX + 20 * KIB + i * 4 * KIB, F32, [NH, 128]) for i in range(2)]
        tms = [self.mk(self.ZX + 28 * KIB + i * 4 * KIB, F32, [NH, 128]) for i in range(4)]
        ot = [self.mk(self.ZX + 44 * KIB + i * KIB, BF16, [512]) for i in range(4)]
        oti = 0
        def load_og(st_):
            if st_ < 32:
                og_ = ogb[st_ % 4]
                P.dma(SP, [og_.t[:, g, :, :] for g in range(3)], [self.Og[g, st_ * 128:(st_ + 1) * 128, :, :] for g in range(3)],
                      reads=[self.R_Og], writes=og_.R)
        for i_ in range(3):
            load_og(i_)
        for tt in range(8):
            for j in range(4):
                st_ = tt * 4 + j
                og = ogb[st_ % 4]
                load_og(st_ + 3)
                L = og.t[:, :, :, 128:129].rearrange("p g h o -> p g (h o)")
                cw = cws[st_ % 2]
                mx = cw.t[:, 0:8]
                ee = cw.t[:, 8:32].rearrange("p (g h) -> p g h", g=3)
                den = cw.t[:, 32:40]
                P.op(DVE, lambda e, L=L, mx=mx: e.tensor_tensor(out=mx, in0=L[:, 0, :], in1=L[:, 1, :], op=ALU.max), reads=og.R, writes=cw.R)
                P.op(DVE, lambda e, L=L, mx=mx: e.tensor_tensor(out=mx, in0=mx, in1=L[:, 2, :], op=ALU.max), reads=og.R + cw.R, writes=cw.R)
                P.op(DVE, lambda e, L=L, mx=mx, ee=ee: e.tensor_tensor(out=ee, in0=L, in1=mx.unsqueeze(1).broadcast_to([128, 3, 8]), op=ALU.subtract), reads=og.R + cw.R, writes=cw.R)
                P.op(ACT, lambda e, ee=ee: e.activation(out=ee, in_=ee, func=AF.Exp), reads=cw.R, writes=cw.R)
                P.op(DVE, lambda e, ee=ee, den=den: e.tensor_tensor(out=den, in0=ee[:, 0, :], in1=ee[:, 1, :], op=ALU.add), reads=cw.R, writes=cw.R)
                P.op(DVE, lambda e, ee=ee, den=den: e.tensor_tensor(out=den, in0=den, in1=ee[:, 2, :], op=ALU.add), reads=cw.R, writes=cw.R)
                P.op(DVE, lambda e, den=den: e.reciprocal(out=den, in_=den), reads=cw.R, writes=cw.R)
                P.op(DVE, lambda e, ee=ee, den=den: e.tensor_tensor(out=ee, in0=ee, in1=den.unsqueeze(1).broadcast_to([128, 3, 8]), op=ALU.mult), reads=cw.R, writes=cw.R)
                o = oc[st_ % 2]
                for g in range(3):
                    wg = ee[:, g, :].unsqueeze(2).broadcast_to([128, 8, 128])
                    tm = tms[(st_ % 2) * 2 + (g - 1)] if g > 0 else None
                    dstb = o if g == 0 else tm
                    P.op(DVE, lambda e, og=og, g=g, wg=wg, dstb=dstb: e.tensor_tensor(out=dstb.t, in0=og.t[:, g, :, 0:128], in1=wg, op=ALU.mult),
                         reads=og.R + cw.R, writes=dstb.R)
                    if g > 0:
                        P.op(POOL, lambda e: e.tensor_tensor(out=o.t, in0=o.t, in1=tm.t, op=ALU.add), reads=o.R + tm.R, writes=o.R)
                for hh in range(0, 8, 4):
                    ps, pr = self.bank()
                    for q in range(4):
                        P.op(PE, lambda e, ps=ps, q=q, hh=hh, o=o: e.transpose(out=ps[:, q * 128:(q + 1) * 128], in_=o.t[:, hh + q, :], identity=self.ident),
                             reads=o.R + self.cstb.R, writes=[pr])
                    ob_ = ot[oti % 4]
                    oti += 1
                    P.op(ACT, lambda e, ps=ps, ob_=ob_: e.copy(out=ob_.t, in_=ps[:, :]), reads=[pr], writes=ob_.R)
                    dap = bass.AP(tensor=self.oT.tensor, offset=self.oT.offset + hh * 128 * T + st_ * 128, ap=[[T, 128], [128 * T, 4], [1, 128]])
                    P.dma(SP, dap, ob_.t.rearrange("p (a b) -> p a b", a=4), reads=ob_.R, writes=[self.R_oT])
        self.out_proj(self.at_w_out[idx], False)

    def build(self, phases=None):
        self.init_consts()
        self.prologue()
        self.adaln(0)
        for l in range(self.n_layers):
            self.set_layer(l)
            if l % 2 == 0:
                self.hgrn(l)
            else:
                self.attn(l)
            self.ffn(l)
        self.P.emit(final_regions=self.R_out)
        return self.nc


def make_consts():
    c = np.zeros((128, 1680), np.float32)
    c[:, 0:128] = np.eye(128, dtype=np.float32)
    c[:, 128:256] = 1.0
    s = np.arange(128)[:, None]
    cc = np.arange(128)[None, :]
    c[:, 256:384] = ((s // 32 == cc // 32) & (s <= cc)).astype(np.float32)
    qi = np.arange(128)[:, None]
    kj = np.arange(128)[None, :]
    c[:, 384:512] = np.where(kj >= qi, 0.0, NEG)
    c[:, 512:640] = np.where(kj <= qi, 0.0, NEG)
    m = np.ones(1024, np.float32)
    m[::32] = 0.0
    c[:, 640:1664] = m[None, :]
    j = (np.arange(128) % 16).astype(np.float64)
    c[:, 1664] = (500000.0 ** (-(2.0 * j) / 32.0)).astype(np.float32)
    c[:, 1665] = EPS
    c[:, 1666] = 1.0
    c[:, 1667] = np.pi / 2
    for i4 in range(4):
        c[i4 * 32:(i4 + 1) * 32, 1668 + i4] = 1.0
    return c


_CACHE = {}


def kernel(x, c, positions, ada_w, ada_b, norm_g, hgrn_w_in, hgrn_lower_bounds, hgrn_norm_g,
           hgrn_w_out, attn_w_in, attn_w_out, ffn_w_in, ffn_w_out, _n_layers=4, _cores=8):
    if _n_layers not in _CACHE:
        _CACHE[_n_layers] = Builder(n_layers=_n_layers).build()
    nc = _CACHE[_n_layers]
    f = lambda a: np.ascontiguousarray(np.asarray(a, dtype=np.float32))
    shared = {
        "cst": make_consts(), "ada_w": f(ada_w), "ada_b": f(ada_b), "norm_g": f(norm_g),
        "hgrn_w_in": f(hgrn_w_in), "hgrn_lower_bounds": f(hgrn_lower_bounds), "hgrn_norm_g": f(hgrn_norm_g),
        "hgrn_w_out": f(hgrn_w_out), "attn_w_in": f(attn_w_in), "attn_w_out": f(attn_w_out),
        "ffn_w_in": f(ffn_w_in), "ffn_w_out": f(ffn_w_out),
    }
    x = np.asarray(x, dtype=np.float32)
    c = np.asarray(c, dtype=np.float32)
    positions = np.asarray(positions, dtype=np.int32)
    in_maps = []
    for b in range(_cores):
        m = dict(shared)
        m["x"] = np.ascontiguousarray(x[b])
        m["c"] = np.ascontiguousarray(c[b:b + 1])
        m["pos"] = np.ascontiguousarray(positions[b:b + 1])
        in_maps.append(m)
    res = run_bass_kernel_spmd(nc, in_maps, core_ids=list(range(_cores)))
    return np.stack([np.asarray(r["out"], dtype=np.float32) for r in res.results], axis=0)
```

```python
import numpy as np
import concourse.bass as bass
import concourse.mybir as mybir
from concourse.bass_utils import run_bass_kernel_spmd

F32 = mybir.dt.float32
BF16 = mybir.dt.bfloat16
I32 = mybir.dt.int32
AF = mybir.ActivationFunctionType
ALU = mybir.AluOpType
AX = mybir.AxisListType

PE, ACT, DVE, POOL, SP = "tensor", "scalar", "vector", "gpsimd", "sync"
ENGINES = [PE, ACT, DVE, POOL, SP]

D = 1024
T = 4096
DFF = 2816
NH = 8
NEG = -30000.0
EPS = 1e-6
TWO_PI = 6.283185307179586


class Region:
    __slots__ = ("name", "last_write", "reads", "dma_sem", "dma_count", "accum", "writers", "last_dma")

    def __init__(self, name, accum=False):
        self.name = name
        self.last_write = None
        self.reads = {}
        self.dma_sem = None
        self.dma_count = 0
        self.accum = accum
        self.writers = {}
        self.last_dma = None


class Instr:
    __slots__ = ("eng", "fn", "deps", "needed", "count", "is_dma", "dma_sem", "dma_val")

    def __init__(self, eng, fn, is_dma=False):
        self.eng = eng
        self.fn = fn
        self.deps = []
        self.needed = False
        self.count = None
        self.is_dma = is_dma
        self.dma_sem = None
        self.dma_val = None


class _Rec:
    def __init__(self):
        self.call = None

    def __getattr__(self, name):
        def f(*a, **k):
            self.call = (name, a, k)
        return f


class Prog:
    def __init__(self, nc):
        self.nc = nc
        self.streams = {e: [] for e in ENGINES}
        self.n_dma_sems = 0

    def _collect(self, ins, reads, writes, extra=()):
        deps = {}

        def add(i):
            if i is not None and i is not ins:
                deps[id(i)] = i
        for i in extra:
            add(i)
        for r in reads:
            if r.accum:
                for i in r.writers.values():
                    add(i)
            else:
                add(r.last_write)
        for w in writes:
            if not w.accum:
                add(w.last_write)
            for i in w.reads.values():
                add(i)
        ins.deps = list(deps.values())
        key = ("s", id(ins.dma_sem)) if ins.is_dma else ins.eng
        for r in reads:
            r.reads[key] = ins
        for w in writes:
            if w.accum:
                w.writers[key] = ins
            else:
                w.last_write = ins
                w.reads = {}

    def op(self, eng, fn, reads=(), writes=()):
        rec = _Rec()
        fn(rec)
        name, a, k = rec.call
        ins = Instr(eng, lambda e, name=name, a=a, k=k: getattr(e, name)(*a, **k))
        self._collect(ins, reads, writes)
        self.streams[eng].append(ins)
        return ins

    def dma(self, queue, out_ap, in_ap, reads=(), writes=(), sem_region=None, **kw):
        outs = out_ap if isinstance(out_ap, (list, tuple)) else [out_ap]
        ins_ = in_ap if isinstance(in_ap, (list, tuple)) else [in_ap]
        n = len(outs)

        def fn(eng, outs=outs, ins_=ins_, kw=kw):
            return [eng.dma_start(out=o, in_=i, **kw) for o, i in zip(outs, ins_)]
        ins = Instr(queue, fn, is_dma=True)
        sr = sem_region
        if sr is None:
            for w in writes:
                if not w.accum:
                    sr = w
                    break
        if sr is None:
            for r in reads:
                if not r.accum:
                    sr = r
                    break
        if sr.dma_sem is None:
            sr.dma_sem = self.nc.alloc_semaphore("d_" + sr.name)
            self.n_dma_sems += 1
        sr.dma_count += 16 * n
        ins.dma_sem = sr.dma_sem
        ins.dma_val = sr.dma_count
        extra = (sr.last_dma,) if sr.last_dma is not None else ()
        sr.last_dma = ins
        self._collect(ins, reads, writes, extra)
        self.streams[queue].append(ins)
        return ins

    def emit(self, final_regions=()):
        nc = self.nc
        for e in ENGINES:
            for ins in self.streams[e]:
                for d in ins.deps:
                    if d.is_dma:
                        continue
                    if d.eng == ins.eng and d.eng == PE and not ins.is_dma:
                        continue
                    d.needed = True
        sems = {e: nc.alloc_semaphore("s_" + e) for e in ENGINES}
        for e in ENGINES:
            c = 0
            for ins in self.streams[e]:
                if not ins.is_dma and ins.needed:
                    c += 1
                    ins.count = c
        streams = self.streams
        final_waits = []
        for r in final_regions:
            for i in r.writers.values():
                final_waits.append((i.dma_sem, i.dma_val))

        def body(e):
            def run(eng):
                waited = {}
                for ins in streams[e]:
                    need = {}
                    for d in ins.deps:
                        if d.is_dma:
                            s, v = d.dma_sem, d.dma_val
                        else:
                            if d.eng == e and e == PE and not ins.is_dma:
                                continue
                            s, v = sems[d.eng], d.count
                        key = id(s)
                        if waited.get(key, 0) >= v:
                            continue
                        if key not in need or need[key][1] < v:
                            need[key] = (s, v)
                    for key, (s, v) in need.items():
                        eng.wait_ge(s, v)
                        waited[key] = v
                    r = ins.fn(eng)
                    if ins.is_dma:
                        for rr_ in r:
                            rr_.then_inc(ins.dma_sem, 16)
                    elif ins.needed:
                        r.then_inc(sems[e], 1)
                if e == SP:
                    for (s, v) in final_waits:
                        eng.wait_ge(s, v)
            return run

        with nc.Block() as block:
            block.sync(body(SP))
            block.tensor(body(PE))
            block.scalar(body(ACT))
            block.vector(body(DVE))
            block.gpsimd(body(POOL))


class Buf:
    def __init__(self, B, off, dtype, shape):
        self.B = B
        self.off = off
        self.dtype = dtype
        esz = 2 if dtype == BF16 else 4
        n = 1
        for s in shape:
            n *= s
        self.size = n * esz
        assert off % 4 == 0
        w0 = off // 4
        w1 = (off + self.size + 3) // 4
        v = B.arena[:, w0:w1]
        if dtype != F32:
            v = v.bitcast(dtype)
        if len(shape) == 2:
            v = v.rearrange("p (a b) -> p a b", b=shape[1])
        elif len(shape) == 3:
            v = v.rearrange("p (a b c) -> p a b c", b=shape[1], c=shape[2])
        elif len(shape) == 4:
            v = v.rearrange("p (a b c d) -> p a b c d", b=shape[1], c=shape[2], d=shape[3])
        self.t = v
        self.R = self.rr(0, self.size)

    def rr(self, lo, hi):
        p0 = (self.off + lo) // 1024
        p1 = (self.off + hi - 1) // 1024
        return [self.B.pages[i] for i in range(p0, p1 + 1)]


KIB = 1024


class SB:
    def __init__(self, nc, name, free, dtype):
        self.tensor = nc.alloc_sbuf_tensor(name, [128, free], dtype)
        self.t = self.tensor[:, :]
        self.R = [Region(name)]

    def rr(self, lo, hi):
        return self.R


class Builder:
    def __init__(self, n_layers=4, dbg=False):
        self.n_layers = n_layers
        nc = bass.Bass("TRN2", target_bir_lowering=False)
        self.nc = nc
        self.P = Prog(nc)
        dt = nc.dram_tensor
        self.x = dt("x", [T, D], F32, kind="ExternalInput").ap()
        self.c = dt("c", [1, D], F32, kind="ExternalInput").ap()
        self.pos = dt("pos", [1, T], I32, kind="ExternalInput").ap()
        self.cst = dt("cst", [128, 1680], F32, kind="ExternalInput").ap()
        self.ada_w = dt("ada_w", [4, D, 6 * D], F32, kind="ExternalInput").ap()
        self.ada_b = dt("ada_b", [4, 6 * D], F32, kind="ExternalInput").ap()
        self.norm_g = dt("norm_g", [4, 4, D], F32, kind="ExternalInput").ap()
        self.hg_w_in = dt("hgrn_w_in", [2, D, 4 * D], F32, kind="ExternalInput").ap()
        self.hg_lb = dt("hgrn_lower_bounds", [2, D], F32, kind="ExternalInput").ap()
        self.hg_ng = dt("hgrn_norm_g", [2, D], F32, kind="ExternalInput").ap()
        self.hg_w_out = dt("hgrn_w_out", [2, D, D], F32, kind="ExternalInput").ap()
        self.at_w_in = dt("attn_w_in", [2, D, 9 * D], F32, kind="ExternalInput").ap()
        self.at_w_out = dt("attn_w_out", [2, D, D], F32, kind="ExternalInput").ap()
        self.ff_w_in = dt("ffn_w_in", [4, D, 2 * DFF], F32, kind="ExternalInput").ap()
        self.ff_w_out = dt("ffn_w_out", [4, DFF, D], F32, kind="ExternalInput").ap()
        self.out = dt("out", [T, D], F32, kind="ExternalOutput").ap()
        self.actT = dt("actT", [DFF, T], BF16).ap()
        self.oT = dt("oT", [D, T], BF16).ap()
        self.QT = dt("QT", [NH, 128, T], BF16).ap()
        self.KT = dt("KT", [NH, 128, T], BF16).ap()
        self.Vd = dt("Vd", [NH, 128, 32, 128], BF16).ap()
        self.Og = dt("Og", [3, T, NH, 136], F32).ap()
        self.rope = dt("rope", [3, 2, 128, T], F32).ap()
        self.R_out = [Region(f"out{i}", accum=True) for i in range(8)]
        self.R_x = Region("x", accum=True)
        self.R_actT = Region("actT", accum=True)
        self.R_oT = Region("oT", accum=True)
        self.R_QKV = Region("QKV", accum=True)
        self.R_Og = Region("Og", accum=True)
        self.R_rope = Region("rope", accum=True)
        self.R_in = Region("inputs", accum=True)
        self.ARENA_KIB = 190
        self.arena = nc.alloc_sbuf_tensor("arena", [128, self.ARENA_KIB * 256], F32)
        self.pages = [Region(f"pg{i}") for i in range(self.ARENA_KIB)]
        self.ps = [nc.alloc_psum_tensor(f"ps{i}", [128, 512], F32) for i in range(8)]
        self.PB = [Region(f"psb{i}") for i in range(8)]
        B = lambda off, dtype, shape: Buf(self, off, dtype, shape)
        self.mk = B
        self.cstb = B(0, F32, [1680])
        c = self.cstb.t
        self.ident = c[:, 0:128]
        self.ones = c[:, 128:256]
        self.maskT = c[:, 256:384]
        self.m2 = c[:, 384:640]
        self.scanmask = c[:, 640:1664]
        self.invf = c[:, 1664:1665]
        self.epsc = c[:, 1665:1666]
        self.one11 = c[0:1, 1666:1667]
        self.halfpi = c[:, 1667:1668]
        self.cmask = c[:, 1668:1672]
        self.zeroc = c[:, 1672:1673]
        self.identb = B(7 * KIB, BF16, [128])
        self.small = B(7 * KIB + 256, F32, [192])
        s = self.small.t
        self.condc = s[:, 0:8]
        self.modcols = s[:, 8:40]
        self.hgc = s[:, 40:104]
        self.Gb = [B(8 * KIB, F32, [1024]), B(12 * KIB, F32, [1024])]
        self.UT = 16 * KIB
        self.ZW = 80 * KIB
        self.ZX = 128 * KIB
        self.uT = B(self.UT, BF16, [8, T])
        self.wst = [B(self.ZW, F32, [8, 512]), B(self.ZW + 16 * KIB, F32, [8, 512])]
        self.wbf = [B(self.ZW + 32 * KIB, BF16, [8, 512]), B(self.ZW + 40 * KIB, BF16, [8, 512])]
        mc2 = SB(nc, "modcols2", 32, F32)
        self.modsets = [(self.modcols, self.small.R, self.Gb),
                        (mc2.t, mc2.R, [SB(nc, "gb2a", 1024, F32), SB(nc, "gb2b", 1024, F32)])]
        self.set_layer(0)
        self.maskb = nc.alloc_sbuf_tensor("maskb", [128, 256], BF16)
        self.R_maskb = [Region("maskb")]
        self.psi = 0
        self.bank_pool = list(range(8))
        self.wi = 0

    def set_layer(self, l):
        self.modcols, self.modR, self.Gb = self.modsets[l % 2]

    def bank(self):
        pool = self.bank_pool
        i = pool[self.psi % len(pool)]
        self.psi += 1
        return self.ps[i], self.PB[i]

    def init_consts(self):
        P = self.P
        P.dma(SP, self.cstb.t, self.cst, reads=[self.R_in], writes=self.cstb.R)
        P.op(POOL, lambda e: e.tensor_copy(out=self.identb.t, in_=self.ident),
             reads=self.cstb.R, writes=self.identb.R)
        P.op(POOL, lambda e: e.tensor_copy(out=self.maskb[:, :], in_=self.m2), reads=self.cstb.R, writes=self.R_maskb)

    def columnize(self, row_ap, row_R, out_ap, out_R, nch, func=None):
        P = self.P
        ps, pr = self.bank()
        for cch in range(nch):
            P.op(PE, lambda e, cch=cch, ps=ps: e.matmul(ps[:, cch:cch + 1], lhsT=row_ap[0:1, cch * 128:(cch + 1) * 128],
                                                        rhs=self.one11, start=True, stop=True),
                 reads=row_R + self.cstb.R, writes=[pr])
        if func is None:
            P.op(DVE, lambda e, ps=ps: e.tensor_copy(out=out_ap, in_=ps[:, 0:nch]), reads=[pr], writes=out_R)
        else:
            P.op(ACT, lambda e, ps=ps: e.activation(out=out_ap, in_=ps[:, 0:nch], func=func), reads=[pr], writes=out_R)

    def rstd_from_ss(self, ss_ap, R, n_div):
        P = self.P
        npart = ss_ap.shape[0]
        P.op(ACT, lambda e: e.activation(out=ss_ap, in_=ss_ap, func=AF.Ln, scale=1.0 / n_div, bias=self.epsc[0:npart, :]),
             reads=R + self.cstb.R, writes=R)
        P.op(ACT, lambda e: e.activation(out=ss_ap, in_=ss_ap, func=AF.Exp, scale=-0.5), reads=R, writes=R)

    def prologue(self):
        P = self.P
        Z = self.ZX
        rowa = self.mk(Z, F32, [1024])
        rowb = self.mk(Z + 4 * KIB, F32, [1024])
        P.dma(SP, rowa.t[0:1, :], self.c, reads=[self.R_in], writes=rowa.R)
        self.columnize(rowa.t, rowa.R, self.condc, self.small.R, 8, func=AF.Silu)
        hgc = self.hgc
        for idx in range(2):
            base = idx * 32
            C1 = hgc[:, base:base + 8]
            C2 = hgc[:, base + 8:base + 16]
            NC1 = hgc[:, base + 16:base + 24]
            HNG = hgc[:, base + 24:base + 32]
            if idx == 0:
                P.op(DVE, lambda e, C1=C1: e.memset(C1, 0.5), writes=self.small.R)
                P.op(DVE, lambda e, C2=C2: e.memset(C2, 0.5), writes=self.small.R)
                P.op(DVE, lambda e, NC1=NC1: e.memset(NC1, -0.5), writes=self.small.R)
            else:
                P.dma(SP, rowa.t[0:1, :], self.hg_lb[0:1, :], reads=[self.R_in], writes=rowa.R)
                P.dma(SP, rowb.t[0:1, :], self.hg_lb[1:2, :], reads=[self.R_in], writes=rowb.R)
                P.op(DVE, lambda e: e.tensor_tensor(out=rowa.t[0:1, :], in0=rowa.t[0:1, :], in1=rowb.t[0:1, :], op=ALU.subtract),
                     reads=rowa.R + rowb.R, writes=rowa.R)
                self.columnize(rowa.t, rowa.R, C1, self.small.R, 8, func=AF.Exp)
                P.op(DVE, lambda e, C1=C1: e.tensor_scalar(out=C1, in0=C1, scalar1=1.0, scalar2=None, op0=ALU.add),
                     reads=self.small.R, writes=self.small.R)
                P.op(DVE, lambda e, C1=C1, C2=C2: e.reciprocal(out=C2, in_=C1), reads=self.small.R, writes=self.small.R)
                P.op(DVE, lambda e, C1=C1, C2=C2: e.tensor_scalar(out=C1, in0=C2, scalar1=-0.5, scalar2=0.5, op0=ALU.mult, op1=ALU.add),
                     reads=self.small.R, writes=self.small.R)
                P.op(DVE, lambda e, C1=C1, NC1=NC1: e.tensor_scalar(out=NC1, in0=C1, scalar1=-1.0, scalar2=None, op0=ALU.mult),
                     reads=self.small.R, writes=self.small.R)
                P.op(DVE, lambda e, C2=C2: e.tensor_scalar(out=C2, in0=C2, scalar1=0.5, scalar2=0.5, op0=ALU.mult, op1=ALU.add),
                     reads=self.small.R, writes=self.small.R)
            P.dma(SP, rowb.t[0:1, :], self.hg_ng[idx:idx + 1, :], reads=[self.R_in], writes=rowb.R)
            self.columnize(rowb.t, rowb.R, HNG, self.small.R, 8)
        posi = self.mk(self.UT, I32, [T])
        ang = self.mk(self.UT + 16 * KIB, F32, [T])
        t1 = self.mk(self.UT + 32 * KIB, F32, [T])
        t2 = self.mk(self.UT + 48 * KIB, F32, [T])
        ti = self.mk(self.ZW, I32, [T])
        tab = [self.mk(self.ZW + 16 * KIB, F32, [T]), self.mk(self.ZW + 32 * KIB, F32, [T])]
        prm = self.mk(self.ZX + 8 * KIB, F32, [T])
        P.dma(SP, posi.t, self.pos[0:1, :].broadcast_to([128, T]), reads=[self.R_in], writes=posi.R)
        P.op(DVE, lambda e: e.tensor_copy(out=ang.t, in_=posi.t), reads=posi.R, writes=ang.R)
        P.op(DVE, lambda e: e.tensor_scalar(out=ang.t, in0=ang.t, scalar1=self.invf, scalar2=None, op0=ALU.mult),
             reads=ang.R + self.cstb.R, writes=ang.R)
        for which in range(2):
            src = ang
            if which == 0:
                P.op(DVE, lambda e: e.tensor_scalar(out=t2.t, in0=ang.t, scalar1=float(np.pi / 2), scalar2=None, op0=ALU.add),
                     reads=ang.R, writes=t2.R)
                src = t2
            P.op(DVE, lambda e, src=src: e.tensor_scalar(out=t1.t, in0=src.t, scalar1=float(1.0 / TWO_PI), scalar2=None, op0=ALU.mult),
                 reads=src.R, writes=t1.R)
            P.op(DVE, lambda e: e.tensor_copy(out=ti.t, in_=t1.t), reads=t1.R, writes=ti.R)
            P.op(DVE, lambda e: e.tensor_copy(out=t1.t, in_=ti.t), reads=ti.R, writes=t1.R)
            r = tab[which]
            P.op(DVE, lambda e, src=src, r=r: e.scalar_tensor_tensor(out=r.t, in0=t1.t, scalar=-TWO_PI, in1=src.t, op0=ALU.mult, op1=ALU.add),
                 reads=t1.R + src.R, writes=r.R)
            P.op(DVE, lambda e, r=r: e.tensor_scalar(out=t1.t, in0=r.t, scalar1=float(np.pi), scalar2=-TWO_PI, op0=ALU.is_gt, op1=ALU.mult),
                 reads=r.R, writes=t1.R)
            P.op(DVE, lambda e, r=r: e.tensor_tensor(out=r.t, in0=r.t, in1=t1.t, op=ALU.add), reads=r.R + t1.R, writes=r.R)
            P.op(DVE, lambda e, r=r: e.tensor_scalar(out=t1.t, in0=r.t, scalar1=float(-np.pi), scalar2=TWO_PI, op0=ALU.is_lt, op1=ALU.mult),
                 reads=r.R, writes=t1.R)
            P.op(DVE, lambda e, r=r: e.tensor_tensor(out=r.t, in0=r.t, in1=t1.t, op=ALU.add), reads=r.R + t1.R, writes=r.R)
            P.op(DVE, lambda e, r=r: e.tensor_scalar(out=r.t, in0=r.t, scalar1=3.14159, scalar2=-3.14159, op0=ALU.min, op1=ALU.max),
                 reads=r.R, writes=r.R)
            P.op(ACT, lambda e, r=r: e.activation(out=r.t, in_=r.t, func=AF.Sin), reads=r.R, writes=r.R)
        for g, d in enumerate((1, 4, 16)):
            for which in range(2):
                if d == 1:
                    src = tab[which]
                else:
                    P.op(POOL, lambda e, which=which, d=d: e.tensor_copy(
                        out=prm.t.rearrange("p (r m) -> p r m", r=d), in_=tab[which].t.rearrange("p (m r) -> p r m", r=d)),
                        reads=tab[which].R, writes=prm.R)
                    src = prm
                P.dma(SP, self.rope[g, which], src.t, reads=src.R, writes=[self.R_rope])

    def adaln(self, l):
        for _ in self.adaln_gen(l):
            pass

    def adaln_gen(self, l):
        P = self.P
        Z = self.ZW + 32 * KIB
        modcols, modR, Gbs = self.modsets[l % 2]
        brow = self.mk(Z, F32, [512])
        nrow = self.mk(Z + 2 * KIB, F32, [512])
        rowt = self.mk(Z + 4 * KIB, F32, [512])
        def load_ada(s_):
            if s_ < 12:
                P.dma(SP, self.wst[s_ % 2].t, self.ada_w[l, :, s_ * 512:(s_ + 1) * 512].rearrange("(k p) n -> p k n", p=128),
                      reads=[self.R_in], writes=self.wst[s_ % 2].R)
        load_ada(0)
        for s in range(12):
            v, half = s // 2, s % 2
            w = self.wst[s % 2]
            load_ada(s + 1)
            P.dma(SP, brow.t[0:1, :], self.ada_b[l:l + 1, s * 512:(s + 1) * 512], reads=[self.R_in], writes=brow.R)
            ps, pr = self.bank()
            for kc in range(8):
                P.op(PE, lambda e, kc=kc, ps=ps, w=w: e.matmul(ps[0:1, :], lhsT=self.condc[:, kc:kc + 1], rhs=w.t[:, kc, :],
                                                               start=(kc == 0), stop=(kc == 7)),
                     reads=self.small.R + w.R, writes=[pr])
            P.op(DVE, lambda e, ps=ps: e.tensor_tensor(out=rowt.t[0:1, :], in0=ps[0:1, :], in1=brow.t[0:1, :], op=ALU.add),
                 reads=[pr] + brow.R, writes=rowt.R)
            if v in (1, 2, 4, 5):
                gi = {1: 0, 2: 1, 4: 2, 5: 3}[v]
                P.dma(SP, nrow.t[0:1, :], self.norm_g[l, gi:gi + 1, half * 512:(half + 1) * 512], reads=[self.R_in], writes=nrow.R)
                P.op(DVE, lambda e: e.scalar_tensor_tensor(out=rowt.t[0:1, :], in0=rowt.t[0:1, :], scalar=1.0, in1=nrow.t[0:1, :],
                                                           op0=ALU.add, op1=ALU.mult),
                     reads=rowt.R + nrow.R, writes=rowt.R)
            if v in (0, 1, 3, 4):
                base = {1: 0, 0: 8, 4: 16, 3: 24}[v] + 4 * half
                self.columnize(rowt.t, rowt.R, modcols[:, base:base + 4], modR, 4)
            else:
                G = Gbs[0 if v == 2 else 1]
                ps2, pr2 = self.bank()
                P.op(PE, lambda e, ps2=ps2: e.matmul(ps2[:, :], lhsT=self.ones[0:1, :], rhs=rowt.t[0:1, :], start=True, stop=True),
                     reads=self.cstb.R + rowt.R, writes=[pr2])
                P.op(ACT, lambda e, ps2=ps2, G=G, half=half: e.copy(out=G.t[:, half * 512:(half + 1) * 512], in_=ps2[:, :]),
                     reads=[pr2], writes=G.rr(half * 2048, half * 2048 + 2048))
            yield s

    def tok_rows(self, src, d, pos0, n):
        L = T // d
        r, m = pos0 // L, pos0 % L
        t0 = m * d + r
        return bass.AP(tensor=src.tensor, offset=src.offset + t0 * D, ap=[[d * D, n], [1, D]])

    def norm_T(self, first, d, mset):
        P = self.P
        src = self.x if first else self.out
        Z = self.ZX
        hin = [self.mk(Z, F32, [4, 1024]), self.mk(Z + 16 * KIB, F32, [4, 1024])]
        xn = self.mk(Z + 32 * KIB, F32, [4, 1024])
        junk = self.mk(Z + 48 * KIB, BF16, [1024])
        ssb = self.mk(Z + 50 * KIB, F32, [8])
        Ac = self.modcols[:, mset * 16:mset * 16 + 8]
        Bc = self.modcols[:, mset * 16 + 8:mset * 16 + 16]
        srcR = [self.R_x] if first else (self.R_out if d > 1 else None)
        for tt in range(8):
            h = hin[tt % 2]
            rr = srcR if srcR is not None else [self.R_out[tt]]
            P.dma(SP, [h.t[:, j, :] for j in range(4)], [self.tok_rows(src, d, tt * 512 + j * 128, 128) for j in range(4)],
                  reads=rr, writes=h.R)
            ss = ssb.t[:, (tt % 2) * 4:(tt % 2) * 4 + 4]
            for j in range(4):
                P.op(ACT, lambda e, h=h, j=j, ss=ss: e.activation(out=junk.t, in_=h.t[:, j, :], func=AF.Square, accum_out=ss[:, j:j + 1]),
                     reads=h.rr(j * 4096, j * 4096 + 4096), writes=junk.R + ssb.R)
            self.rstd_from_ss(ss, ssb.R, D)
            for j in range(4):
                P.op(DVE, lambda e, h=h, j=j, ss=ss: e.tensor_scalar(out=xn.t[:, j, :], in0=h.t[:, j, :], scalar1=ss[:, j:j + 1], scalar2=None, op0=ALU.mult),
                     reads=h.rr(j * 4096, j * 4096 + 4096) + ssb.R, writes=xn.rr(j * 4096, j * 4096 + 4096))
            for cch in range(8):
                ps, pr = self.bank()
                for j in range(4):
                    P.op(PE, lambda e, ps=ps, j=j, cch=cch: e.transpose(out=ps[:, j * 128:(j + 1) * 128], in_=xn.t[:, j, cch * 128:(cch + 1) * 128], identity=self.ident),
                         reads=xn.rr(j * 4096 + cch * 512, j * 4096 + cch * 512 + 512) + self.cstb.R, writes=[pr])
                o = self.uT.t[:, cch, tt * 512:(tt + 1) * 512]
                oR = self.uT.rr((cch * T + tt * 512) * 2, (cch * T + tt * 512 + 512) * 2)
                if cch % 2 == 0:
                    P.op(ACT, lambda e, ps=ps, o=o, cch=cch: e.activation(out=o, in_=ps[:, :], func=AF.Identity, scale=Ac[:, cch:cch + 1], bias=Bc[:, cch:cch + 1]),
                         reads=[pr] + self.modR, writes=oR)
                else:
                    P.op(DVE, lambda e, ps=ps, o=o, cch=cch: e.tensor_scalar(out=o, in0=ps[:, :], scalar1=Ac[:, cch:cch + 1], scalar2=Bc[:, cch:cch + 1], op0=ALU.mult, op1=ALU.add),
                         reads=[pr] + self.modR, writes=oR)

    def load_w(self, dram_ap, ncols, cast_views=None):
        P = self.P
        i = self.wi % 2
        self.wi += 1
        st, wb = self.wst[i], self.wbf[i]
        P.dma(SP, st.t[:, :, 0:ncols], dram_ap.rearrange("(k p) n -> p k n", p=128), reads=[self.R_in], writes=st.R)
        P.op(POOL, lambda e: e.tensor_copy(out=wb.t[:, :, 0:ncols], in_=st.t[:, :, 0:ncols]), reads=st.R, writes=wb.R)
        return wb

    def resid_setup(self, base, hbase):
        self.r_h = [self.mk(hbase + i * 4 * KIB, F32, [1024]) for i in range(3)]
        self.r_t = [self.mk(base + i * 4 * KIB, F32, [1024]) for i in range(2)]
        self.r_ss = self.mk(base + 8 * KIB, F32, [8])
        self.r_junk = self.mk(base + 9 * KIB, BF16, [512])
        self.r_i = 0

    def resid_prefetch(self, first, st):
        if st >= 32:
            return
        h = self.r_h[st % 3]
        src = self.x if first else self.out
        Rsrc = [self.R_x] if first else [self.R_out[st // 4]]
        self.P.dma(SP, h.t, src[st * 128:(st + 1) * 128, :], reads=Rsrc, writes=h.R)

    def resid(self, first, st, banks, G):
        P = self.P
        i = self.r_i % 2
        self.r_i += 1
        h, t = self.r_h[st % 3], self.r_t[i]
        ss = self.r_ss.t[:, i * 4:i * 4 + 2]
        sst = self.r_ss.t[:, i * 4 + 2:i * 4 + 3]
        for n, (ps, pr) in enumerate(banks):
            P.op(ACT, lambda e, ps=ps, n=n, ss=ss: e.activation(out=self.r_junk.t, in_=ps[:, :], func=AF.Square, accum_out=ss[:, n:n + 1]),
                 reads=[pr], writes=self.r_junk.R + self.r_ss.R)
        P.op(DVE, lambda e, ss=ss, sst=sst: e.tensor_tensor(out=sst, in0=ss[:, 0:1], in1=ss[:, 1:2], op=ALU.add),
             reads=self.r_ss.R, writes=self.r_ss.R)
        self.rstd_from_ss(sst, self.r_ss.R, D)
        for n, (ps, pr) in enumerate(banks):
            P.op(DVE, lambda e, ps=ps, n=n, sst=sst, t=t: e.scalar_tensor_tensor(out=t.t[:, n * 512:(n + 1) * 512], in0=ps[:, :], scalar=sst, in1=G.t[:, n * 512:(n + 1) * 512], op0=ALU.mult, op1=ALU.mult),
                 reads=[pr] + self.r_ss.R + G.rr(n * 2048, n * 2048 + 2048), writes=t.rr(n * 2048, n * 2048 + 2048))
        P.op(POOL, lambda e, t=t, h=h: e.tensor_tensor(out=t.t, in0=t.t, in1=h.t, op=ALU.add), reads=t.R + h.R, writes=t.R)
        P.dma(SP, self.out[st * 128:(st + 1) * 128, :], t.t, reads=t.R, writes=[self.R_out[st // 4]])

    def ffn(self, l):
        P = self.P
        self.norm_T(False, 1, 1)
        Z = self.ZX
        sg = [self.mk(Z + i * 2 * KIB, F32, [512]) for i in range(2)]
        ao = [self.mk(Z + 4 * KIB + i * KIB, BF16, [512]) for i in range(4)]
        k = 0

        def load_slab(s):
            st, wb = self.wst[s % 2], self.wbf[s % 2]
            P.dma(SP, [st.t[:, :, 0:256], st.t[:, :, 256:512]],
                  [self.ff_w_in[l, :, s * 256:(s + 1) * 256].rearrange("(k p) n -> p k n", p=128),
                   self.ff_w_in[l, :, DFF + s * 256:DFF + (s + 1) * 256].rearrange("(k p) n -> p k n", p=128)],
                  reads=[self.R_in], writes=st.R)
            P.op(POOL, lambda e: e.tensor_copy(out=wb.t, in_=st.t), reads=st.R, writes=wb.R)
        load_slab(0)
        for s in range(11):
            wb = self.wbf[s % 2]
            if s + 1 < 11:
                load_slab(s + 1)
            for tt in range(8):
                for cc in range(2):
                    pg, rg = self.bank()
                    pu, ru = self.bank()
                    for which, ps, pr in ((0, pg, rg), (1, pu, ru)):
                        for kc in range(8):
                            P.op(PE, lambda e, ps=ps, kc=kc, wb=wb, which=which, cc=cc, tt=tt: e.matmul(
                                ps[:, :], lhsT=wb.t[:, kc, which * 256 + cc * 128:which * 256 + cc * 128 + 128],
                                rhs=self.uT.t[:, kc, tt * 512:(tt + 1) * 512], start=(kc == 0), stop=(kc == 7)),
                                reads=wb.R + self.uT.rr((kc * T + tt * 512) * 2, (kc * T + tt * 512 + 512) * 2), writes=[pr])
                    sgb = sg[k % 2]
                    aob = ao[k % 4]
                    k += 1
                    P.op(ACT, lambda e, pg=pg, sgb=sgb: e.activation(out=sgb.t, in_=pg[:, :], func=AF.Silu), reads=[rg], writes=sgb.R)
                    P.op(DVE, lambda e, pu=pu, sgb=sgb, aob=aob: e.tensor_tensor(out=aob.t, in0=pu[:, :], in1=sgb.t, op=ALU.mult),
                         reads=[ru] + sgb.R, writes=aob.R)
                    row0 = (s * 2 + cc) * 128
                    P.dma(SP, self.actT[row0:row0 + 128, tt * 512:(tt + 1) * 512], aob.t, reads=aob.R, writes=[self.R_actT])
        wo = self.mk(self.UT, BF16, [22, 1024])
        for kc in range(22):
            i = self.wi % 2
            self.wi += 1
            st = self.wst[i]
            stv = st.t.rearrange("p a b -> p (a b)")[:, 0:1024]
            P.dma(SP, stv, self.ff_w_out[l, kc * 128:(kc + 1) * 128, :], reads=[self.R_in], writes=st.rr(0, 4096))
            P.op(POOL, lambda e, stv=stv, kc=kc: e.tensor_copy(out=wo.t[:, kc, :], in_=stv), reads=st.rr(0, 4096),
                 writes=wo.rr(kc * 2048, kc * 2048 + 2048))
        at = [self.mk(self.ZX + 18 * KIB, BF16, [22, 512]), self.mk(self.ZX + 40 * KIB, BF16, [22, 512])]
        self.resid_setup(self.ZX, self.UT + 48 * KIB)
        gen = self.adaln_gen(l + 1) if l + 1 < self.n_layers else None

        def load_act(tt_):
            if tt_ < 8:
                P.dma(SP, at[tt_ % 2].t, self.actT[:, tt_ * 512:(tt_ + 1) * 512].rearrange("(k p) t -> p k t", p=128), reads=[self.R_actT], writes=at[tt_ % 2].R)
        load_act(0)
        self.resid_prefetch(False, 0)
        for st_ in range(32):
            if gen is not None and st_ >= 2 and st_ % 2 == 0:
                next(gen, None)
            a = at[(st_ // 4) % 2]
            j_ = st_ % 4
            if j_ == 0:
                load_act(st_ // 4 + 1)
            self.resid_prefetch(False, st_ + 1)
            banks = []
            for n in range(2):
                ps, pr = self.bank()
                for kc in range(22):
                    P.op(PE, lambda e, ps=ps, kc=kc, a=a, n=n, j_=j_: e.matmul(ps[:, :], lhsT=a.t[:, kc, j_ * 128:(j_ + 1) * 128], rhs=wo.t[:, kc, n * 512:(n + 1) * 512],
                                                                        start=(kc == 0), stop=(kc == 21)),
                         reads=a.R + wo.rr(kc * 2048 + n * 1024, kc * 2048 + n * 1024 + 1024), writes=[pr])
                banks.append((ps, pr))
            self.resid(False, st_, banks, self.Gb[1])
        if gen is not None:
            for _ in gen:
                pass

    def out_proj(self, w_dram, first):
        P = self.P
        wo = self.mk(self.ZW + 32 * KIB, BF16, [8, 1024])
        for kc in range(8):
            st = self.wst[kc % 2]
            stv = st.t.rearrange("p a b -> p (a b)")[:, 0:1024]
            P.dma(SP, stv, w_dram[kc * 128:(kc + 1) * 128, :], reads=[self.R_in], writes=st.rr(0, 4096))
            P.op(POOL, lambda e, stv=stv, kc=kc: e.tensor_copy(out=wo.t[:, kc, :], in_=stv), reads=st.rr(0, 4096),
                 writes=wo.rr(kc * 2048, kc * 2048 + 2048))
        at = [self.mk(self.UT + i * 8 * KIB, BF16, [8, 512]) for i in range(2)]
        self.resid_setup(self.ZX, self.UT + 48 * KIB)

        def load_o(tt):
            if tt < 8:
                P.dma(SP, at[tt % 2].t, self.oT[:, tt * 512:(tt + 1) * 512].rearrange("(k p) t -> p k t", p=128), reads=[self.R_oT], writes=at[tt % 2].R)
        load_o(0)
        self.resid_prefetch(first, 0)
        for tt in range(8):
            a = at[tt % 2]
            load_o(tt + 1)
            for j in range(4):
                self.resid_prefetch(first, tt * 4 + j + 1)
                banks = []
                for n in range(2):
                    ps, pr = self.bank()
                    for kc in range(8):
                        P.op(PE, lambda e, ps=ps, kc=kc, a=a, n=n, j=j: e.matmul(ps[:, :], lhsT=a.t[:, kc, j * 128:(j + 1) * 128],
                                                                                 rhs=wo.t[:, kc, n * 512:(n + 1) * 512], start=(kc == 0), stop=(kc == 7)),
                             reads=a.R + wo.rr(kc * 2048 + n * 1024, kc * 2048 + n * 1024 + 1024), writes=[pr])
                    banks.append((ps, pr))
                self.resid(first, tt * 4 + j, banks, self.Gb[0])

    def hgrn(self, l):
        P = self.P
        idx = l // 2
        first = (l == 0)
        self.norm_T(first, 1, 0)
        hb = idx * 32
        C1 = self.hgc[:, hb:hb + 8]
        C2 = self.hgc[:, hb + 8:hb + 16]
        NC1 = self.hgc[:, hb + 16:hb + 24]
        HNG = self.hgc[:, hb + 24:hb + 32]
        Z = self.ZX
        N = 1024
        th = self.mk(Z, F32, [N])
        qs = self.mk(Z + 4 * KIB, F32, [N])
        kk = self.mk(Z + 8 * KIB, F32, [N])
        bb = self.mk(Z + 12 * KIB, F32, [N])
        sets = []
        for i in range(2):
            b0 = Z + 16 * KIB + i * 13 * KIB
            sets.append(dict(qd=self.mk(b0, BF16, [N]), ki=self.mk(b0 + 2 * KIB, BF16, [N]), ke=self.mk(b0 + 4 * KIB, BF16, [N]),
                             vtm=self.mk(b0 + 6 * KIB, BF16, [8, 128]), gs=self.mk(b0 + 8 * KIB, F32, [N]), dec=self.mk(b0 + 12 * KIB, F32, [32])))
        ketm = self.mk(Z + 42 * KIB, BF16, [8, 128])
        vexp = self.mk(Z + 44 * KIB, BF16, [8, 4, 128])
        Sbf = self.mk(Z + 52 * KIB, BF16, [32, 128])
        sqo = self.mk(Z + 60 * KIB, F32, [512])
        t1 = sqo
        S32 = self.mk(self.ZW + 24 * KIB, F32, [33, 128])
        amt = self.mk(self.ZW + 41 * KIB, BF16, [4, 128])
        oo = [self.mk(self.ZW + 42 * KIB + i * KIB, BF16, [512]) for i in range(2)]
        rso = self.mk(self.ZW + 44 * KIB, F32, [512])
        stg = self.wst[0]
        wb = self.mk(self.ZW + 16 * KIB, BF16, [8, 4, 128])
        w_in = self.hg_w_in[idx]
        okc = [0]
        tbanks = {}

        def S1a(ui):
            h, qd_ = ui // 4, ui % 4
            B = sets[ui % 2]
            vtm, gs = B["vtm"], B["gs"]
            tok0 = qd_ * N
            if qd_ == 0:
                stv = stg.t.rearrange("p k (a b) -> p k a b", a=4)
                P.dma(SP, [stv[:, :, a, :] for a in range(4)],
                      [w_in[:, a * 1024 + h * 128:a * 1024 + (h + 1) * 128].rearrange("(k p) n -> p k n", p=128) for a in range(4)],
                      reads=[self.R_in], writes=stg.R)
                P.op(POOL, lambda e: e.tensor_copy(out=wb.t, in_=stv), reads=stg.R, writes=wb.R)
            for which, dst, func, scale in ((1, th, AF.Tanh, 0.5), (0, qs, AF.Silu, 1.0), (3, gs, AF.Silu, 1.0)):
                for t2 in range(2):
                    ps, pr = self.bank()
                    for kc in range(8):
                        P.op(PE, lambda e: e.matmul(ps[:, :], lhsT=wb.t[:, kc, which, :], rhs=self.uT.t[:, kc, tok0 + t2 * 512:tok0 + (t2 + 1) * 512],
                                                    start=(kc == 0), stop=(kc == 7)),
                             reads=wb.R + self.uT.rr((kc * T + tok0 + t2 * 512) * 2, (kc * T + tok0 + t2 * 512 + 512) * 2), writes=[pr])
                    P.op(ACT, lambda e: e.activation(out=dst.t[:, t2 * 512:(t2 + 1) * 512], in_=ps[:, :], func=func, scale=scale),
                         reads=[pr], writes=dst.rr(t2 * 2048, t2 * 2048 + 2048))
                    yield
            for t2 in range(2):
                ps, pr = self.bank()
                for j in range(4):
                    for kc in range(8):
                        p0 = tok0 + t2 * 512 + j * 128
                        P.op(PE, lambda e: e.matmul(ps[:, j * 128:(j + 1) * 128], lhsT=self.uT.t[:, kc, p0:p0 + 128], rhs=wb.t[:, kc, 2, :],
                                                    start=(kc == 0), stop=(kc == 7)),
                             reads=wb.R + self.uT.rr((kc * T + p0) * 2, (kc * T + p0 + 128) * 2), writes=[pr])
                    if j % 2 == 1:
                        yield
                P.op(ACT, lambda e: e.copy(out=vtm.t[:, t2 * 4:(t2 + 1) * 4, :].rearrange("p a b -> p (a b)"), in_=ps[:, :]),
                     reads=[pr], writes=vtm.rr(t2 * 1024, t2 * 1024 + 1024))

        def S1b(ui):
            h, qd_ = ui // 4, ui % 4
            B = sets[ui % 2]
            qd, ki, ke, dec = B["qd"], B["ki"], B["ke"], B["dec"]
            c1, c2, nc1 = C1[:, h:h + 1], C2[:, h:h + 1], NC1[:, h:h + 1]
            P.op(DVE, lambda e: e.tensor_scalar(out=kk.t, in0=th.t, scalar1=nc1, scalar2=c1, op0=ALU.mult, op1=ALU.add),
                 reads=th.R + self.small.R, writes=kk.R)
            yield
            P.op(DVE, lambda e: e.tensor_scalar(out=th.t, in0=th.t, scalar1=c1, scalar2=c2, op0=ALU.mult, op1=ALU.add),
                 reads=th.R + self.small.R, writes=th.R)
            P.op(ACT, lambda e: e.activation(out=th.t, in_=th.t, func=AF.Ln), reads=th.R, writes=th.R)
            yield
            P.op(DVE, lambda e: e.tensor_tensor_scan(out=bb.t, data0=self.scanmask, data1=th.t, initial=0.0, op0=ALU.mult, op1=ALU.add),
                 reads=th.R + self.cstb.R, writes=bb.R)
            P.op(ACT, lambda e: e.activation(out=th.t, in_=bb.t, func=AF.Exp), reads=bb.R, writes=th.R)
            b3 = bb.t.rearrange("p (n c) -> p n c", c=32)
            blast = b3[:, :, 31:32]
            P.op(ACT, lambda e: e.activation(out=dec.t, in_=blast.rearrange("p n c -> p (n c)"), func=AF.Exp), reads=bb.R, writes=dec.R)
            yield
            P.op(DVE, lambda e: e.tensor_tensor(out=qd.t, in0=qs.t, in1=th.t, op=ALU.mult), reads=qs.R + th.R, writes=qd.R)
            P.op(ACT, lambda e: e.activation(out=th.t, in_=bb.t, func=AF.Exp, scale=-1.0), reads=bb.R, writes=th.R)
            yield
            P.op(DVE, lambda e: e.tensor_tensor(out=ki.t, in0=kk.t, in1=th.t, op=ALU.mult), reads=kk.R + th.R, writes=ki.R)
            yield
            P.op(DVE, lambda e: e.tensor_tensor(out=th.t.rearrange("p (n c) -> p n c", c=32), in0=blast.broadcast_to([128, 32, 32]),
                                                in1=b3, op=ALU.subtract), reads=bb.R, writes=th.R)
            P.op(ACT, lambda e: e.activation(out=th.t, in_=th.t, func=AF.Exp), reads=th.R, writes=th.R)
            yield
            P.op(DVE, lambda e: e.tensor_tensor(out=ke.t, in0=kk.t, in1=th.t, op=ALU.mult), reads=kk.R + th.R, writes=ke.R)
            yield

        def S2T(ui):
            ke = sets[ui % 2]["ke"]
            tbanks[ui] = []
            for t2 in range(2):
                ps, pr = self.ps[6 + t2], self.PB[6 + t2]
                pb = ps[:, :].bitcast(BF16)
                for j in range(4):
                    blk = t2 * 4 + j
                    P.op(PE, lambda e: e.transpose(out=pb[:, j * 128:(j + 1) * 128], in_=ke.t[:, blk * 128:(blk + 1) * 128], identity=self.identb.t),
                         reads=ke.R + self.identb.R, writes=[pr])
                tbanks[ui].append((pb, pr))

        def S2E(ui):
            vtm = sets[ui % 2]["vtm"]
            for t2 in range(2):
                pb, pr = tbanks[ui][t2]
                P.op(DVE, lambda e: e.tensor_copy(out=ketm.t[:, t2 * 4:(t2 + 1) * 4, :].rearrange("p a b -> p (a b)"), in_=pb[:, 0:512]),
                     reads=[pr], writes=ketm.rr(t2 * 1024, t2 * 1024 + 1024))

        def S2Ev(ui):
            vtm = sets[ui % 2]["vtm"]
            for i4 in range(4):
                P.op(POOL, lambda e: e.tensor_scalar(out=vexp.t[:, :, i4, :], in0=vtm.t, scalar1=self.cmask[:, i4:i4 + 1], scalar2=1.0,
                                                     op0=ALU.mult, op1=ALU.mult),
                     reads=vtm.R + self.cstb.R, writes=vexp.R)

        def S2K(ui):
            qd_ = ui % 4
            dec = sets[ui % 2]["dec"]
            if qd_ == 0:
                P.op(DVE, lambda e: e.memset(S32.t[:, 0, :], 0.0), writes=S32.rr(0, 512))
            else:
                P.op(DVE, lambda e: e.tensor_copy(out=S32.t[:, 0, :], in_=S32.t[:, 32, :]), reads=S32.rr(32 * 512, 33 * 512), writes=S32.rr(0, 512))
            for blk in range(8):
                ps, pr = self.bank()
                P.op(PE, lambda e: e.matmul(ps[:, :], lhsT=ketm.t[:, blk, :], rhs=vexp.t[:, blk, :, :].rearrange("p a b -> p (a b)"), start=True, stop=True),
                     reads=ketm.R + vexp.R, writes=[pr])
                for i4 in range(4):
                    j = blk * 4 + i4
                    P.op(DVE, lambda e: e.scalar_tensor_tensor(
                        out=S32.t[:, j + 1, :], in0=S32.t[:, j, :], scalar=dec.t[:, j:j + 1], in1=ps[:, i4 * 128:(i4 + 1) * 128], op0=ALU.mult, op1=ALU.add),
                        reads=S32.rr(j * 512, j * 512 + 512) + dec.R + [pr], writes=S32.rr((j + 1) * 512, (j + 1) * 512 + 512))
                yield
            P.op(ACT, lambda e: e.copy(out=Sbf.t, in_=S32.t[:, 0:32, :]), reads=S32.R, writes=Sbf.R)

        def S2R(ui):
            h, qd_ = ui // 4, ui % 4
            B = sets[ui % 2]
            qd, ki, vtm, gs = B["qd"], B["ki"], B["vtm"], B["gs"]
            tok0 = qd_ * N
            for t2 in range(2):
                pa, ra = self.bank()
                for j in range(4):
                    blk = t2 * 4 + j
                    P.op(PE, lambda e: e.matmul(pa[:, j * 128:(j + 1) * 128], lhsT=ki.t[:, blk * 128:(blk + 1) * 128],
                                                rhs=qd.t[:, blk * 128:(blk + 1) * 128], start=True, stop=True),
                         reads=ki.R + qd.R, writes=[ra])
                P.op(DVE, lambda e: e.tensor_tensor(out=amt.t, in0=pa[:, :].rearrange("p (a b) -> p a b", a=4),
                                                    in1=self.maskT.unsqueeze(1).broadcast_to([128, 4, 128]), op=ALU.mult),
                     reads=[ra] + self.cstb.R, writes=amt.R)
                po, ro = self.bank()
                for j in range(4):
                    blk = t2 * 4 + j
                    P.op(PE, lambda e: e.matmul(po[:, j * 128:(j + 1) * 128], lhsT=vtm.t[:, blk, :], rhs=amt.t[:, j, :], start=True, stop=False),
                         reads=vtm.R + amt.R, writes=[ro])
                    for i4 in range(4):
                        ch = blk * 4 + i4
                        P.op(PE, lambda e: e.matmul(po[:, j * 128 + i4 * 32:j * 128 + (i4 + 1) * 32], lhsT=Sbf.t[:, ch, :], rhs=qd.t[:, ch * 32:(ch + 1) * 32],
                                                    start=False, stop=(i4 == 3)),
                             reads=Sbf.R + qd.R, writes=[ro])
                P.op(ACT, lambda e: e.activation(out=sqo.t, in_=po[:, :], func=AF.Square), reads=[ro], writes=sqo.R)
                pn, rn = self.bank()
                P.op(PE, lambda e: e.matmul(pn[:, :], lhsT=self.ones, rhs=sqo.t, start=True, stop=True), reads=self.cstb.R + sqo.R, writes=[rn])
                P.op(ACT, lambda e: e.activation(out=rso.t, in_=pn[:, :], func=AF.Ln, scale=1.0 / 128, bias=self.epsc), reads=[rn] + self.cstb.R, writes=rso.R)
                P.op(ACT, lambda e: e.activation(out=rso.t, in_=rso.t, func=AF.Exp, scale=-0.5), reads=rso.R, writes=rso.R)
                P.op(DVE, lambda e: e.tensor_tensor(out=t1.t, in0=po[:, :], in1=rso.t, op=ALU.mult), reads=[ro] + rso.R, writes=t1.R)
                o = oo[okc[0] % 2]
                okc[0] += 1
                P.op(DVE, lambda e: e.scalar_tensor_tensor(out=o.t, in0=t1.t, scalar=HNG[:, h:h + 1], in1=gs.t[:, t2 * 512:(t2 + 1) * 512],
                                                           op0=ALU.mult, op1=ALU.mult),
                     reads=t1.R + self.small.R + gs.rr(t2 * 2048, t2 * 2048 + 2048), writes=o.R)
                P.dma(SP, self.oT[h * 128:(h + 1) * 128, tok0 + t2 * 512:tok0 + (t2 + 1) * 512], o.t, reads=o.R, writes=[self.R_oT])
                yield

        def zipgen(g1, g2):
            d1 = d2 = False
            while not (d1 and d2):
                if not d1:
                    try:
                        next(g1)
                    except StopIteration:
                        d1 = True
                if not d2:
                    try:
                        next(g2)
                    except StopIteration:
                        d2 = True

        NU = NH * 4
        self.bank_pool = list(range(6))
        for _ in S1a(0):
            pass
        for _ in S1b(0):
            pass
        S2Ev(0)
        for ui in range(NU):
            nxt = ui + 1 < NU
            S2T(ui)
            S2E(ui)
            zipgen(S2K(ui), S1a(ui + 1) if nxt else iter(()))
            if nxt:
                S2Ev(ui + 1)
            zipgen(S2R(ui), S1b(ui + 1) if nxt else iter(()))
        self.bank_pool = list(range(8))
        self.out_proj(self.hg_w_out[idx], first)

    def attn(self, l):
        P = self.P
        idx = l // 2
        w_in = self.at_w_in[idx]
        scale = 128.0 ** -0.5
        Z = self.ZX
        for g, d in enumerate((1, 4, 16)):
            self.norm_T(False, d, 0)
            cs = [[self.mk(Z + (i * 2 + w) * 2 * KIB, F32, [512]) for w in range(2)] for i in range(2)]
            rt = [self.mk(Z + 8 * KIB + i * 2 * KIB, F32, [512]) for i in range(4)]
            ob = [self.mk(Z + 16 * KIB + i * KIB, BF16, [512]) for i in range(4)]
            wqk = self.mk(self.ZW + 32 * KIB, BF16, [8, 8, 128])
            oi = 0
            for qk in range(2):
                dst = self.QT if qk == 0 else self.KT
                col0 = g * 3072 + qk * 1024
                for half in range(2):
                    st = self.wst[half]
                    P.dma(SP, st.t, w_in[:, col0 + half * 512:col0 + (half + 1) * 512].rearrange("(k p) n -> p k n", p=128),
                          reads=[self.R_in], writes=st.R)
                    for kc in range(8):
                        src = st.t[:, kc, :].rearrange("p (h c j) -> p c h j", h=4, c=8)
                        dv = wqk.t[:, kc, :, :].rearrange("p c (h j) -> p c h j", h=8)[:, :, half * 4:(half + 1) * 4, :]
                        P.op(POOL, lambda e, src=src, dv=dv: e.tensor_copy(out=dv, in_=src), reads=st.rr(kc * 2048, kc * 2048 + 2048),
                             writes=wqk.rr(kc * 2048, kc * 2048 + 2048))
                for tt in range(8):
                    cst_ = cs[tt % 2]
                    for w in range(2):
                        P.dma(SP, cst_[w].t, self.rope[g, w, :, tt * 512:(tt + 1) * 512], reads=[self.R_rope], writes=cst_[w].R)
                    pss = []
                    for cch in range(8):
                        ps, pr = self.bank()
                        for kc in range(8):
                            P.op(PE, lambda e, ps=ps, kc=kc, cch=cch, tt=tt: e.matmul(ps[:, :], lhsT=wqk.t[:, kc, cch, :], rhs=self.uT.t[:, kc, tt * 512:(tt + 1) * 512],
                                                                                      start=(kc == 0), stop=(kc == 7)),
                                 reads=wqk.rr(kc * 2048, kc * 2048 + 2048) + self.uT.rr((kc * T + tt * 512) * 2, (kc * T + tt * 512 + 512) * 2), writes=[pr])
                        pss.append((ps, pr))
                        if cch == 1:
                            (pa, ra), (pb_, rb) = pss[0], pss[1]
                            cosb, sinb = cst_[0], cst_[1]
                            qsc = scale if qk == 0 else 1.0
                            for ri, (psx, rx, tb) in enumerate(((pa, ra, cosb), (pb_, rb, sinb), (pa, ra, sinb), (pb_, rb, cosb))):
                                P.op(DVE, lambda e: e.scalar_tensor_tensor(out=rt[ri].t, in0=psx[:, :], scalar=qsc, in1=tb.t, op0=ALU.mult, op1=ALU.mult),
                                     reads=[rx] + tb.R, writes=rt[ri].R)
                            o0, o1 = ob[oi % 4], ob[(oi + 1) % 4]
                            oi += 2
                            P.op(POOL, lambda e, o0=o0: e.tensor_tensor(out=o0.t, in0=rt[0].t, in1=rt[1].t, op=ALU.subtract), reads=rt[0].R + rt[1].R, writes=o0.R)
                            P.op(POOL, lambda e, o1=o1: e.tensor_tensor(out=o1.t, in0=rt[3].t, in1=rt[2].t, op=ALU.add), reads=rt[2].R + rt[3].R, writes=o1.R)
                            outs = [(0, o0), (1, o1)]
                        elif cch >= 2:
                            o0 = ob[oi % 4]
                            oi += 1
                            P.op(ACT, lambda e: e.activation(out=o0.t, in_=ps[:, :], func=AF.Copy, scale=(scale if qk == 0 else 1.0)), reads=[pr], writes=o0.R)
                            outs = [(cch, o0)]
                        else:
                            outs = []
                        for (cc, o) in outs:
                            P.dma(SP, dst[cc, :, tt * 512:(tt + 1) * 512], o.t, reads=o.R, writes=[self.R_QKV])
            vo = [self.mk(Z + 20 * KIB + i * KIB, BF16, [512]) for i in range(2)]
            vi = 0
            for n in range(2):
                wb = self.load_w(w_in[:, g * 3072 + 2048 + n * 512:g * 3072 + 2048 + (n + 1) * 512], 512)
                for st_ in range(32):
                    ps, pr = self.bank()
                    for kc in range(8):
                        P.op(PE, lambda e, ps=ps, kc=kc, st_=st_, wb=wb: e.matmul(ps[:, :], lhsT=self.uT.t[:, kc, st_ * 128:(st_ + 1) * 128], rhs=wb.t[:, kc, :],
                                                                                  start=(kc == 0), stop=(kc == 7)),
                             reads=wb.R + self.uT.rr((kc * T + st_ * 128) * 2, (kc * T + st_ * 128 + 128) * 2), writes=[pr])
                    v = vo[vi % 2]
                    vi += 1
                    P.op(ACT, lambda e, ps=ps, v=v: e.copy(out=v.t, in_=ps[:, :]), reads=[pr], writes=v.R)
                    dap = bass.AP(tensor=self.Vd.tensor, offset=self.Vd.offset + (n * 4) * T * 128 + st_ * 128, ap=[[32 * 128, 128], [T * 128, 4], [1, 128]])
                    P.dma(SP, dap, v.t.rearrange("p (a b) -> p a b", a=4), reads=v.R, writes=[self.R_QKV])
            nb = 32 // d
            qkv = [[self.mk(self.UT + (i * 3 + w) * 8 * KIB, BF16, [T]) for w in range(3)] for i in range(2)]
            NB3 = 6
            pp = [self.mk(Z + i * KIB, BF16, [256]) for i in range(NB3)]
            pT = [self.mk(Z + 6 * KIB + i * KIB, BF16, [2, 128]) for i in range(NB3)]
            Ot = [self.mk(Z + 12 * KIB + i * KIB, F32, [136]) for i in range(NB3)]
            stt_ = [self.mk(Z + 18 * KIB + i * KIB, F32, [8]) for i in range(NB3)]
            def load_head(h_, part):
                if h_ >= NH:
                    return
                qb_, kb_, vb_ = qkv[h_ % 2]
                if part == 0:
                    P.dma(SP, [qb_.t[cc * 16:(cc + 1) * 16, :] for cc in range(8)], [self.QT[cc, h_ * 16:(h_ + 1) * 16, :] for cc in range(8)],
                          reads=[self.R_QKV], writes=qb_.R)
                elif part == 1:
                    P.dma(SP, [kb_.t[cc * 16:(cc + 1) * 16, :] for cc in range(8)], [self.KT[cc, h_ * 16:(h_ + 1) * 16, :] for cc in range(8)],
                          reads=[self.R_QKV], writes=kb_.R)
                else:
                    P.dma(SP, vb_.t, self.Vd[h_].rearrange("p b v -> p (b v)"), reads=[self.R_QKV], writes=vb_.R)
            for part in range(3):
                load_head(0, part)
            for h in range(NH):
                qb, kb, vb = qkv[h % 2]
                vv = vb.t.rearrange("p (b v) -> p b v", v=128)
                stA = {}

                def stage_A(u):
                    hasprev = (u % nb) != 0
                    ps, pr = self.bank()
                    k0 = (u - 1) * 128 if hasprev else u * 128
                    nk = 256 if hasprev else 128
                    mview = self.maskb[:, 0:256] if hasprev else self.maskb[:, 128:256]
                    P.op(PE, lambda e: e.matmul(ps[:, 0:nk], lhsT=qb.t[:, u * 128:(u + 1) * 128], rhs=kb.t[:, k0:k0 + nk], start=True, stop=False),
                         reads=qb.R + kb.R, writes=[pr])
                    P.op(PE, lambda e: e.matmul(ps[:, 0:nk], lhsT=self.identb.t, rhs=mview, start=False, stop=True),
                         reads=self.identb.R + self.R_maskb, writes=[pr])
                    i = u % NB3
                    p_, st = pp[i], stt_[i]
                    P.op(DVE, lambda e: e.tensor_reduce(out=st.t[:, 1:2], in_=ps[:, 0:nk], axis=AX.X, op=ALU.max, negate=True), reads=[pr], writes=st.R)
                    P.op(ACT, lambda e: e.activation(out=p_.t[:, 0:nk], in_=ps[:, 0:nk], func=AF.Exp, bias=st.t[:, 1:2], accum_out=st.t[:, 2:3]),
                         reads=[pr] + st.R, writes=p_.R + st.R)
                    stA[u] = (hasprev, nk)

                def stage_B(u):
                    hasprev, nk = stA[u]
                    i = u % NB3
                    p_, pt = pp[i], pT[i]
                    ps, pr = self.bank()
                    pb = ps[:, :].bitcast(BF16)
                    nparts = 2 if hasprev else 1
                    for a in range(nparts):
                        P.op(PE, lambda e: e.transpose(out=pb[:, a * 128:(a + 1) * 128], in_=p_.t[:, a * 128:(a + 1) * 128], identity=self.identb.t),
                             reads=p_.R + self.identb.R, writes=[pr])
                    P.op(DVE, lambda e: e.tensor_copy(out=pt.t[:, 0:nparts, :].rearrange("p a b -> p (a b)"), in_=pb[:, 0:nparts * 128]),
                         reads=[pr], writes=pt.R)

                def stage_C(u):
                    hasprev, nk = stA[u]
                    i = u % NB3
                    pt, O, st = pT[i], Ot[i], stt_[i]
                    ps, pr = self.bank()
                    if hasprev:
                        P.op(PE, lambda e: e.matmul(ps[:, 0:128], lhsT=pt.t[:, 0, :], rhs=vv[:, u - 1, :], start=True, stop=False), reads=pt.R + vb.R, writes=[pr])
                        P.op(PE, lambda e: e.matmul(ps[:, 0:128], lhsT=pt.t[:, 1, :], rhs=vv[:, u, :], start=False, stop=True), reads=pt.R + vb.R, writes=[pr])
                    else:
                        P.op(PE, lambda e: e.matmul(ps[:, 0:128], lhsT=pt.t[:, 0, :], rhs=vv[:, u, :], start=True, stop=True), reads=pt.R + vb.R, writes=[pr])
                    P.op(DVE, lambda e: e.reciprocal(out=st.t[:, 3:4], in_=st.t[:, 2:3]), reads=st.R, writes=st.R)
                    P.op(DVE, lambda e: e.tensor_scalar(out=O.t[:, 0:128], in0=ps[:, 0:128], scalar1=st.t[:, 3:4], scalar2=None, op0=ALU.mult),
                         reads=[pr] + st.R, writes=O.R)
                    P.op(ACT, lambda e: e.activation(out=st.t[:, 4:5], in_=st.t[:, 2:3], func=AF.Ln), reads=st.R, writes=st.R)
                    P.op(POOL, lambda e: e.tensor_tensor(out=O.t[:, 128:129], in0=st.t[:, 4:5], in1=st.t[:, 1:2], op=ALU.subtract), reads=st.R, writes=O.R)
                    r, n = u // nb, u % nb
                    t0 = n * 128 * d + r
                    dap = bass.AP(tensor=self.Og.tensor, offset=self.Og.offset + g * T * NH * 136 + t0 * NH * 136 + h * 136, ap=[[d * NH * 136, 128], [1, 136]])
                    P.dma(SP, dap, O.t, reads=O.R, writes=[self.R_Og])

                for s in range(32 + 4):
                    if s in (5, 13, 21):
                        load_head(h + 1, (s - 5) // 8)
                    if s < 32:
                        stage_A(s)
                    if 0 <= s - 2 < 32:
                        stage_B(s - 2)
                    if 0 <= s - 4 < 32:
                        stage_C(s - 4)
        ogb = [self.mk(self.UT + i * 14 * KIB, F32, [3, NH, 136]) for i in range(4)]
        cws = [self.mk(self.ZX + 18 * KIB + i * KIB, F32, [64]) for i in range(2)]
        oc = [self.mk(self.ZX + 20 * KIB + i * 4 * KIB, F32, [NH, 128]) for i in range(2)]
        tms = [self.mk(self.ZX + 28 * KIB + i * 4 * KIB, F32, [NH, 128]) for i in range(4)]
        ot = [self.mk(self.ZX + 44 * KIB + i * KIB, BF16, [512]) for i in range(4)]
        oti = 0
        def load_og(st_):
            if st_ < 32:
                og_ = ogb[st_ % 4]
                P.dma(SP, [og_.t[:, g, :, :] for g in range(3)], [self.Og[g, st_ * 128:(st_ + 1) * 128, :, :] for g in range(3)],
                      reads=[self.R_Og], writes=og_.R)
        for i_ in range(3):
            load_og(i_)
        for tt in range(8):
            for j in range(4):
                st_ = tt * 4 + j
                og = ogb[st_ % 4]
                load_og(st_ + 3)
                L = og.t[:, :, :, 128:129].rearrange("p g h o -> p g (h o)")
                cw = cws[st_ % 2]
                mx = cw.t[:, 0:8]
                ee = cw.t[:, 8:32].rearrange("p (g h) -> p g h", g=3)
                den = cw.t[:, 32:40]
                P.op(DVE, lambda e, L=L, mx=mx: e.tensor_tensor(out=mx, in0=L[:, 0, :], in1=L[:, 1, :], op=ALU.max), reads=og.R, writes=cw.R)
                P.op(DVE, lambda e, L=L, mx=mx: e.tensor_tensor(out=mx, in0=mx, in1=L[:, 2, :], op=ALU.max), reads=og.R + cw.R, writes=cw.R)
                P.op(DVE, lambda e, L=L, mx=mx, ee=ee: e.tensor_tensor(out=ee, in0=L, in1=mx.unsqueeze(1).broadcast_to([128, 3, 8]), op=ALU.subtract), reads=og.R + cw.R, writes=cw.R)
                P.op(ACT, lambda e, ee=ee: e.activation(out=ee, in_=ee, func=AF.Exp), reads=cw.R, writes=cw.R)
                P.op(DVE, lambda e, ee=ee, den=den: e.tensor_tensor(out=den, in0=ee[:, 0, :], in1=ee[:, 1, :], op=ALU.add), reads=cw.R, writes=cw.R)
                P.op(DVE, lambda e, ee=ee, den=den: e.tensor_tensor(out=den, in0=den, in1=ee[:, 2, :], op=ALU.add), reads=cw.R, writes=cw.R)
                P.op(DVE, lambda e, den=den: e.reciprocal(out=den, in_=den), reads=cw.R, writes=cw.R)
                P.op(DVE, lambda e, ee=ee, den=den: e.tensor_tensor(out=ee, in0=ee, in1=den.unsqueeze(1).broadcast_to([128, 3, 8]), op=ALU.mult), reads=cw.R, writes=cw.R)
                o = oc[st_ % 2]
                for g in range(3):
                    wg = ee[:, g, :].unsqueeze(2).broadcast_to([128, 8, 128])
                    tm = tms[(st_ % 2) * 2 + (g - 1)] if g > 0 else None
                    dstb = o if g == 0 else tm
                    P.op(DVE, lambda e, og=og, g=g, wg=wg, dstb=dstb: e.tensor_tensor(out=dstb.t, in0=og.t[:, g, :, 0:128], in1=wg, op=ALU.mult),
                         reads=og.R + cw.R, writes=dstb.R)
                    if g > 0:
                        P.op(POOL, lambda e: e.tensor_tensor(out=o.t, in0=o.t, in1=tm.t, op=ALU.add), reads=o.R + tm.R, writes=o.R)
                for hh in range(0, 8, 4):
                    ps, pr = self.bank()
                    for q in range(4):
                        P.op(PE, lambda e, ps=ps, q=q, hh=hh, o=o: e.transpose(out=ps[:, q * 128:(q + 1) * 128], in_=o.t[:, hh + q, :], identity=self.ident),
                             reads=o.R + self.cstb.R, writes=[pr])
                    ob_ = ot[oti % 4]
                    oti += 1
                    P.op(ACT, lambda e, ps=ps, ob_=ob_: e.copy(out=ob_.t, in_=ps[:, :]), reads=[pr], writes=ob_.R)
                    dap = bass.AP(tensor=self.oT.tensor, offset=self.oT.offset + hh * 128 * T + st_ * 128, ap=[[T, 128], [128 * T, 4], [1, 128]])
                    P.dma(SP, dap, ob_.t.rearrange("p (a b) -> p a b", a=4), reads=ob_.R, writes=[self.R_oT])
        self.out_proj(self.at_w_out[idx], False)

    def build(self, phases=None):
        self.init_consts()
        self.prologue()
        self.adaln(0)
        for l in range(self.n_layers):
            self.set_layer(l)
            if l % 2 == 0:
                self.hgrn(l)
            else:
                self.attn(l)
            self.ffn(l)
        self.P.emit(final_regions=self.R_out)
        return self.nc


def make_consts():
    c = np.zeros((128, 1680), np.float32)
    c[:, 0:128] = np.eye(128, dtype=np.float32)
    c[:, 128:256] = 1.0
    s = np.arange(128)[:, None]
    cc = np.arange(128)[None, :]
    c[:, 256:384] = ((s // 32 == cc // 32) & (s <= cc)).astype(np.float32)
    qi = np.arange(128)[:, None]
    kj = np.arange(128)[None, :]
    c[:, 384:512] = np.where(kj >= qi, 0.0, NEG)
    c[:, 512:640] = np.where(kj <= qi, 0.0, NEG)
    m = np.ones(1024, np.float32)
    m[::32] = 0.0
    c[:, 640:1664] = m[None, :]
    j = (np.arange(128) % 16).astype(np.float64)
    c[:, 1664] = (500000.0 ** (-(2.0 * j) / 32.0)).astype(np.float32)
    c[:, 1665] = EPS
    c[:, 1666] = 1.0
    c[:, 1667] = np.pi / 2
    for i4 in range(4):
        c[i4 * 32:(i4 + 1) * 32, 1668 + i4] = 1.0
    return c


_CACHE = {}


def kernel(x, c, positions, ada_w, ada_b, norm_g, hgrn_w_in, hgrn_lower_bounds, hgrn_norm_g,
           hgrn_w_out, attn_w_in, attn_w_out, ffn_w_in, ffn_w_out, _n_layers=4, _cores=8):
    if _n_layers not in _CACHE:
        _CACHE[_n_layers] = Builder(n_layers=_n_layers).build()
    nc = _CACHE[_n_layers]
    f = lambda a: np.ascontiguousarray(np.asarray(a, dtype=np.float32))
    shared = {
        "cst": make_consts(), "ada_w": f(ada_w), "ada_b": f(ada_b), "norm_g": f(norm_g),
        "hgrn_w_in": f(hgrn_w_in), "hgrn_lower_bounds": f(hgrn_lower_bounds), "hgrn_norm_g": f(hgrn_norm_g),
        "hgrn_w_out": f(hgrn_w_out), "attn_w_in": f(attn_w_in), "attn_w_out": f(attn_w_out),
        "ffn_w_in": f(ffn_w_in), "ffn_w_out": f(ffn_w_out),
    }
    x = np.asarray(x, dtype=np.float32)
    c = np.asarray(c, dtype=np.float32)
    positions = np.asarray(positions, dtype=np.int32)
    in_maps = []
    for b in range(_cores):
        m = dict(shared)
        m["x"] = np.ascontiguousarray(x[b])
        m["c"] = np.ascontiguousarray(c[b:b + 1])
        m["pos"] = np.ascontiguousarray(positions[b:b + 1])
        in_maps.append(m)
    res = run_bass_kernel_spmd(nc, in_maps, core_ids=list(range(_cores)))
    return np.stack([np.asarray(r["out"], dtype=np.float32) for r in res.results], axis=0)
```

```python
import numpy as np
import concourse.bass as bass
import concourse.mybir as mybir
from concourse.bass_utils import run_bass_kernel_spmd

F32 = mybir.dt.float32
BF16 = mybir.dt.bfloat16
I32 = mybir.dt.int32
AF = mybir.ActivationFunctionType
ALU = mybir.AluOpType
AX = mybir.AxisListType

PE, ACT, DVE, POOL, SP = "tensor", "scalar", "vector", "gpsimd", "sync"
ENGINES = [PE, ACT, DVE, POOL, SP]

D = 1024
T = 4096
DFF = 2816
NH = 8
NEG = -30000.0
EPS = 1e-6
TWO_PI = 6.283185307179586


class Region:
    __slots__ = ("name", "last_write", "reads", "dma_sem", "dma_count", "accum", "writers", "last_dma")

    def __init__(self, name, accum=False):
        self.name = name
        self.last_write = None
        self.reads = {}
        self.dma_sem = None
        self.dma_count = 0
        self.accum = accum
        self.writers = {}
        self.last_dma = None


class Instr:
    __slots__ = ("eng", "fn", "deps", "needed", "count", "is_dma", "dma_sem", "dma_val")

    def __init__(self, eng, fn, is_dma=False):
        self.eng = eng
        self.fn = fn
        self.deps = []
        self.needed = False
        self.count = None
        self.is_dma = is_dma
        self.dma_sem = None
        self.dma_val = None


class _Rec:
    def __init__(self):
        self.call = None

    def __getattr__(self, name):
        def f(*a, **k):
            self.call = (name, a, k)
        return f


class Prog:
    def __init__(self, nc):
        self.nc = nc
        self.streams = {e: [] for e in ENGINES}
        self.n_dma_sems = 0

    def _collect(self, ins, reads, writes, extra=()):
        deps = {}

        def add(i):
            if i is not None and i is not ins:
                deps[id(i)] = i
        for i in extra:
            add(i)
        for r in reads:
            if r.accum:
                for i in r.writers.values():
                    add(i)
            else:
                add(r.last_write)
        for w in writes:
            if not w.accum:
                add(w.last_write)
            for i in w.reads.values():
                add(i)
        ins.deps = list(deps.values())
        key = ("s", id(ins.dma_sem)) if ins.is_dma else ins.eng
        for r in reads:
            r.reads[key] = ins
        for w in writes:
            if w.accum:
                w.writers[key] = ins
            else:
                w.last_write = ins
                w.reads = {}

    def op(self, eng, fn, reads=(), writes=()):
        rec = _Rec()
        fn(rec)
        name, a, k = rec.call
        ins = Instr(eng, lambda e, name=name, a=a, k=k: getattr(e, name)(*a, **k))
        self._collect(ins, reads, writes)
        self.streams[eng].append(ins)
        return ins

    def dma(self, queue, out_ap, in_ap, reads=(), writes=(), sem_region=None, **kw):
        outs = out_ap if isinstance(out_ap, (list, tuple)) else [out_ap]
        ins_ = in_ap if isinstance(in_ap, (list, tuple)) else [in_ap]
        n = len(outs)

        def fn(eng, outs=outs, ins_=ins_, kw=kw):
            return [eng.dma_start(out=o, in_=i, **kw) for o, i in zip(outs, ins_)]
        ins = Instr(queue, fn, is_dma=True)
        sr = sem_region
        if sr is None:
            for w in writes:
                if not w.accum:
                    sr = w
                    break
        if sr is None:
            for r in reads:
                if not r.accum:
                    sr = r
                    break
        if sr.dma_sem is None:
            sr.dma_sem = self.nc.alloc_semaphore("d_" + sr.name)
            self.n_dma_sems += 1
        sr.dma_count += 16 * n
        ins.dma_sem = sr.dma_sem
        ins.dma_val = sr.dma_count
        extra = (sr.last_dma,) if sr.last_dma is not None else ()
        sr.last_dma = ins
        self._collect(ins, reads, writes, extra)
        self.streams[queue].append(ins)
        return ins

    def emit(self, final_regions=()):
        nc = self.nc
        for e in ENGINES:
            for ins in self.streams[e]:
                for d in ins.deps:
                    if d.is_dma:
                        continue
                    if d.eng == ins.eng and d.eng == PE and not ins.is_dma:
                        continue
                    d.needed = True
        sems = {e: nc.alloc_semaphore("s_" + e) for e in ENGINES}
        for e in ENGINES:
            c = 0
            for ins in self.streams[e]:
                if not ins.is_dma and ins.needed:
                    c += 1
                    ins.count = c
        streams = self.streams
        final_waits = []
        for r in final_regions:
            for i in r.writers.values():
                final_waits.append((i.dma_sem, i.dma_val))

        def body(e):
            def run(eng):
                waited = {}
                for ins in streams[e]:
                    need = {}
                    for d in ins.deps:
                        if d.is_dma:
                            s, v = d.dma_sem, d.dma_val
                        else:
                            if d.eng == e and e == PE and not ins.is_dma:
                                continue
                            s, v = sems[d.eng], d.count
                        key = id(s)
                        if waited.get(key, 0) >= v:
                            continue
                        if key not in need or need[key][1] < v:
                            need[key] = (s, v)
                    for key, (s, v) in need.items():
                        eng.wait_ge(s, v)
                        waited[key] = v
                    r = ins.fn(eng)
                    if ins.is_dma:
                        for rr_ in r:
                            rr_.then_inc(ins.dma_sem, 16)
                    elif ins.needed:
                        r.then_inc(sems[e], 1)
                if e == SP:
                    for (s, v) in final_waits:
                        eng.wait_ge(s, v)
            return run

        with nc.Block() as block:
            block.sync(body(SP))
            block.tensor(body(PE))
            block.scalar(body(ACT))
            block.vector(body(DVE))
            block.gpsimd(body(POOL))


class Buf:
    def __init__(self, B, off, dtype, shape):
        self.B = B
        self.off = off
        self.dtype = dtype
        esz = 2 if dtype == BF16 else 4
        n = 1
        for s in shape:
            n *= s
        self.size = n * esz
        assert off % 4 == 0
        w0 = off // 4
        w1 = (off + self.size + 3) // 4
        v = B.arena[:, w0:w1]
        if dtype != F32:
            v = v.bitcast(dtype)
        if len(shape) == 2:
            v = v.rearrange("p (a b) -> p a b", b=shape[1])
        elif len(shape) == 3:
            v = v.rearrange("p (a b c) -> p a b c", b=shape[1], c=shape[2])
        elif len(shape) == 4:
            v = v.rearrange("p (a b c d) -> p a b c d", b=shape[1], c=shape[2], d=shape[3])
        self.t = v
        self.R = self.rr(0, self.size)

    def rr(self, lo, hi):
        p0 = (self.off + lo) // 1024
        p1 = (self.off + hi - 1) // 1024
        return [self.B.pages[i] for i in range(p0, p1 + 1)]


KIB = 1024


class SB:
    def __init__(self, nc, name, free, dtype):
        self.tensor = nc.alloc_sbuf_tensor(name, [128, free], dtype)
        self.t = self.tensor[:, :]
        self.R = [Region(name)]

    def rr(self, lo, hi):
        return self.R


class Builder:
    def __init__(self, n_layers=4, dbg=False):
        self.n_layers = n_layers
        nc = bass.Bass("TRN2", target_bir_lowering=False)
        self.nc = nc
        self.P = Prog(nc)
        dt = nc.dram_tensor
        self.x = dt("x", [T, D], F32, kind="ExternalInput").ap()
        self.c = dt("c", [1, D], F32, kind="ExternalInput").ap()
        self.pos = dt("pos", [1, T], I32, kind="ExternalInput").ap()
        self.cst = dt("cst", [128, 1680], F32, kind="ExternalInput").ap()
        self.ada_w = dt("ada_w", [4, D, 6 * D], F32, kind="ExternalInput").ap()
        self.ada_b = dt("ada_b", [4, 6 * D], F32, kind="ExternalInput").ap()
        self.norm_g = dt("norm_g", [4, 4, D], F32, kind="ExternalInput").ap()
        self.hg_w_in = dt("hgrn_w_in", [2, D, 4 * D], F32, kind="ExternalInput").ap()
        self.hg_lb = dt("hgrn_lower_bounds", [2, D], F32, kind="ExternalInput").ap()
        self.hg_ng = dt("hgrn_norm_g", [2, D], F32, kind="ExternalInput").ap()
        self.hg_w_out = dt("hgrn_w_out", [2, D, D], F32, kind="ExternalInput").ap()
        self.at_w_in = dt("attn_w_in", [2, D, 9 * D], F32, kind="ExternalInput").ap()
        self.at_w_out = dt("attn_w_out", [2, D, D], F32, kind="ExternalInput").ap()
        self.ff_w_in = dt("ffn_w_in", [4, D, 2 * DFF], F32, kind="ExternalInput").ap()
        self.ff_w_out = dt("ffn_w_out", [4, DFF, D], F32, kind="ExternalInput").ap()
        self.out = dt("out", [T, D], F32, kind="ExternalOutput").ap()
        self.actT = dt("actT", [DFF, T], BF16).ap()
        self.oT = dt("oT", [D, T], BF16).ap()
        self.QT = dt("QT", [NH, 128, T], BF16).ap()
        self.KT = dt("KT", [NH, 128, T], BF16).ap()
        self.Vd = dt("Vd", [NH, 128, 32, 128], BF16).ap()
        self.Og = dt("Og", [3, T, NH, 136], F32).ap()
        self.rope = dt("rope", [3, 2, 128, T], F32).ap()
        self.R_out = [Region(f"out{i}", accum=True) for i in range(8)]
        self.R_x = Region("x", accum=True)
        self.R_actT = Region("actT", accum=True)
        self.R_oT = Region("oT", accum=True)
        self.R_QKV = Region("QKV", accum=True)
        self.R_Og = Region("Og", accum=True)
        self.R_rope = Region("rope", accum=True)
        self.R_in = Region("inputs", accum=True)
        self.ARENA_KIB = 190
        self.arena = nc.alloc_sbuf_tensor("arena", [128, self.ARENA_KIB * 256], F32)
        self.pages = [Region(f"pg{i}") for i in range(self.ARENA_KIB)]
        self.ps = [nc.alloc_psum_tensor(f"ps{i}", [128, 512], F32) for i in range(8)]
        self.PB = [Region(f"psb{i}") for i in range(8)]
        B = lambda off, dtype, shape: Buf(self, off, dtype, shape)
        self.mk = B
        self.cstb = B(0, F32, [1680])
        c = self.cstb.t
        self.ident = c[:, 0:128]
        self.ones = c[:, 128:256]
        self.maskT = c[:, 256:384]
        self.m2 = c[:, 384:640]
        self.scanmask = c[:, 640:1664]
        self.invf = c[:, 1664:1665]
        self.epsc = c[:, 1665:1666]
        self.one11 = c[0:1, 1666:1667]
        self.halfpi = c[:, 1667:1668]
        self.cmask = c[:, 1668:1672]
        self.zeroc = c[:, 1672:1673]
        self.identb = B(7 * KIB, BF16, [128])
        self.small = B(7 * KIB + 256, F32, [192])
        s = self.small.t
        self.condc = s[:, 0:8]
        self.modcols = s[:, 8:40]
        self.hgc = s[:, 40:104]
        self.Gb = [B(8 * KIB, F32, [1024]), B(12 * KIB, F32, [1024])]
        self.UT = 16 * KIB
        self.ZW = 80 * KIB
        self.ZX = 128 * KIB
        self.uT = B(self.UT, BF16, [8, T])
        self.wst = [B(self.ZW, F32, [8, 512]), B(self.ZW + 16 * KIB, F32, [8, 512])]
        self.wbf = [B(self.ZW + 32 * KIB, BF16, [8, 512]), B(self.ZW + 40 * KIB, BF16, [8, 512])]
        mc2 = SB(nc, "modcols2", 32, F32)
        self.modsets = [(self.modcols, self.small.R, self.Gb),
                        (mc2.t, mc2.R, [SB(nc, "gb2a", 1024, F32), SB(nc, "gb2b", 1024, F32)])]
        self.set_layer(0)
        self.maskb = nc.alloc_sbuf_tensor("maskb", [128, 256], BF16)
        self.R_maskb = [Region("maskb")]
        self.psi = 0
        self.bank_pool = list(range(8))
        self.wi = 0

    def set_layer(self, l):
        self.modcols, self.modR, self.Gb = self.modsets[l % 2]

    def bank(self):
        pool = self.bank_pool
        i = pool[self.psi % len(pool)]
        self.psi += 1
        return self.ps[i], self.PB[i]

    def init_consts(self):
        P = self.P
        P.dma(SP, self.cstb.t, self.cst, reads=[self.R_in], writes=self.cstb.R)
        P.op(POOL, lambda e: e.tensor_copy(out=self.identb.t, in_=self.ident),
             reads=self.cstb.R, writes=self.identb.R)
        P.op(POOL, lambda e: e.tensor_copy(out=self.maskb[:, :], in_=self.m2), reads=self.cstb.R, writes=self.R_maskb)

    def columnize(self, row_ap, row_R, out_ap, out_R, nch, func=None):
        P = self.P
        ps, pr = self.bank()
        for cch in range(nch):
            P.op(PE, lambda e, cch=cch, ps=ps: e.matmul(ps[:, cch:cch + 1], lhsT=row_ap[0:1, cch * 128:(cch + 1) * 128],
                                                        rhs=self.one11, start=True, stop=True),
                 reads=row_R + self.cstb.R, writes=[pr])
        if func is None:
            P.op(DVE, lambda e, ps=ps: e.tensor_copy(out=out_ap, in_=ps[:, 0:nch]), reads=[pr], writes=out_R)
        else:
            P.op(ACT, lambda e, ps=ps: e.activation(out=out_ap, in_=ps[:, 0:nch], func=func), reads=[pr], writes=out_R)

    def rstd_from_ss(self, ss_ap, R, n_div):
        P = self.P
        npart = ss_ap.shape[0]
        P.op(ACT, lambda e: e.activation(out=ss_ap, in_=ss_ap, func=AF.Ln, scale=1.0 / n_div, bias=self.epsc[0:npart, :]),
             reads=R + self.cstb.R, writes=R)
        P.op(ACT, lambda e: e.activation(out=ss_ap, in_=ss_ap, func=AF.Exp, scale=-0.5), reads=R, writes=R)

    def prologue(self):
        P = self.P
        Z = self.ZX
        rowa = self.mk(Z, F32, [1024])
        rowb = self.mk(Z + 4 * KIB, F32, [1024])
        P.dma(SP, rowa.t[0:1, :], self.c, reads=[self.R_in], writes=rowa.R)
        self.columnize(rowa.t, rowa.R, self.condc, self.small.R, 8, func=AF.Silu)
        hgc = self.hgc
        for idx in range(2):
            base = idx * 32
            C1 = hgc[:, base:base + 8]
            C2 = hgc[:, base + 8:base + 16]
            NC1 = hgc[:, base + 16:base + 24]
            HNG = hgc[:, base + 24:base + 32]
            if idx == 0:
                P.op(DVE, lambda e, C1=C1: e.memset(C1, 0.5), writes=self.small.R)
                P.op(DVE, lambda e, C2=C2: e.memset(C2, 0.5), writes=self.small.R)
                P.op(DVE, lambda e, NC1=NC1: e.memset(NC1, -0.5), writes=self.small.R)
            else:
                P.dma(SP, rowa.t[0:1, :], self.hg_lb[0:1, :], reads=[self.R_in], writes=rowa.R)
                P.dma(SP, rowb.t[0:1, :], self.hg_lb[1:2, :], reads=[self.R_in], writes=rowb.R)
                P.op(DVE, lambda e: e.tensor_tensor(out=rowa.t[0:1, :], in0=rowa.t[0:1, :], in1=rowb.t[0:1, :], op=ALU.subtract),
                     reads=rowa.R + rowb.R, writes=rowa.R)
                self.columnize(rowa.t, rowa.R, C1, self.small.R, 8, func=AF.Exp)
                P.op(DVE, lambda e, C1=C1: e.tensor_scalar(out=C1, in0=C1, scalar1=1.0, scalar2=None, op0=ALU.add),
                     reads=self.small.R, writes=self.small.R)
                P.op(DVE, lambda e, C1=C1, C2=C2: e.reciprocal(out=C2, in_=C1), reads=self.small.R, writes=self.small.R)
                P.op(DVE, lambda e, C1=C1, C2=C2: e.tensor_scalar(out=C1, in0=C2, scalar1=-0.5, scalar2=0.5, op0=ALU.mult, op1=ALU.add),
                     reads=self.small.R, writes=self.small.R)
                P.op(DVE, lambda e, C1=C1, NC1=NC1: e.tensor_scalar(out=NC1, in0=C1, scalar1=-1.0, scalar2=None, op0=ALU.mult),
                     reads=self.small.R, writes=self.small.R)
                P.op(DVE, lambda e, C2=C2: e.tensor_scalar(out=C2, in0=C2, scalar1=0.5, scalar2=0.5, op0=ALU.mult, op1=ALU.add),
                     reads=self.small.R, writes=self.small.R)
            P.dma(SP, rowb.t[0:1, :], self.hg_ng[idx:idx + 1, :], reads=[self.R_in], writes=rowb.R)
            self.columnize(rowb.t, rowb.R, HNG, self.small.R, 8)
        posi = self.mk(self.UT, I32, [T])
        ang = self.mk(self.UT + 16 * KIB, F32, [T])
        t1 = self.mk(self.UT + 32 * KIB, F32, [T])
        t2 = self.mk(self.UT + 48 * KIB, F32, [T])
        ti = self.mk(self.ZW, I32, [T])
        tab = [self.mk(self.ZW + 16 * KIB, F32, [T]), self.mk(self.ZW + 32 * KIB, F32, [T])]
        prm = self.mk(self.ZX + 8 * KIB, F32, [T])
        P.dma(SP, posi.t, self.pos[0:1, :].broadcast_to([128, T]), reads=[self.R_in], writes=posi.R)
        P.op(DVE, lambda e: e.tensor_copy(out=ang.t, in_=posi.t), reads=posi.R, writes=ang.R)
        P.op(DVE, lambda e: e.tensor_scalar(out=ang.t, in0=ang.t, scalar1=self.invf, scalar2=None, op0=ALU.mult),
             reads=ang.R + self.cstb.R, writes=ang.R)
        for which in range(2):
            src = ang
            if which == 0:
                P.op(DVE, lambda e: e.tensor_scalar(out=t2.t, in0=ang.t, scalar1=float(np.pi / 2), scalar2=None, op0=ALU.add),
                     reads=ang.R, writes=t2.R)
                src = t2
            P.op(DVE, lambda e, src=src: e.tensor_scalar(out=t1.t, in0=src.t, scalar1=float(1.0 / TWO_PI), scalar2=None, op0=ALU.mult),
                 reads=src.R, writes=t1.R)
            P.op(DVE, lambda e: e.tensor_copy(out=ti.t, in_=t1.t), reads=t1.R, writes=ti.R)
            P.op(DVE, lambda e: e.tensor_copy(out=t1.t, in_=ti.t), reads=ti.R, writes=t1.R)
            r = tab[which]
            P.op(DVE, lambda e, src=src, r=r: e.scalar_tensor_tensor(out=r.t, in0=t1.t, scalar=-TWO_PI, in1=src.t, op0=ALU.mult, op1=ALU.add),
                 reads=t1.R + src.R, writes=r.R)
            P.op(DVE, lambda e, r=r: e.tensor_scalar(out=t1.t, in0=r.t, scalar1=float(np.pi), scalar2=-TWO_PI, op0=ALU.is_gt, op1=ALU.mult),
                 reads=r.R, writes=t1.R)
            P.op(DVE, lambda e, r=r: e.tensor_tensor(out=r.t, in0=r.t, in1=t1.t, op=ALU.add), reads=r.R + t1.R, writes=r.R)
            P.op(DVE, lambda e, r=r: e.tensor_scalar(out=t1.t, in0=r.t, scalar1=float(-np.pi), scalar2=TWO_PI, op0=ALU.is_lt, op1=ALU.mult),
                 reads=r.R, writes=t1.R)
            P.op(DVE, lambda e, r=r: e.tensor_tensor(out=r.t, in0=r.t, in1=t1.t, op=ALU.add), reads=r.R + t1.R, writes=r.R)
            P.op(DVE, lambda e, r=r: e.tensor_scalar(out=r.t, in0=r.t, scalar1=3.14159, scalar2=-3.14159, op0=ALU.min, op1=ALU.max),
                 reads=r.R, writes=r.R)
            P.op(ACT, lambda e, r=r: e.activation(out=r.t, in_=r.t, func=AF.Sin), reads=r.R, writes=r.R)
        for g, d in enumerate((1, 4, 16)):
            for which in range(2):
                if d == 1:
                    src = tab[which]
                else:
                    P.op(POOL, lambda e, which=which, d=d: e.tensor_copy(
                        out=prm.t.rearrange("p (r m) -> p r m", r=d), in_=tab[which].t.rearrange("p (m r) -> p r m", r=d)),
                        reads=tab[which].R, writes=prm.R)
                    src = prm
                P.dma(SP, self.rope[g, which], src.t, reads=src.R, writes=[self.R_rope])

    def adaln(self, l):
        for _ in self.adaln_gen(l):
            pass

    def adaln_gen(self, l):
        P = self.P
        Z = self.ZW + 32 * KIB
        modcols, modR, Gbs = self.modsets[l % 2]
        brow = self.mk(Z, F32, [512])
        nrow = self.mk(Z + 2 * KIB, F32, [512])
        rowt = self.mk(Z + 4 * KIB, F32, [512])
        def load_ada(s_):
            if s_ < 12:
                P.dma(SP, self.wst[s_ % 2].t, self.ada_w[l, :, s_ * 512:(s_ + 1) * 512].rearrange("(k p) n -> p k n", p=128),
                      reads=[self.R_in], writes=self.wst[s_ % 2].R)
        load_ada(0)
        for s in range(12):
            v, half = s // 2, s % 2
            w = self.wst[s % 2]
            load_ada(s + 1)
            P.dma(SP, brow.t[0:1, :], self.ada_b[l:l + 1, s * 512:(s + 1) * 512], reads=[self.R_in], writes=brow.R)
            ps, pr = self.bank()
            for kc in range(8):
                P.op(PE, lambda e, kc=kc, ps=ps, w=w: e.matmul(ps[0:1, :], lhsT=self.condc[:, kc:kc + 1], rhs=w.t[:, kc, :],
                                                               start=(kc == 0), stop=(kc == 7)),
                     reads=self.small.R + w.R, writes=[pr])
            P.op(DVE, lambda e, ps=ps: e.tensor_tensor(out=rowt.t[0:1, :], in0=ps[0:1, :], in1=brow.t[0:1, :], op=ALU.add),
                 reads=[pr] + brow.R, writes=rowt.R)
            if v in (1, 2, 4, 5):
                gi = {1: 0, 2: 1, 4: 2, 5: 3}[v]
                P.dma(SP, nrow.t[0:1, :], self.norm_g[l, gi:gi + 1, half * 512:(half + 1) * 512], reads=[self.R_in], writes=nrow.R)
                P.op(DVE, lambda e: e.scalar_tensor_tensor(out=rowt.t[0:1, :], in0=rowt.t[0:1, :], scalar=1.0, in1=nrow.t[0:1, :],
                                                           op0=ALU.add, op1=ALU.mult),
                     reads=rowt.R + nrow.R, writes=rowt.R)
            if v in (0, 1, 3, 4):
                base = {1: 0, 0: 8, 4: 16, 3: 24}[v] + 4 * half
                self.columnize(rowt.t, rowt.R, modcols[:, base:base + 4], modR, 4)
            else:
                G = Gbs[0 if v == 2 else 1]
                ps2, pr2 = self.bank()
                P.op(PE, lambda e, ps2=ps2: e.matmul(ps2[:, :], lhsT=self.ones[0:1, :], rhs=rowt.t[0:1, :], start=True, stop=True),
                     reads=self.cstb.R + rowt.R, writes=[pr2])
                P.op(ACT, lambda e, ps2=ps2, G=G, half=half: e.copy(out=G.t[:, half * 512:(half + 1) * 512], in_=ps2[:, :]),
                     reads=[pr2], writes=G.rr(half * 2048, half * 2048 + 2048))
            yield s

    def tok_rows(self, src, d, pos0, n):
        L = T // d
        r, m = pos0 // L, pos0 % L
        t0 = m * d + r
        return bass.AP(tensor=src.tensor, offset=src.offset + t0 * D, ap=[[d * D, n], [1, D]])

    def norm_T(self, first, d, mset):
        P = self.P
        src = self.x if first else self.out
        Z = self.ZX
        hin = [self.mk(Z, F32, [4, 1024]), self.mk(Z + 16 * KIB, F32, [4, 1024])]
        xn = self.mk(Z + 32 * KIB, F32, [4, 1024])
        junk = self.mk(Z + 48 * KIB, BF16, [1024])
        ssb = self.mk(Z + 50 * KIB, F32, [8])
        Ac = self.modcols[:, mset * 16:mset * 16 + 8]
        Bc = self.modcols[:, mset * 16 + 8:mset * 16 + 16]
        srcR = [self.R_x] if first else (self.R_out if d > 1 else None)
        for tt in range(8):
            h = hin[tt % 2]
            rr = srcR if srcR is not None else [self.R_out[tt]]
            P.dma(SP, [h.t[:, j, :] for j in range(4)], [self.tok_rows(src, d, tt * 512 + j * 128, 128) for j in range(4)],
                  reads=rr, writes=h.R)
            ss = ssb.t[:, (tt % 2) * 4:(tt % 2) * 4 + 4]
            for j in range(4):
                P.op(ACT, lambda e, h=h, j=j, ss=ss: e.activation(out=junk.t, in_=h.t[:, j, :], func=AF.Square, accum_out=ss[:, j:j + 1]),
                     reads=h.rr(j * 4096, j * 4096 + 4096), writes=junk.R + ssb.R)
            self.rstd_from_ss(ss, ssb.R, D)
            for j in range(4):
                P.op(DVE, lambda e, h=h, j=j, ss=ss: e.tensor_scalar(out=xn.t[:, j, :], in0=h.t[:, j, :], scalar1=ss[:, j:j + 1], scalar2=None, op0=ALU.mult),
                     reads=h.rr(j * 4096, j * 4096 + 4096) + ssb.R, writes=xn.rr(j * 4096, j * 4096 + 4096))
            for cch in range(8):
                ps, pr = self.bank()
                for j in range(4):
                    P.op(PE, lambda e, ps=ps, j=j, cch=cch: e.transpose(out=ps[:, j * 128:(j + 1) * 128], in_=xn.t[:, j, cch * 128:(cch + 1) * 128], identity=self.ident),
                         reads=xn.rr(j * 4096 + cch * 512, j * 4096 + cch * 512 + 512) + self.cstb.R, writes=[pr])
                o = self.uT.t[:, cch, tt * 512:(tt + 1) * 512]
                oR = self.uT.rr((cch * T + tt * 512) * 2, (cch * T + tt * 512 + 512) * 2)
                if cch % 2 == 0:
                    P.op(ACT, lambda e, ps=ps, o=o, cch=cch: e.activation(out=o, in_=ps[:, :], func=AF.Identity, scale=Ac[:, cch:cch + 1], bias=Bc[:, cch:cch + 1]),
                         reads=[pr] + self.modR, writes=oR)
                else:
                    P.op(DVE, lambda e, ps=ps, o=o, cch=cch: e.tensor_scalar(out=o, in0=ps[:, :], scalar1=Ac[:, cch:cch + 1], scalar2=Bc[:, cch:cch + 1], op0=ALU.mult, op1=ALU.add),
                         reads=[pr] + self.modR, writes=oR)

    def load_w(self, dram_ap, ncols, cast_views=None):
        P = self.P
        i = self.wi % 2
        self.wi += 1
        st, wb = self.wst[i], self.wbf[i]
        P.dma(SP, st.t[:, :, 0:ncols], dram_ap.rearrange("(k p) n -> p k n", p=128), reads=[self.R_in], writes=st.R)
        P.op(POOL, lambda e: e.tensor_copy(out=wb.t[:, :, 0:ncols], in_=st.t[:, :, 0:ncols]), reads=st.R, writes=wb.R)
        return wb

    def resid_setup(self, base, hbase):
        self.r_h = [self.mk(hbase + i * 4 * KIB, F32, [1024]) for i in range(3)]
        self.r_t = [self.mk(base + i * 4 * KIB, F32, [1024]) for i in range(2)]
        self.r_ss = self.mk(base + 8 * KIB, F32, [8])
        self.r_junk = self.mk(base + 9 * KIB, BF16, [512])
        self.r_i = 0

    def resid_prefetch(self, first, st):
        if st >= 32:
            return
        h = self.r_h[st % 3]
        src = self.x if first else self.out
        Rsrc = [self.R_x] if first else [self.R_out[st // 4]]
        self.P.dma(SP, h.t, src[st * 128:(st + 1) * 128, :], reads=Rsrc, writes=h.R)

    def resid(self, first, st, banks, G):
        P = self.P
        i = self.r_i % 2
        self.r_i += 1
        h, t = self.r_h[st % 3], self.r_t[i]
        ss = self.r_ss.t[:, i * 4:i * 4 + 2]
        sst = self.r_ss.t[:, i * 4 + 2:i * 4 + 3]
        for n, (ps, pr) in enumerate(banks):
            P.op(ACT, lambda e, ps=ps, n=n, ss=ss: e.activation(out=self.r_junk.t, in_=ps[:, :], func=AF.Square, accum_out=ss[:, n:n + 1]),
                 reads=[pr], writes=self.r_junk.R + self.r_ss.R)
        P.op(DVE, lambda e, ss=ss, sst=sst: e.tensor_tensor(out=sst, in0=ss[:, 0:1], in1=ss[:, 1:2], op=ALU.add),
             reads=self.r_ss.R, writes=self.r_ss.R)
        self.rstd_from_ss(sst, self.r_ss.R, D)
        for n, (ps, pr) in enumerate(banks):
            P.op(DVE, lambda e, ps=ps, n=n, sst=sst, t=t: e.scalar_tensor_tensor(out=t.t[:, n * 512:(n + 1) * 512], in0=ps[:, :], scalar=sst, in1=G.t[:, n * 512:(n + 1) * 512], op0=ALU.mult, op1=ALU.mult),
                 reads=[pr] + self.r_ss.R + G.rr(n * 2048, n * 2048 + 2048), writes=t.rr(n * 2048, n * 2048 + 2048))
        P.op(POOL, lambda e, t=t, h=h: e.tensor_tensor(out=t.t, in0=t.t, in1=h.t, op=ALU.add), reads=t.R + h.R, writes=t.R)
        P.dma(SP, self.out[st * 128:(st + 1) * 128, :], t.t, reads=t.R, writes=[self.R_out[st // 4]])

    def ffn(self, l):
        P = self.P
        self.norm_T(False, 1, 1)
        Z = self.ZX
        sg = [self.mk(Z + i * 2 * KIB, F32, [512]) for i in range(2)]
        ao = [self.mk(Z + 4 * KIB + i * KIB, BF16, [512]) for i in range(4)]
        k = 0

        def load_slab(s):
            st, wb = self.wst[s % 2], self.wbf[s % 2]
            P.dma(SP, [st.t[:, :, 0:256], st.t[:, :, 256:512]],
                  [self.ff_w_in[l, :, s * 256:(s + 1) * 256].rearrange("(k p) n -> p k n", p=128),
                   self.ff_w_in[l, :, DFF + s * 256:DFF + (s + 1) * 256].rearrange("(k p) n -> p k n", p=128)],
                  reads=[self.R_in], writes=st.R)
            P.op(POOL, lambda e: e.tensor_copy(out=wb.t, in_=st.t), reads=st.R, writes=wb.R)
        wo = self.mk(Z + 10 * KIB, BF16, [22, 1024])
        wos = [self.mk(Z + 54 * KIB + i * 4 * KIB, F32, [1024]) for i in range(2)]

        def load_wo(kc):
            st = wos[kc % 2]
            P.dma(SP, st.t, self.ff_w_out[l, kc * 128:(kc + 1) * 128, :], reads=[self.R_in], writes=st.R)
            P.op(POOL, lambda e: e.tensor_copy(out=wo.t[:, kc, :], in_=st.t), reads=st.R, writes=wo.rr(kc * 2048, kc * 2048 + 2048))
        load_slab(0)
        for s in range(11):
            wb = self.wbf[s % 2]
            if s + 1 < 11:
                load_slab(s + 1)
            load_wo(2 * s)
            load_wo(2 * s + 1)
            for tt in range(8):
                for cc in range(2):
                    pg, rg = self.bank()
                    pu, ru = self.bank()
                    for which, ps, pr in ((0, pg, rg), (1, pu, ru)):
                        for kc in range(8):
                            P.op(PE, lambda e, ps=ps, kc=kc, wb=wb, which=which, cc=cc, tt=tt: e.matmul(
                                ps[:, :], lhsT=wb.t[:, kc, which * 256 + cc * 128:which * 256 + cc * 128 + 128],
                                rhs=self.uT.t[:, kc, tt * 512:(tt + 1) * 512], start=(kc == 0), stop=(kc == 7)),
                                reads=wb.R + self.uT.rr((kc * T + tt * 512) * 2, (kc * T + tt * 512 + 512) * 2), writes=[pr])
                    sgb = sg[k % 2]
                    aob = ao[k % 4]
                    k += 1
                    P.op(ACT, lambda e, pg=pg, sgb=sgb: e.activation(out=sgb.t, in_=pg[:, :], func=AF.Silu), reads=[rg], writes=sgb.R)
                    P.op(DVE, lambda e, pu=pu, sgb=sgb, aob=aob: e.tensor_tensor(out=aob.t, in0=pu[:, :], in1=sgb.t, op=ALU.mult),
                         reads=[ru] + sgb.R, writes=aob.R)
                    row0 = (s * 2 + cc) * 128
                    P.dma(SP, self.actT[row0:row0 + 128, tt * 512:(tt + 1) * 512], aob.t, reads=aob.R, writes=[self.R_actT])
        at = [self.mk(self.UT, BF16, [22, 512]), self.mk(self.UT + 22 * KIB, BF16, [22, 512])]
        self.resid_setup(self.ZX, self.UT + 48 * KIB)
        gen = self.adaln_gen(l + 1) if l + 1 < self.n_layers else None

        def load_act(tt_):
            if tt_ < 8:
                P.dma(SP, at[tt_ % 2].t, self.actT[:, tt_ * 512:(tt_ + 1) * 512].rearrange("(k p) t -> p k t", p=128), reads=[self.R_actT], writes=at[tt_ % 2].R)
        load_act(0)
        self.resid_prefetch(False, 0)
        for st_ in range(32):
            if gen is not None and st_ >= 2 and st_ % 2 == 0:
                next(gen, None)
            a = at[(st_ // 4) % 2]
            j_ = st_ % 4
            if j_ == 0:
                load_act(st_ // 4 + 1)
            self.resid_prefetch(False, st_ + 1)
            banks = []
            for n in range(2):
                ps, pr = self.bank()
                for kc in range(22):
                    P.op(PE, lambda e, ps=ps, kc=kc, a=a, n=n, j_=j_: e.matmul(ps[:, :], lhsT=a.t[:, kc, j_ * 128:(j_ + 1) * 128], rhs=wo.t[:, kc, n * 512:(n + 1) * 512],
                                                                        start=(kc == 0), stop=(kc == 21)),
                         reads=a.R + wo.rr(kc * 2048 + n * 1024, kc * 2048 + n * 1024 + 1024), writes=[pr])
                banks.append((ps, pr))
            self.resid(False, st_, banks, self.Gb[1])
        if gen is not None:
            for _ in gen:
                pass

    def out_proj(self, w_dram, first):
        P = self.P
        wo = self.mk(self.ZW + 32 * KIB, BF16, [8, 1024])
        for kc in range(8):
            st = self.wst[kc % 2]
            stv = st.t.rearrange("p a b -> p (a b)")[:, 0:1024]
            P.dma(SP, stv, w_dram[kc * 128:(kc + 1) * 128, :], reads=[self.R_in], writes=st.rr(0, 4096))
            P.op(POOL, lambda e, stv=stv, kc=kc: e.tensor_copy(out=wo.t[:, kc, :], in_=stv), reads=st.rr(0, 4096),
                 writes=wo.rr(kc * 2048, kc * 2048 + 2048))
        at = [self.mk(self.UT + i * 8 * KIB, BF16, [8, 512]) for i in range(2)]
        self.resid_setup(self.ZX, self.UT + 48 * KIB)

        def load_o(tt):
            if tt < 8:
                P.dma(SP, at[tt % 2].t, self.oT[:, tt * 512:(tt + 1) * 512].rearrange("(k p) t -> p k t", p=128), reads=[self.R_oT], writes=at[tt % 2].R)
        load_o(0)
        self.resid_prefetch(first, 0)
        for tt in range(8):
            a = at[tt % 2]
            load_o(tt + 1)
            for j in range(4):
                self.resid_prefetch(first, tt * 4 + j + 1)
                banks = []
                for n in range(2):
                    ps, pr = self.bank()
                    for kc in range(8):
                        P.op(PE, lambda e, ps=ps, kc=kc, a=a, n=n, j=j: e.matmul(ps[:, :], lhsT=a.t[:, kc, j * 128:(j + 1) * 128],
                                                                                 rhs=wo.t[:, kc, n * 512:(n + 1) * 512], start=(kc == 0), stop=(kc == 7)),
                             reads=a.R + wo.rr(kc * 2048 + n * 1024, kc * 2048 + n * 1024 + 1024), writes=[pr])
                    banks.append((ps, pr))
                self.resid(first, tt * 4 + j, banks, self.Gb[0])

    def hgrn(self, l):
        P = self.P
        idx = l // 2
        first = (l == 0)
        self.norm_T(first, 1, 0)
        hb = idx * 32
        C1 = self.hgc[:, hb:hb + 8]
        C2 = self.hgc[:, hb + 8:hb + 16]
        NC1 = self.hgc[:, hb + 16:hb + 24]
        HNG = self.hgc[:, hb + 24:hb + 32]
        Z = self.ZX
        N = 1024
        th = self.mk(Z, F32, [N])
        qs = self.mk(Z + 4 * KIB, F32, [N])
        kk = self.mk(Z + 8 * KIB, F32, [N])
        bb = self.mk(Z + 12 * KIB, F32, [N])
        sets = []
        for i in range(2):
            b0 = Z + 16 * KIB + i * 13 * KIB
            sets.append(dict(qd=self.mk(b0, BF16, [N]), ki=self.mk(b0 + 2 * KIB, BF16, [N]), ke=self.mk(b0 + 4 * KIB, BF16, [N]),
                             vtm=self.mk(b0 + 6 * KIB, BF16, [8, 128]), gs=self.mk(b0 + 8 * KIB, F32, [N]), dec=self.mk(b0 + 12 * KIB, F32, [32])))
        ketm = self.mk(Z + 42 * KIB, BF16, [8, 128])
        vexp = self.mk(Z + 44 * KIB, BF16, [8, 4, 128])
        Sbf = self.mk(Z + 52 * KIB, BF16, [32, 128])
        sqo = self.mk(Z + 60 * KIB, F32, [512])
        t1 = sqo
        S32 = self.mk(self.ZW + 24 * KIB, F32, [33, 128])
        amt = self.mk(self.ZW + 41 * KIB, BF16, [4, 128])
        oo = [self.mk(self.ZW + 42 * KIB + i * KIB, BF16, [512]) for i in range(2)]
        rso = self.mk(self.ZW + 44 * KIB, F32, [512])
        stg = self.wst[0]
        wb = self.mk(self.ZW + 16 * KIB, BF16, [8, 4, 128])
        w_in = self.hg_w_in[idx]
        okc = [0]
        tbanks = {}

        def S1a(ui):
            h, qd_ = ui // 4, ui % 4
            B = sets[ui % 2]
            vtm, gs = B["vtm"], B["gs"]
            tok0 = qd_ * N
            if qd_ == 0:
                stv = stg.t.rearrange("p k (a b) -> p k a b", a=4)
                P.dma(SP, [stv[:, :, a, :] for a in range(4)],
                      [w_in[:, a * 1024 + h * 128:a * 1024 + (h + 1) * 128].rearrange("(k p) n -> p k n", p=128) for a in range(4)],
                      reads=[self.R_in], writes=stg.R)
                P.op(POOL, lambda e: e.tensor_copy(out=wb.t, in_=stv), reads=stg.R, writes=wb.R)
            for which, dst, func, scale in ((1, th, AF.Tanh, 0.5), (0, qs, AF.Silu, 1.0), (3, gs, AF.Silu, 1.0)):
                for t2 in range(2):
                    ps, pr = self.bank()
                    for kc in range(8):
                        P.op(PE, lambda e: e.matmul(ps[:, :], lhsT=wb.t[:, kc, which, :], rhs=self.uT.t[:, kc, tok0 + t2 * 512:tok0 + (t2 + 1) * 512],
                                                    start=(kc == 0), stop=(kc == 7)),
                             reads=wb.R + self.uT.rr((kc * T + tok0 + t2 * 512) * 2, (kc * T + tok0 + t2 * 512 + 512) * 2), writes=[pr])
                    P.op(ACT, lambda e: e.activation(out=dst.t[:, t2 * 512:(t2 + 1) * 512], in_=ps[:, :], func=func, scale=scale),
                         reads=[pr], writes=dst.rr(t2 * 2048, t2 * 2048 + 2048))
                    yield
            for t2 in range(2):
                ps, pr = self.bank()
                for j in range(4):
                    for kc in range(8):
                        p0 = tok0 + t2 * 512 + j * 128
                        P.op(PE, lambda e: e.matmul(ps[:, j * 128:(j + 1) * 128], lhsT=self.uT.t[:, kc, p0:p0 + 128], rhs=wb.t[:, kc, 2, :],
                                                    start=(kc == 0), stop=(kc == 7)),
                             reads=wb.R + self.uT.rr((kc * T + p0) * 2, (kc * T + p0 + 128) * 2), writes=[pr])
                    if j % 2 == 1:
                        yield
                P.op(ACT, lambda e: e.copy(out=vtm.t[:, t2 * 4:(t2 + 1) * 4, :].rearrange("p a b -> p (a b)"), in_=ps[:, :]),
                     reads=[pr], writes=vtm.rr(t2 * 1024, t2 * 1024 + 1024))

        def S1b(ui):
            h, qd_ = ui // 4, ui % 4
            B = sets[ui % 2]
            qd, ki, ke, dec = B["qd"], B["ki"], B["ke"], B["dec"]
            c1, c2, nc1 = C1[:, h:h + 1], C2[:, h:h + 1], NC1[:, h:h + 1]
            P.op(DVE, lambda e: e.tensor_scalar(out=kk.t, in0=th.t, scalar1=nc1, scalar2=c1, op0=ALU.mult, op1=ALU.add),
                 reads=th.R + self.small.R, writes=kk.R)
            yield
            P.op(DVE, lambda e: e.tensor_scalar(out=th.t, in0=th.t, scalar1=c1, scalar2=c2, op0=ALU.mult, op1=ALU.add),
                 reads=th.R + self.small.R, writes=th.R)
            P.op(ACT, lambda e: e.activation(out=th.t, in_=th.t, func=AF.Ln), reads=th.R, writes=th.R)
            yield
            P.op(DVE, lambda e: e.tensor_tensor_scan(out=bb.t, data0=self.scanmask, data1=th.t, initial=0.0, op0=ALU.mult, op1=ALU.add),
                 reads=th.R + self.cstb.R, writes=bb.R)
            P.op(ACT, lambda e: e.activation(out=th.t, in_=bb.t, func=AF.Exp), reads=bb.R, writes=th.R)
            b3 = bb.t.rearrange("p (n c) -> p n c", c=32)
            blast = b3[:, :, 31:32]
            P.op(ACT, lambda e: e.activation(out=dec.t, in_=blast.rearrange("p n c -> p (n c)"), func=AF.Exp), reads=bb.R, writes=dec.R)
            yield
            P.op(DVE, lambda e: e.tensor_tensor(out=qd.t, in0=qs.t, in1=th.t, op=ALU.mult), reads=qs.R + th.R, writes=qd.R)
            P.op(ACT, lambda e: e.activation(out=th.t, in_=bb.t, func=AF.Exp, scale=-1.0), reads=bb.R, writes=th.R)
            yield
            P.op(DVE, lambda e: e.tensor_tensor(out=ki.t, in0=kk.t, in1=th.t, op=ALU.mult), reads=kk.R + th.R, writes=ki.R)
            yield
            P.op(DVE, lambda e: e.tensor_tensor(out=th.t.rearrange("p (n c) -> p n c", c=32), in0=blast.broadcast_to([128, 32, 32]),
                                                in1=b3, op=ALU.subtract), reads=bb.R, writes=th.R)
            P.op(ACT, lambda e: e.activation(out=th.t, in_=th.t, func=AF.Exp), reads=th.R, writes=th.R)
            yield
            P.op(DVE, lambda e: e.tensor_tensor(out=ke.t, in0=kk.t, in1=th.t, op=ALU.mult), reads=kk.R + th.R, writes=ke.R)
            yield

        def S2T(ui):
            ke = sets[ui % 2]["ke"]
            tbanks[ui] = []
            for t2 in range(2):
                ps, pr = self.ps[6 + t2], self.PB[6 + t2]
                pb = ps[:, :].bitcast(BF16)
                for j in range(4):
                    blk = t2 * 4 + j
                    P.op(PE, lambda e: e.transpose(out=pb[:, j * 128:(j + 1) * 128], in_=ke.t[:, blk * 128:(blk + 1) * 128], identity=self.identb.t),
                         reads=ke.R + self.identb.R, writes=[pr])
                tbanks[ui].append((pb, pr))

        def S2E(ui):
            vtm = sets[ui % 2]["vtm"]
            for t2 in range(2):
                pb, pr = tbanks[ui][t2]
                P.op(DVE, lambda e: e.tensor_copy(out=ketm.t[:, t2 * 4:(t2 + 1) * 4, :].rearrange("p a b -> p (a b)"), in_=pb[:, 0:512]),
                     reads=[pr], writes=ketm.rr(t2 * 1024, t2 * 1024 + 1024))

        def S2Ev(ui):
            vtm = sets[ui % 2]["vtm"]
            for i4 in range(4):
                P.op(POOL, lambda e: e.tensor_scalar(out=vexp.t[:, :, i4, :], in0=vtm.t, scalar1=self.cmask[:, i4:i4 + 1], scalar2=1.0,
                                                     op0=ALU.mult, op1=ALU.mult),
                     reads=vtm.R + self.cstb.R, writes=vexp.R)

        def S2K(ui):
            qd_ = ui % 4
            dec = sets[ui % 2]["dec"]
            if qd_ == 0:
                P.op(DVE, lambda e: e.memset(S32.t[:, 0, :], 0.0), writes=S32.rr(0, 512))
            else:
                P.op(DVE, lambda e: e.tensor_copy(out=S32.t[:, 0, :], in_=S32.t[:, 32, :]), reads=S32.rr(32 * 512, 33 * 512), writes=S32.rr(0, 512))
            for blk in range(8):
                ps, pr = self.bank()
                P.op(PE, lambda e: e.matmul(ps[:, :], lhsT=ketm.t[:, blk, :], rhs=vexp.t[:, blk, :, :].rearrange("p a b -> p (a b)"), start=True, stop=True),
                     reads=ketm.R + vexp.R, writes=[pr])
                for i4 in range(4):
                    j = blk * 4 + i4
                    P.op(DVE, lambda e: e.scalar_tensor_tensor(
                        out=S32.t[:, j + 1, :], in0=S32.t[:, j, :], scalar=dec.t[:, j:j + 1], in1=ps[:, i4 * 128:(i4 + 1) * 128], op0=ALU.mult, op1=ALU.add),
                        reads=S32.rr(j * 512, j * 512 + 512) + dec.R + [pr], writes=S32.rr((j + 1) * 512, (j + 1) * 512 + 512))
                yield
            P.op(ACT, lambda e: e.copy(out=Sbf.t, in_=S32.t[:, 0:32, :]), reads=S32.R, writes=Sbf.R)

        def S2R(ui):
            h, qd_ = ui // 4, ui % 4
            B = sets[ui % 2]
            qd, ki, vtm, gs = B["qd"], B["ki"], B["vtm"], B["gs"]
            tok0 = qd_ * N
            for t2 in range(2):
                pa, ra = self.bank()
                for j in range(4):
                    blk = t2 * 4 + j
                    P.op(PE, lambda e: e.matmul(pa[:, j * 128:(j + 1) * 128], lhsT=ki.t[:, blk * 128:(blk + 1) * 128],
                                                rhs=qd.t[:, blk * 128:(blk + 1) * 128], start=True, stop=True),
                         reads=ki.R + qd.R, writes=[ra])
                P.op(DVE, lambda e: e.tensor_tensor(out=amt.t, in0=pa[:, :].rearrange("p (a b) -> p a b", a=4),
                                                    in1=self.maskT.unsqueeze(1).broadcast_to([128, 4, 128]), op=ALU.mult),
                     reads=[ra] + self.cstb.R, writes=amt.R)
                po, ro = self.bank()
                for j in range(4):
                    blk = t2 * 4 + j
                    P.op(PE, lambda e: e.matmul(po[:, j * 128:(j + 1) * 128], lhsT=vtm.t[:, blk, :], rhs=amt.t[:, j, :], start=True, stop=False),
                         reads=vtm.R + amt.R, writes=[ro])
                    for i4 in range(4):
                        ch = blk * 4 + i4
                        P.op(PE, lambda e: e.matmul(po[:, j * 128 + i4 * 32:j * 128 + (i4 + 1) * 32], lhsT=Sbf.t[:, ch, :], rhs=qd.t[:, ch * 32:(ch + 1) * 32],
                                                    start=False, stop=(i4 == 3)),
                             reads=Sbf.R + qd.R, writes=[ro])
                P.op(ACT, lambda e: e.activation(out=sqo.t, in_=po[:, :], func=AF.Square), reads=[ro], writes=sqo.R)
                pn, rn = self.bank()
                P.op(PE, lambda e: e.matmul(pn[:, :], lhsT=self.ones, rhs=sqo.t, start=True, stop=True), reads=self.cstb.R + sqo.R, writes=[rn])
                P.op(ACT, lambda e: e.activation(out=rso.t, in_=pn[:, :], func=AF.Ln, scale=1.0 / 128, bias=self.epsc), reads=[rn] + self.cstb.R, writes=rso.R)
                P.op(ACT, lambda e: e.activation(out=rso.t, in_=rso.t, func=AF.Exp, scale=-0.5), reads=rso.R, writes=rso.R)
                P.op(DVE, lambda e: e.tensor_tensor(out=t1.t, in0=po[:, :], in1=rso.t, op=ALU.mult), reads=[ro] + rso.R, writes=t1.R)
                o = oo[okc[0] % 2]
                okc[0] += 1
                P.op(DVE, lambda e: e.scalar_tensor_tensor(out=o.t, in0=t1.t, scalar=HNG[:, h:h + 1], in1=gs.t[:, t2 * 512:(t2 + 1) * 512],
                                                           op0=ALU.mult, op1=ALU.mult),
                     reads=t1.R + self.small.R + gs.rr(t2 * 2048, t2 * 2048 + 2048), writes=o.R)
                P.dma(SP, self.oT[h * 128:(h + 1) * 128, tok0 + t2 * 512:tok0 + (t2 + 1) * 512], o.t, reads=o.R, writes=[self.R_oT])
                yield

        def zipgen(g1, g2):
            d1 = d2 = False
            while not (d1 and d2):
                if not d1:
                    try:
                        next(g1)
                    except StopIteration:
                        d1 = True
                if not d2:
                    try:
                        next(g2)
                    except StopIteration:
                        d2 = True

        NU = NH * 4
        self.bank_pool = list(range(6))
        for _ in S1a(0):
            pass
        for _ in S1b(0):
            pass
        S2Ev(0)
        for ui in range(NU):
            nxt = ui + 1 < NU
            S2T(ui)
            S2E(ui)
            zipgen(S2K(ui), S1a(ui + 1) if nxt else iter(()))
            if nxt:
                S2Ev(ui + 1)
            zipgen(S2R(ui), S1b(ui + 1) if nxt else iter(()))
        self.bank_pool = list(range(8))
        self.out_proj(self.hg_w_out[idx], first)

    def attn(self, l):
        P = self.P
        idx = l // 2
        w_in = self.at_w_in[idx]
        scale = 128.0 ** -0.5
        Z = self.ZX
        for g, d in enumerate((1, 4, 16)):
            self.norm_T(False, d, 0)
            cs = [[self.mk(Z + (i * 2 + w) * 2 * KIB, F32, [512]) for w in range(2)] for i in range(2)]
            rt = [self.mk(Z + 8 * KIB + i * 2 * KIB, F32, [512]) for i in range(4)]
            ob = [self.mk(Z + 16 * KIB + i * KIB, BF16, [512]) for i in range(4)]
            wqk = self.mk(self.ZW + 32 * KIB, BF16, [8, 8, 128])
            oi = 0
            for qk in range(2):
                dst = self.QT if qk == 0 else self.KT
                col0 = g * 3072 + qk * 1024
                for half in range(2):
                    st = self.wst[half]
                    P.dma(SP, st.t, w_in[:, col0 + half * 512:col0 + (half + 1) * 512].rearrange("(k p) n -> p k n", p=128),
                          reads=[self.R_in], writes=st.R)
                    for kc in range(8):
                        src = st.t[:, kc, :].rearrange("p (h c j) -> p c h j", h=4, c=8)
                        dv = wqk.t[:, kc, :, :].rearrange("p c (h j) -> p c h j", h=8)[:, :, half * 4:(half + 1) * 4, :]
                        P.op(POOL, lambda e, src=src, dv=dv: e.tensor_copy(out=dv, in_=src), reads=st.rr(kc * 2048, kc * 2048 + 2048),
                             writes=wqk.rr(kc * 2048, kc * 2048 + 2048))
                for tt in range(8):
                    cst_ = cs[tt % 2]
                    for w in range(2):
                        P.dma(SP, cst_[w].t, self.rope[g, w, :, tt * 512:(tt + 1) * 512], reads=[self.R_rope], writes=cst_[w].R)
                    pss = []
                    for cch in range(8):
                        ps, pr = self.bank()
                        for kc in range(8):
                            P.op(PE, lambda e, ps=ps, kc=kc, cch=cch, tt=tt: e.matmul(ps[:, :], lhsT=wqk.t[:, kc, cch, :], rhs=self.uT.t[:, kc, tt * 512:(tt + 1) * 512],
                                                                                      start=(kc == 0), stop=(kc == 7)),
                                 reads=wqk.rr(kc * 2048, kc * 2048 + 2048) + self.uT.rr((kc * T + tt * 512) * 2, (kc * T + tt * 512 + 512) * 2), writes=[pr])
                        pss.append((ps, pr))
                        if cch == 1:
                            (pa, ra), (pb_, rb) = pss[0], pss[1]
                            cosb, sinb = cst_[0], cst_[1]
                            qsc = scale if qk == 0 else 1.0
                            for ri, (psx, rx, tb) in enumerate(((pa, ra, cosb), (pb_, rb, sinb), (pa, ra, sinb), (pb_, rb, cosb))):
                                P.op(DVE, lambda e: e.scalar_tensor_tensor(out=rt[ri].t, in0=psx[:, :], scalar=qsc, in1=tb.t, op0=ALU.mult, op1=ALU.mult),
                                     reads=[rx] + tb.R, writes=rt[ri].R)
                            o0, o1 = ob[oi % 4], ob[(oi + 1) % 4]
                            oi += 2
                            P.op(POOL, lambda e, o0=o0: e.tensor_tensor(out=o0.t, in0=rt[0].t, in1=rt[1].t, op=ALU.subtract), reads=rt[0].R + rt[1].R, writes=o0.R)
                            P.op(POOL, lambda e, o1=o1: e.tensor_tensor(out=o1.t, in0=rt[3].t, in1=rt[2].t, op=ALU.add), reads=rt[2].R + rt[3].R, writes=o1.R)
                            outs = [(0, o0), (1, o1)]
                        elif cch >= 2:
                            o0 = ob[oi % 4]
                            oi += 1
                            P.op(ACT, lambda e: e.activation(out=o0.t, in_=ps[:, :], func=AF.Copy, scale=(scale if qk == 0 else 1.0)), reads=[pr], writes=o0.R)
                            outs = [(cch, o0)]
                        else:
                            outs = []
                        for (cc, o) in outs:
                            P.dma(SP, dst[cc, :, tt * 512:(tt + 1) * 512], o.t, reads=o.R, writes=[self.R_QKV])
            vo = [self.mk(Z + 20 * KIB + i * KIB, BF16, [512]) for i in range(2)]
            vi = 0
            for n in range(2):
                wb = self.load_w(w_in[:, g * 3072 + 2048 + n * 512:g * 3072 + 2048 + (n + 1) * 512], 512)
                for st_ in range(32):
                    ps, pr = self.bank()
                    for kc in range(8):
                        P.op(PE, lambda e, ps=ps, kc=kc, st_=st_, wb=wb: e.matmul(ps[:, :], lhsT=self.uT.t[:, kc, st_ * 128:(st_ + 1) * 128], rhs=wb.t[:, kc, :],
                                                                                  start=(kc == 0), stop=(kc == 7)),
                             reads=wb.R + self.uT.rr((kc * T + st_ * 128) * 2, (kc * T + st_ * 128 + 128) * 2), writes=[pr])
                    v = vo[vi % 2]
                    vi += 1
                    P.op(ACT, lambda e, ps=ps, v=v: e.copy(out=v.t, in_=ps[:, :]), reads=[pr], writes=v.R)
                    dap = bass.AP(tensor=self.Vd.tensor, offset=self.Vd.offset + (n * 4) * T * 128 + st_ * 128, ap=[[32 * 128, 128], [T * 128, 4], [1, 128]])
                    P.dma(SP, dap, v.t.rearrange("p (a b) -> p a b", a=4), reads=v.R, writes=[self.R_QKV])
            nb = 32 // d
            qkv = [[self.mk(self.UT + (i * 3 + w) * 8 * KIB, BF16, [T]) for w in range(3)] for i in range(2)]
            NB3 = 9
            pp = [self.mk(Z + i * KIB, BF16, [256]) for i in range(NB3)]
            pT = [self.mk(Z + 9 * KIB + i * KIB, BF16, [2, 128]) for i in range(NB3)]
            Ot = [self.mk(Z + 18 * KIB + i * KIB, F32, [136]) for i in range(NB3)]
            stt_ = [self.mk(Z + 27 * KIB + i * KIB, F32, [8]) for i in range(NB3)]
            def load_head(h_, part):
                if h_ >= NH:
                    return
                qb_, kb_, vb_ = qkv[h_ % 2]
                if part == 0:
                    P.dma(SP, [qb_.t[cc * 16:(cc + 1) * 16, :] for cc in range(8)], [self.QT[cc, h_ * 16:(h_ + 1) * 16, :] for cc in range(8)],
                          reads=[self.R_QKV], writes=qb_.R)
                elif part == 1:
                    P.dma(SP, [kb_.t[cc * 16:(cc + 1) * 16, :] for cc in range(8)], [self.KT[cc, h_ * 16:(h_ + 1) * 16, :] for cc in range(8)],
                          reads=[self.R_QKV], writes=kb_.R)
                else:
                    P.dma(SP, vb_.t, self.Vd[h_].rearrange("p b v -> p (b v)"), reads=[self.R_QKV], writes=vb_.R)
            for part in range(3):
                load_head(0, part)
            for h in range(NH):
                qb, kb, vb = qkv[h % 2]
                vv = vb.t.rearrange("p (b v) -> p b v", v=128)
                stA = {}

                def stage_A(u):
                    hasprev = (u % nb) != 0
                    ps, pr = self.bank()
                    k0 = (u - 1) * 128 if hasprev else u * 128
                    nk = 256 if hasprev else 128
                    mview = self.maskb[:, 0:256] if hasprev else self.maskb[:, 128:256]
                    P.op(PE, lambda e: e.matmul(ps[:, 0:nk], lhsT=qb.t[:, u * 128:(u + 1) * 128], rhs=kb.t[:, k0:k0 + nk], start=True, stop=False),
                         reads=qb.R + kb.R, writes=[pr])
                    P.op(PE, lambda e: e.matmul(ps[:, 0:nk], lhsT=self.identb.t, rhs=mview, start=False, stop=True),
                         reads=self.identb.R + self.R_maskb, writes=[pr])
                    i = u % NB3
                    p_, st = pp[i], stt_[i]
                    P.op(DVE, lambda e: e.tensor_reduce(out=st.t[:, 1:2], in_=ps[:, 0:nk], axis=AX.X, op=ALU.max, negate=True), reads=[pr], writes=st.R)
                    P.op(ACT, lambda e: e.activation(out=p_.t[:, 0:nk], in_=ps[:, 0:nk], func=AF.Exp, bias=st.t[:, 1:2], accum_out=st.t[:, 2:3]),
                         reads=[pr] + st.R, writes=p_.R + st.R)
                    stA[u] = (hasprev, nk)

                def stage_B(u):
                    hasprev, nk = stA[u]
                    i = u % NB3
                    p_, pt = pp[i], pT[i]
                    ps, pr = self.bank()
                    pb = ps[:, :].bitcast(BF16)
                    nparts = 2 if hasprev else 1
                    for a in range(nparts):
                        P.op(PE, lambda e: e.transpose(out=pb[:, a * 128:(a + 1) * 128], in_=p_.t[:, a * 128:(a + 1) * 128], identity=self.identb.t),
                             reads=p_.R + self.identb.R, writes=[pr])
                    P.op(DVE, lambda e: e.tensor_copy(out=pt.t[:, 0:nparts, :].rearrange("p a b -> p (a b)"), in_=pb[:, 0:nparts * 128]),
                         reads=[pr], writes=pt.R)

                def stage_C(u):
                    hasprev, nk = stA[u]
                    i = u % NB3
                    pt, O, st = pT[i], Ot[i], stt_[i]
                    ps, pr = self.bank()
                    if hasprev:
                        P.op(PE, lambda e: e.matmul(ps[:, 0:128], lhsT=pt.t[:, 0, :], rhs=vv[:, u - 1, :], start=True, stop=False), reads=pt.R + vb.R, writes=[pr])
                        P.op(PE, lambda e: e.matmul(ps[:, 0:128], lhsT=pt.t[:, 1, :], rhs=vv[:, u, :], start=False, stop=True), reads=pt.R + vb.R, writes=[pr])
                    else:
                        P.op(PE, lambda e: e.matmul(ps[:, 0:128], lhsT=pt.t[:, 0, :], rhs=vv[:, u, :], start=True, stop=True), reads=pt.R + vb.R, writes=[pr])
                    P.op(DVE, lambda e: e.reciprocal(out=st.t[:, 3:4], in_=st.t[:, 2:3]), reads=st.R, writes=st.R)
                    P.op(DVE, lambda e: e.tensor_scalar(out=O.t[:, 0:128], in0=ps[:, 0:128], scalar1=st.t[:, 3:4], scalar2=None, op0=ALU.mult),
                         reads=[pr] + st.R, writes=O.R)
                    P.op(ACT, lambda e: e.activation(out=st.t[:, 4:5], in_=st.t[:, 2:3], func=AF.Ln), reads=st.R, writes=st.R)
                    P.op(POOL, lambda e: e.tensor_tensor(out=O.t[:, 128:129], in0=st.t[:, 4:5], in1=st.t[:, 1:2], op=ALU.subtract), reads=st.R, writes=O.R)
                    r, n = u // nb, u % nb
                    t0 = n * 128 * d + r
                    dap = bass.AP(tensor=self.Og.tensor, offset=self.Og.offset + g * T * NH * 136 + t0 * NH * 136 + h * 136, ap=[[d * NH * 136, 128], [1, 136]])
                    P.dma(SP, dap, O.t, reads=O.R, writes=[self.R_Og])

                for s in range(32 + 6):
                    if s in (5, 13, 21):
                        load_head(h + 1, (s - 5) // 8)
                    if s < 32:
                        stage_A(s)
                    if 0 <= s - 3 < 32:
                        stage_B(s - 3)
                    if 0 <= s - 6 < 32:
                        stage_C(s - 6)
        ogb = [self.mk(self.UT + i * 14 * KIB, F32, [3, NH, 136]) for i in range(4)]
        cws = [self.mk(self.ZX + 18 * KIB + i * KIB, F32, [64]) for i in range(2)]
        oc = [self.mk(self.ZX + 20 * KIB + i * 4 * KIB, F32, [NH, 128]) for i in range(2)]
        tms = [self.mk(self.ZX + 28 * KIB + i * 4 * KIB, F32, [NH, 128]) for i in range(4)]
        ot = [self.mk(self.ZX + 44 * KIB + i * KIB, BF16, [512]) for i in range(4)]
        oti = 0
        def load_og(st_):
            if st_ < 32:
                og_ = ogb[st_ % 4]
                P.dma(SP, [og_.t[:, g, :, :] for g in range(3)], [self.Og[g, st_ * 128:(st_ + 1) * 128, :, :] for g in range(3)],
                      reads=[self.R_Og], writes=og_.R)
        for i_ in range(3):
            load_og(i_)
        for tt in range(8):
            for j in range(4):
                st_ = tt * 4 + j
                og = ogb[st_ % 4]
                load_og(st_ + 3)
                L = og.t[:, :, :, 128:129].rearrange("p g h o -> p g (h o)")
                cw = cws[st_ % 2]
                mx = cw.t[:, 0:8]
                ee = cw.t[:, 8:32].rearrange("p (g h) -> p g h", g=3)
                den = cw.t[:, 32:40]
                P.op(DVE, lambda e, L=L, mx=mx: e.tensor_tensor(out=mx, in0=L[:, 0, :], in1=L[:, 1, :], op=ALU.max), reads=og.R, writes=cw.R)
                P.op(DVE, lambda e, L=L, mx=mx: e.tensor_tensor(out=mx, in0=mx, in1=L[:, 2, :], op=ALU.max), reads=og.R + cw.R, writes=cw.R)
                P.op(DVE, lambda e, L=L, mx=mx, ee=ee: e.tensor_tensor(out=ee, in0=L, in1=mx.unsqueeze(1).broadcast_to([128, 3, 8]), op=ALU.subtract), reads=og.R + cw.R, writes=cw.R)
                P.op(ACT, lambda e, ee=ee: e.activation(out=ee, in_=ee, func=AF.Exp), reads=cw.R, writes=cw.R)
                P.op(DVE, lambda e, ee=ee, den=den: e.tensor_tensor(out=den, in0=ee[:, 0, :], in1=ee[:, 1, :], op=ALU.add), reads=cw.R, writes=cw.R)
                P.op(DVE, lambda e, ee=ee, den=den: e.tensor_tensor(out=den, in0=den, in1=ee[:, 2, :], op=ALU.add), reads=cw.R, writes=cw.R)
                P.op(DVE, lambda e, den=den: e.reciprocal(out=den, in_=den), reads=cw.R, writes=cw.R)
                P.op(DVE, lambda e, ee=ee, den=den: e.tensor_tensor(out=ee, in0=ee, in1=den.unsqueeze(1).broadcast_to([128, 3, 8]), op=ALU.mult), reads=cw.R, writes=cw.R)
                o = oc[st_ % 2]
                for g in range(3):
                    wg = ee[:, g, :].unsqueeze(2).broadcast_to([128, 8, 128])
                    tm = tms[(st_ % 2) * 2 + (g - 1)] if g > 0 else None
                    dstb = o if g == 0 else tm
                    P.op(DVE, lambda e, og=og, g=g, wg=wg, dstb=dstb: e.tensor_tensor(out=dstb.t, in0=og.t[:, g, :, 0:128], in1=wg, op=ALU.mult),
                         reads=og.R + cw.R, writes=dstb.R)
                    if g > 0:
                        P.op(POOL, lambda e: e.tensor_tensor(out=o.t, in0=o.t, in1=tm.t, op=ALU.add), reads=o.R + tm.R, writes=o.R)
                for hh in range(0, 8, 4):
                    ps, pr = self.bank()
                    for q in range(4):
                        P.op(PE, lambda e, ps=ps, q=q, hh=hh, o=o: e.transpose(out=ps[:, q * 128:(q + 1) * 128], in_=o.t[:, hh + q, :], identity=self.ident),
                             reads=o.R + self.cstb.R, writes=[pr])
                    ob_ = ot[oti % 4]
                    oti += 1
                    P.op(ACT, lambda e, ps=ps, ob_=ob_: e.copy(out=ob_.t, in_=ps[:, :]), reads=[pr], writes=ob_.R)
                    dap = bass.AP(tensor=self.oT.tensor, offset=self.oT.offset + hh * 128 * T + st_ * 128, ap=[[T, 128], [128 * T, 4], [1, 128]])
                    P.dma(SP, dap, ob_.t.rearrange("p (a b) -> p a b", a=4), reads=ob_.R, writes=[self.R_oT])
        self.out_proj(self.at_w_out[idx], False)

    def build(self, phases=None):
        self.init_consts()
        self.prologue()
        self.adaln(0)
        for l in range(self.n_layers):
            self.set_layer(l)
            if l % 2 == 0:
                self.hgrn(l)
            else:
                self.attn(l)
            self.ffn(l)
        self.P.emit(final_regions=self.R_out)
        return self.nc


def make_consts():
    c = np.zeros((128, 1680), np.float32)
    c[:, 0:128] = np.eye(128, dtype=np.float32)
    c[:, 128:256] = 1.0
    s = np.arange(128)[:, None]
    cc = np.arange(128)[None, :]
    c[:, 256:384] = ((s // 32 == cc // 32) & (s <= cc)).astype(np.float32)
    qi = np.arange(128)[:, None]
    kj = np.arange(128)[None, :]
    c[:, 384:512] = np.where(kj >= qi, 0.0, NEG)
    c[:, 512:640] = np.where(kj <= qi, 0.0, NEG)
    m = np.ones(1024, np.float32)
    m[::32] = 0.0
    c[:, 640:1664] = m[None, :]
    j = (np.arange(128) % 16).astype(np.float64)
    c[:, 1664] = (500000.0 ** (-(2.0 * j) / 32.0)).astype(np.float32)
    c[:, 1665] = EPS
    c[:, 1666] = 1.0
    c[:, 1667] = np.pi / 2
    for i4 in range(4):
        c[i4 * 32:(i4 + 1) * 32, 1668 + i4] = 1.0
    return c


_CACHE = {}


def kernel(x, c, positions, ada_w, ada_b, norm_g, hgrn_w_in, hgrn_lower_bounds, hgrn_norm_g,
           hgrn_w_out, attn_w_in, attn_w_out, ffn_w_in, ffn_w_out, _n_layers=4, _cores=8):
    if _n_layers not in _CACHE:
        _CACHE[_n_layers] = Builder(n_layers=_n_layers).build()
    nc = _CACHE[_n_layers]
    f = lambda a: np.ascontiguousarray(np.asarray(a, dtype=np.float32))
    shared = {
        "cst": make_consts(), "ada_w": f(ada_w), "ada_b": f(ada_b), "norm_g": f(norm_g),
        "hgrn_w_in": f(hgrn_w_in), "hgrn_lower_bounds": f(hgrn_lower_bounds), "hgrn_norm_g": f(hgrn_norm_g),
        "hgrn_w_out": f(hgrn_w_out), "attn_w_in": f(attn_w_in), "attn_w_out": f(attn_w_out),
        "ffn_w_in": f(ffn_w_in), "ffn_w_out": f(ffn_w_out),
    }
    x = np.asarray(x, dtype=np.float32)
    c = np.asarray(c, dtype=np.float32)
    positions = np.asarray(positions, dtype=np.int32)
    in_maps = []
    for b in range(_cores):
        m = dict(shared)
        m["x"] = np.ascontiguousarray(x[b])
        m["c"] = np.ascontiguousarray(c[b:b + 1])
        m["pos"] = np.ascontiguousarray(positions[b:b + 1])
        in_maps.append(m)
    res = run_bass_kernel_spmd(nc, in_maps, core_ids=list(range(_cores)))
    return np.stack([np.asarray(r["out"], dtype=np.float32) for r in res.results], axis=0)
```
